# Optimizing a Trainium2 kernel written in Bass

```python
import jax, jax.numpy as jnp
from jax import lax
import numpy as np

D_MODEL = 2048
BATCH = 4
SEQ = 2048
DEPTH = 1
DEC_BATCH = 128
DEC_SEQ = 1
PAST_LEN = 16384
PAGE_SIZE = 128

PLE_DIM = 256
GLA_HEADS = 4
GLA_DK = D_MODEL // 2 // GLA_HEADS
GLA_DV = D_MODEL // GLA_HEADS
GLA_RANK = 16
GLA_TAU = 16.0
RET_HEADS = 8
RET_DK = D_MODEL // RET_HEADS
RET_DV = D_MODEL // RET_HEADS
ROPE_BASE = 10000.0
CHUNK = 64
EPS = 1e-6

GLA_QK = GLA_HEADS * GLA_DK
GLA_V = GLA_HEADS * GLA_DV
RET_QK = RET_HEADS * RET_DK
RET_V = RET_HEADS * RET_DV
IN_SPLITS = (GLA_QK, GLA_QK, GLA_V, GLA_V, GLA_RANK, RET_QK, RET_QK, RET_V, RET_V, D_MODEL, D_MODEL)
N_IN = GLA_QK * 2 + GLA_V * 2 + GLA_RANK + RET_QK * 2 + RET_V * 2 + D_MODEL * 2

kernel_name = "gla_retnet_parallel_gated_decode_step"


def rmsnorm(x, g=None):
    xf = x.astype(jnp.float32)
    y = xf * lax.rsqrt(jnp.mean(xf * xf, axis=-1, keepdims=True) + EPS)
    if g is not None:
        y = y * g.astype(jnp.float32)
    return y.astype(x.dtype)


def rotary(x, pos):
    half = x.shape[-1] // 2
    inv = 1.0 / (ROPE_BASE ** jnp.linspace(0.0, 1.0, half, dtype=jnp.float32))
    ang = pos[:, None] * inv[None, :]
    cos = jnp.cos(ang)[None, :, None, :]
    sin = jnp.sin(ang)[None, :, None, :]
    xf = x.astype(jnp.float32)
    x1, x2 = xf[..., :half], xf[..., half:]
    return jnp.concatenate([x1 * cos - x2 * sin, x1 * sin + x2 * cos], axis=-1).astype(x.dtype)


def chunked_linear_recurrence(q, k, v, log_a, state):
    B, L, H, dk = q.shape
    dv = v.shape[-1]
    da = log_a.shape[-1]
    c = min(CHUNK, L)
    n = -(-L // c)
    pad = n * c - L

    def blocks(t):
        t = jnp.pad(t.astype(jnp.float32), ((0, 0), (0, pad), (0, 0), (0, 0)))
        return t.reshape(B, n, c, H, t.shape[-1]).transpose(1, 0, 3, 2, 4)

    qs, ks, vs, als = blocks(q), blocks(k), blocks(v), blocks(log_a)
    causal = jnp.tril(jnp.ones((c, c), dtype=bool))

    def step(S, blk):
        qc, kc, vc, ac = blk
        b = jnp.cumsum(ac, axis=2)
        diff = jnp.where(causal[:, :, None], b[:, :, :, None, :] - b[:, :, None, :, :], -jnp.inf)
        decay = jnp.exp(diff)
        if da == 1:
            A = jnp.einsum('bhtk,bhsk->bhts', qc, kc) * decay[..., 0]
        else:
            A = jnp.einsum('bhtk,bhsk,bhtsk->bhts', qc, kc, decay)
        o = (jnp.einsum('bhtk,bhkv->bhtv', qc * jnp.exp(b), S)
             + jnp.einsum('bhts,bhsv->bhtv', A, vc))
        b_last = b[:, :, -1:, :]
        S = (jnp.exp(b_last[:, :, 0, :, None]) * S
             + jnp.einsum('bhsk,bhsv->bhkv', kc * jnp.exp(b_last - b), vc))
        return S, o

    S, o = lax.scan(step, state.astype(jnp.float32), (qs, ks, vs, als))
    o = o.transpose(1, 0, 3, 2, 4).reshape(B, n * c, H, dv)[:, :L]
    return o.astype(v.dtype), S.astype(state.dtype)


def hybrid_layer(x, p, pos, st_gla, st_ret, norm_mix, w_in, w_gla_up, b_gla, gla_norm,
                 w_out, norm_ple, w_ple_gate, w_ple_proj):
    B, L, _ = x.shape
    u = rmsnorm(x, norm_mix)
    z = u @ w_in
    offs = np.cumsum(IN_SPLITS)[:-1].tolist()
    q_a, k_a, v_a, g_a, r_a, q_b, k_b, v_b, g_b, m_a, m_b = jnp.split(z, offs, axis=-1)

    q_a = q_a.reshape(B, L, GLA_HEADS, GLA_DK) * (GLA_DK ** -0.5)
    k_a = k_a.reshape(B, L, GLA_HEADS, GLA_DK)
    v_a = v_a.reshape(B, L, GLA_HEADS, GLA_DV)
    log_alpha = jax.nn.log_sigmoid((r_a @ w_gla_up + b_gla).astype(jnp.float32)) / GLA_TAU
    log_alpha = log_alpha.reshape(B, L, GLA_HEADS, GLA_DK)
    o_a, new_gla = chunked_linear_recurrence(q_a, k_a, v_a, log_alpha, st_gla)
    o_a = rmsnorm(o_a, gla_norm).reshape(B, L, GLA_V) * jax.nn.silu(g_a)

    q_b = rotary(q_b.reshape(B, L, RET_HEADS, RET_DK), pos)
    k_b = rotary(k_b.reshape(B, L, RET_HEADS, RET_DK), pos) * (RET_DK ** -0.5)
    v_b = v_b.reshape(B, L, RET_HEADS, RET_DV)
    log_gamma = jnp.log(1.0 - jnp.exp2(-5.0 - jnp.arange(RET_HEADS, dtype=jnp.float32)))
    log_g = jnp.broadcast_to(log_gamma[None, None, :, None], (B, L, RET_HEADS, 1))
    o_b, new_ret = chunked_linear_recurrence(q_b, k_b, v_b, log_g, st_ret)
    o_b = rmsnorm(o_b).reshape(B, L, RET_V) * jax.nn.silu(g_b)

    merged = jax.nn.sigmoid(m_a) * o_a + jax.nn.sigmoid(m_b) * o_b
    h = x + merged @ w_out

    gate = jax.nn.sigmoid(rmsnorm(h, norm_ple) @ w_ple_gate)
    h = h + gate * (p @ w_ple_proj)
    return h, new_gla, new_ret


def setup_inputs(seed: int = 0) -> dict:
    key = jax.random.key(seed)
    ks = jax.random.split(key, 16)

    def nrm(k, shape, s):
        return jax.random.normal(k, shape, jnp.float32) * s

    return {
        'x_prompt': nrm(ks[0], (BATCH, SEQ, D_MODEL), 1.0),
        'x_sample': nrm(ks[1], (DEC_BATCH, DEC_SEQ, D_MODEL), 1.0),
        'state_gla': nrm(ks[2], (DEPTH, DEC_BATCH, GLA_HEADS, GLA_DK, GLA_DV), 0.5),
        'state_ret': nrm(ks[3], (DEPTH, DEC_BATCH, RET_HEADS, RET_DK, RET_DV), 0.5),
        'p_prompt': nrm(ks[4], (DEPTH, BATCH, SEQ, PLE_DIM), 1.0),
        'p_sample': nrm(ks[5], (DEPTH, DEC_BATCH, DEC_SEQ, PLE_DIM), 1.0),
        'norm_mix': 1.0 + nrm(ks[6], (DEPTH, D_MODEL), 0.02),
        'w_in': nrm(ks[7], (DEPTH, D_MODEL, N_IN), D_MODEL ** -0.5),
        'w_gla_up': nrm(ks[8], (DEPTH, GLA_RANK, GLA_QK), GLA_RANK ** -0.5),
        'b_gla': nrm(ks[9], (DEPTH, GLA_QK), 0.1),
        'gla_norm': 1.0 + nrm(ks[10], (DEPTH, GLA_DV), 0.02),
        'w_out': nrm(ks[11], (DEPTH, D_MODEL, D_MODEL), D_MODEL ** -0.5),
        'norm_ple': 1.0 + nrm(ks[12], (DEPTH, D_MODEL), 0.02),
        'w_ple_gate': nrm(ks[13], (DEPTH, D_MODEL, D_MODEL), D_MODEL ** -0.5),
        'w_ple_proj': nrm(ks[14], (DEPTH, PLE_DIM, D_MODEL), PLE_DIM ** -0.5),
        'norm_final': 1.0 + nrm(ks[15], (D_MODEL,), 0.02),
    }


def reference(x_prompt, x_sample, state_gla, state_ret, p_prompt, p_sample, norm_mix, w_in,
              w_gla_up, b_gla, gla_norm, w_out, norm_ple, w_ple_gate, w_ple_proj, norm_final):
    Bp, Lp, _ = x_prompt.shape
    Bs, Ls, _ = x_sample.shape
    pos_prompt = jnp.arange(Lp, dtype=jnp.float32)
    pos_sample = PAST_LEN + jnp.arange(Ls, dtype=jnp.float32)
    hp, hs = x_prompt, x_sample
    gla_p, ret_p, gla_s, ret_s = [], [], [], []
    for i in range(DEPTH):
        lw = (norm_mix[i], w_in[i], w_gla_up[i], b_gla[i], gla_norm[i], w_out[i],
              norm_ple[i], w_ple_gate[i], w_ple_proj[i])
        z_gla = jnp.zeros((Bp, GLA_HEADS, GLA_DK, GLA_DV), x_prompt.dtype)
        z_ret = jnp.zeros((Bp, RET_HEADS, RET_DK, RET_DV), x_prompt.dtype)
        hp, sg, sr = hybrid_layer(hp, p_prompt[i], pos_prompt, z_gla, z_ret, *lw)
        gla_p.append(sg)
        ret_p.append(sr)
        hs, sg, sr = hybrid_layer(hs, p_sample[i], pos_sample, state_gla[i], state_ret[i], *lw)
        gla_s.append(sg)
        ret_s.append(sr)
    y_prompt = rmsnorm(hp, norm_final)
    y_sample = rmsnorm(hs, norm_final)
    return (y_prompt, y_sample, jnp.stack(gla_p), jnp.stack(ret_p), jnp.stack(gla_s), jnp.stack(ret_s))
```

```python
import contextlib
import numpy as np
import ml_dtypes
import concourse.bass as bass
import concourse.mybir as mybir
from concourse.bass_utils import run_bass_kernel_spmd

F32 = mybir.dt.float32
BF16 = mybir.dt.bfloat16
AF = mybir.ActivationFunctionType
ALU = mybir.AluOpType

NIN = 18448
OFF = dict(qa=0, ka=1024, va=2048, ga=4096, ra=6144, qb=6160, kb=8208, vb=10256, gb=12304, ma=14352, mb=16400)
EPS = 1e-6
NT = 1024
NS = 32
NS1 = 16
SAMPLE_PER_POST = 0.7
STRICT_SAME_ENGINE = True

C_ID = 0
C_TRI = 128
C_DT = 256
C_GAM = C_DT + 512
C_GKH = C_GAM + 512
C_G128 = C_GKH + 4
C_G1 = C_G128 + 4
C_MASK = C_G1 + 4
C_NM = C_MASK + 2
C_NP = C_NM + 16
C_GN = C_NP + 16
C_CS = C_GN + 512
C_WUP = C_CS + 4
NCST = C_WUP + 512
B_ID = 0
NCBF = 128


GLA_PERM = (2 * np.arange(128)[None, :] + np.arange(2)[:, None]).reshape(256)


def _gammas():
    h = np.arange(8, dtype=np.float64)
    return np.log(1.0 - np.exp2(-5.0 - h))


def wblocks():
    bl = {}
    bl['r'] = (OFF['ra'], 16)
    for h in range(4):
        bl['qa%d' % h] = (OFF['qa'] + h * 256, 256)
        bl['ka%d' % h] = (OFF['ka'] + h * 256, 256)
        bl['va%d' % h] = (OFF['va'] + h * 512, 512)
        bl['ga%d' % h] = (OFF['ga'] + h * 512, 512)
        bl['ma%d' % h] = (OFF['ma'] + h * 512, 512)
        bl['qb%d' % h] = (OFF['qb'] + h * 512, 512)
        bl['kb%d' % h] = (OFF['kb'] + h * 512, 512)
        bl['vb%d' % h] = (OFF['vb'] + h * 512, 512)
        bl['gb%d' % h] = (OFF['gb'] + h * 512, 512)
        bl['mb%d' % h] = (OFF['mb'] + h * 512, 512)
    offs = {}
    o = 0
    for k, (c0, n) in bl.items():
        offs[k] = o
        o += 16 * n
    return bl, offs, o


WBL, _WOFF_G, _WTOT_G = wblocks()
LKEYS = ['r'] + ['%s%d' % (k, g) for g in range(2) for k in ('qa', 'ka', 'va', 'ga', 'ma', 'qb', 'kb', 'vb', 'gb', 'mb')]
WOFF = {}
_o = 0
for _k in LKEYS:
    WOFF[_k] = _o
    _o += 16 * WBL[_k][1]
WTOT = _o


class Res:
    __slots__ = ("lw", "rd", "dsem", "dcnt")

    def __init__(self):
        self.lw = None
        self.rd = {}
        self.dsem = None
        self.dcnt = 0


class Trk:
    def __init__(self, nc, es):
        self.nc = nc
        self.es = es
        self.E = {}
        for n, e in (("pe", nc.tensor), ("act", nc.scalar), ("dve", nc.vector), ("pool", nc.gpsimd), ("sp", nc.sync)):
            sem = es.enter_context(nc.semaphore("s_" + n))
            self.E[n] = dict(e=e, sem=sem, cnt=0, seen={})
        self.R = {}
        self.nd = 0

    def res(self, key):
        r = self.R.get(key)
        if r is None:
            r = self.R[key] = Res()
        return r

    def _deps(self, reads, writes):
        d = []
        for k in reads:
            r = self.res(k)
            if r.lw is not None:
                d.append((r.lw, 'raw'))
        for k in writes:
            r = self.res(k)
            if r.lw is not None:
                d.append((r.lw, 'waw'))
            for x in r.rd.values():
                d.append((x, 'war'))
        return d

    def _wait(self, en, deps):
        E = self.E[en]
        for (src, kind) in deps:
            tag, ref, val = src
            if tag == 'E':
                if ref == en and (en == 'pe' or (kind != 'raw' and not STRICT_SAME_ENGINE)):
                    continue
                key = ('E', ref)
                sem = self.E[ref]['sem']
            else:
                key = ('D', id(ref))
                sem = ref.dsem
            if E['seen'].get(key, 0) >= val:
                continue
            E['e'].wait_ge(sem, val)
            E['seen'][key] = val

    def _commit(self, me, mkey, reads, writes):
        for k in reads:
            self.res(k).rd[mkey] = me
        for k in writes:
            r = self.res(k)
            r.lw = me
            r.rd = {}

    def op(self, en, fn, reads=(), writes=()):
        self._wait(en, self._deps(reads, writes))
        E = self.E[en]
        ins = fn(E['e'])
        E['cnt'] += 1
        ins.then_inc(E['sem'], 1)
        self._commit(('E', en, E['cnt']), ('E', en), reads, writes)

    def group(self, en, fns, reads=(), writes=()):
        self._wait(en, self._deps(reads, writes))
        E = self.E[en]
        ins = None
        for f in fns:
            ins = f(E['e'])
        E['cnt'] += 1
        ins.then_inc(E['sem'], 1)
        self._commit(('E', en, E['cnt']), ('E', en), reads, writes)

    def dma(self, en, out, in_, sb, load, dram=None, extra=(), **kw):
        r = self.res(sb)
        if r.dsem is None:
            r.dsem = self.es.enter_context(self.nc.semaphore("d%d" % self.nd))
            self.nd += 1
        if load:
            reads = [dram] if dram is not None else []
            writes = [sb] + list(extra)
        else:
            reads = [sb]
            writes = [dram] if dram is not None else []
        self._wait(en, self._deps(reads, writes))
        ins = self.E[en]['e'].dma_start(out=out, in_=in_, **kw)
        r.dcnt += 16
        ins.then_inc(r.dsem, 16)
        self._commit(('D', r, r.dcnt), ('D', id(r)), reads, writes)

    def coll(self, ins_ap, outs_ap, reads, writes, groups):
        r = self.res(('coll', self.nd))
        r.dsem = self.es.enter_context(self.nc.semaphore("c%d" % self.nd))
        self.nd += 1
        self._wait('pool', self._deps(reads, writes))
        ins = self.nc.gpsimd.collective_compute("AllGather", ALU.bypass, replica_groups=groups, ins=[ins_ap], outs=[outs_ap])
        r.dcnt += 1
        ins.then_inc(r.dsem, 1)
        self._commit(('D', r, r.dcnt), ('D', id(r)), reads, writes)

    def barrier(self):
        for en, E in self.E.items():
            for e2, E2 in self.E.items():
                if e2 != en and E2['cnt'] > 0 and E['seen'].get(('E', e2), 0) < E2['cnt']:
                    E['e'].wait_ge(E2['sem'], E2['cnt'])
                    E['seen'][('E', e2)] = E2['cnt']
            for r in self.R.values():
                if r.dsem is not None and r.dcnt > 0 and E['seen'].get(('D', id(r)), 0) < r.dcnt:
                    E['e'].wait_ge(r.dsem, r.dcnt)
                    E['seen'][('D', id(r))] = r.dcnt


def build_nc(stop_after=None):
    nc = bass.Bass("TRN2", target_bir_lowering=False)

    def din(name, shape, dt=F32):
        return nc.dram_tensor(name, list(shape), dt, kind="ExternalInput").ap()

    def dout(name, shape, dt=F32):
        return nc.dram_tensor(name, list(shape), dt, kind="ExternalOutput").ap()

    def dint(name, shape, dt=F32):
        return nc.dram_tensor(name, list(shape), dt, kind="Internal").ap()

    xm = din("xm", [NT, 2048]); xp = din("xp", [NT, 2048]); xs = din("xs", [NS, 2048])
    xpost = din("xpost", [NT, 2048]); xs16 = din("xs16", [NS1, 2048])
    pm = din("pm", [NT, 256]); psd = din("psd", [NS1, 256])
    sg = din("sg", [NS, 2, 256, 512]); sr = din("sr", [NS, 4, 256, 256])
    wpk = din("wpk", [128, WTOT]); wout = din("wout", [128, 16 * 2048]); wgate = din("wgate", [128, 16 * 2048])
    wproj = din("wproj", [128, 2 * 2048])
    cst = din("cst", [128, NCST]); cbf = din("cbf", [128, NCBF], BF16)
    cosm = din("cosm", [128, NT]); sinm = din("sinm", [128, NT]); cosp = din("cosp", [128, NT]); sinp = din("sinp", [128, NT])
    nfb = din("nfb", [128, 2048])
    y_m = dout("y_m", [NT, 2048]); y_s = dout("y_s", [NS1, 2048])
    ng = dout("ng", [2, 256, 512]); nr = dout("nr", [4, 256, 256])
    nsg = dout("nsg", [NS, 2, 256, 512]); nsr = dout("nsr", [NS, 4, 256, 256])
    def cbuf(name, p, n):
        t = nc.dram_tensor(name, [p, n], F32, kind="Internal")
        return t, t.bitcast(BF16).ap().rearrange("p (a c) -> (p a) c", c=1024)
    mrga_l = [cbuf("mrga%d" % i, 128, 4096) for i in range(2)]; mrgb_l = [cbuf("mrgb%d" % i, 128, 4096) for i in range(2)]
    gat_a_l = [cbuf("gat_a%d" % i, 256, 4096) for i in range(2)]; gat_b_l = [cbuf("gat_b%d" % i, 256, 4096) for i in range(2)]
    mrgsa_f, mrgsa = cbuf("mrgsa", NS, 512); mrgsb_f, mrgsb = cbuf("mrgsb", NS, 512)
    gat_sa_f, gat_sa = cbuf("gat_sa", 2 * NS, 512); gat_sb_f, gat_sb = cbuf("gat_sb", 2 * NS, 512)
    spg = dint("spg", [2, 256, 512]); spr = dint("spr", [4, 256, 256])
    PAIRS = [[0, 1], [2, 3], [4, 5], [6, 7]]
    cur = dict(first=True, smp=False, pidx=0)

    es = contextlib.ExitStack()
    with es:
        T = Trk(nc, es)

        def sb(name, shape, dt=F32):
            return es.enter_context(nc.sbuf_tensor("t_" + name, list(shape), dt))

        PS = [es.enter_context(nc.psum_tensor("ps%d" % i, [128, 512], F32)) for i in range(8)]
        psi = [0]

        reserved = set()

        def rstd_op(o, i, reads, writes):
            T.op('act', lambda e: e.activation(o, i, AF.Ln, bias=EPS), reads=reads, writes=writes)
            T.op('act', lambda e: e.activation(o, o, AF.Exp, scale=-0.5), reads=writes, writes=writes)

        def nps():
            i = psi[0]
            while i in reserved:
                i = (i + 1) % 8
            psi[0] = (i + 1) % 8
            return PS[i], ('ps', i)

        W = [sb("W%d" % i, [128, 8192], BF16) for i in range(3)]
        C = sb("cst", [128, NCST])
        CB = sb("cbf", [128, NCBF], BF16)
        esB = contextlib.ExitStack()
        esB.__enter__()

        def sbB(name, shape, dt=F32):
            return esB.enter_context(nc.sbuf_tensor("t_" + name, list(shape), dt))
        qTs_a = sbB("qTs_a", [128, 4, NS]); kTs_a = sbB("kTs_a", [128, 4, NS]); aTs = sbB("aTs", [128, 4, NS])
        qTs_b = sbB("qTs_b", [128, 8, NS]); kTs_b = sbB("kTs_b", [128, 8, NS])
        vs_a = sbB("vs_a", [NS, 1024], BF16); vs_b = sbB("vs_b", [NS, 1024], BF16)
        Gs_a = sbB("Gs_a", [NS, 1024], BF16); Gs_b = sbB("Gs_b", [NS, 1024], BF16)

        T.dma('sp', C[:], cst[:, :], sb='C', load=True)
        T.dma('sp', CB[:], cbf[:, :], sb='CB', load=True)
        ident = C[:, C_ID:C_ID + 128]
        tri = C[:, C_TRI:C_TRI + 128]
        identB = CB[:, B_ID:B_ID + 128]

        wsched = []
        wstate = dict(issued=0, cur=-1)

        def w_issue_upto(n):
            while wstate['issued'] < min(n, len(wsched)):
                j = wstate['issued']
                src, ncols, nk = wsched[j][0:3]
                dk = wsched[j][3] if len(wsched[j]) > 3 else None
                T.dma('pool', W[j % 3][:, 0:nk * ncols], src, sb=('W', j % 3), load=True, dram=dk, max_dma_last_dim=8192)
                wstate['issued'] += 1

        def w_next(hold=False):
            wstate['cur'] += 1
            j = wstate['cur']
            w_issue_upto(j + 2 if hold else j + 3)
            src, ncols, nk = wsched[j][0:3]
            return W[j % 3][:, 0:nk * ncols].rearrange("p (k c) -> p k c", k=nk), ('W', j % 3)

        def sched_in(key):
            c0, n = WBL[key]
            wsched.append((wpk[:, WOFF[key]:WOFF[key] + 16 * n], n, 16))

        for pidx_ in range(2):
            sched_in('r')
            for g in range(2):
                for k in ('qa', 'ka', 'va', 'ga', 'ma'):
                    sched_in('%s%d' % (k, g))
                for k in ('qb', 'kb', 'vb', 'gb', 'mb'):
                    sched_in('%s%d' % (k, g))
        n_in_blocks = len(wsched)
        wcv = [dint("wcv%d" % i, [128, 8192], BF16) for i in range(8)]
        for grp in range(5):
            for cb in range(4):
                wsched.append((wcv[cb][:, :], 512, 16, ('wcv', cb)))
            for cb in range(4):
                wsched.append((wcv[4 + cb][:, :], 512, 16, ('wcv', 4 + cb)))

        es1 = contextlib.ExitStack()
        with es1:
            def sb1(name, shape, dt=F32):
                return es1.enter_context(nc.sbuf_tensor("t_" + name, list(shape), dt))

            uT = sb1("uT", [128, 16, NT], BF16)
            uTs = sb1("uTs", [128, 16, NS], BF16)
            Et = sb1("E", [128, 2, NT])
            qT = sb1("qT", [128, 4, NT], BF16); kT = sb1("kT", [128, 4, NT], BF16)
            vt = sb1("v", [128, 8, 512], BF16)
            cosT = sb1("cos", [128, NT]); sinT = sb1("sin", [128, NT])
            raT = sb1("raT", [17, NT + NS])
            xst = sb1("xst", [128, 2048]); junk5 = sb1("junk5", [128, 512], BF16)
            xsc = [sb1("xsc%d" % i, [128, 512]) for i in range(2)]
            stat = sb1("stat", [128, 64])
            rtA = [sb1("rtA%d" % i, [128, 512]) for i in range(1)]
            rtB = [sb1("rtB%d" % i, [128, 512]) for i in range(1)]
            rtR = [sb1("rtR%d" % i, [128, 512]) for i in range(1)]
            Lx = sb1("Lx", [128, 256]); Lt = sb1("Lt", [128, 256])
            ATm = [sb1("ATm%d" % i, [128, 256], BF16) for i in range(2)]
            khT = sb1("khT", [128, 2, 128], BF16)
            khat = [sb1("khat%d" % i, [128, 512], BF16) for i in range(2)]
            slt = sb1("sl", [128, 512]); smt = sb1("sm", [128, 512]); Gt = sb1("G", [128, 512])
            St = sb1("S", [128, 1024]); Sbf = sb1("Sbf", [128, 1024], BF16)
            mst = [sb1("mst%d" % i, [128, 512], BF16) for i in range(2)]
            cnt = dict(st=0, rot=0, at=0, kh=0, ms=0)

            def new_stat(n=1):
                i = cnt['st']
                if i + n > 64:
                    i = 0
                cnt['st'] = i + n
                return i

            T.op('pool', lambda e: e.memset(raT[:], 1.0), writes=[('raT', 0), ('raT', 1), ('raT', 's')])

            qTf = qT.bitcast(F32)[:].rearrange("p a b -> p (a b)")
            qT_keys = [('qT', ct_, c_) for ct_ in range(4) for c_ in range(8)]
            kTf = kT.bitcast(F32)[:].rearrange("p a b -> p (a b)")
            kT_keys = [('kT', ct_, c_) for ct_ in range(4) for c_ in range(8)]
            Ef = Et[:].rearrange("p a b -> p (a b)")
            E_keys = [('E', c_) for c_ in range(8)]
            nt_cnt = [0]

            def norm_transpose(x_ap, rows, dst, dst_key, col0, gain_off):
                alt = nt_cnt[0] % 4
                nt_cnt[0] += 1
                if alt == 0:
                    xst_, xk = xst, ['xst']
                    T.dma('sp', xst_[0:rows, :], x_ap, sb='xst', load=True)
                elif alt == 1:
                    xst_, xk = qTf, ['qTf'] + qT_keys
                    T.dma('sp', xst_[0:rows, :], x_ap, sb='qTf', load=True, extra=qT_keys)
                elif alt == 2:
                    xst_, xk = kTf, ['kTf'] + kT_keys
                    T.dma('sp', xst_[0:rows, :], x_ap, sb='kTf', load=True, extra=kT_keys)
                else:
                    xst_, xk = Ef, ['Ef'] + E_keys
                    T.dma('sp', xst_[0:rows, :], x_ap, sb='Ef', load=True, extra=E_keys)
                si = new_stat(2)
                ss = stat[0:rows, si:si + 1]
                rs = stat[0:rows, si + 1:si + 2]
                vj = vt[:, 0:4, :].rearrange("p a b -> p (a b)")
                T.op('act', lambda e: e.activation(vj[0:rows, :], xst_[0:rows, :], AF.Square, scale=float(2048 ** -0.5), accum_out=ss),
                     reads=xk, writes=[('v', 0), ('v', 1), ('v', 2), ('v', 3), ('stat', si)])
                rstd_op(rs, ss, [('stat', si)], [('stat', si + 1)])
                for cq in range(4):
                    xc = xsc[cq % 2]
                    T.op('dve', lambda e: e.tensor_scalar(xc[0:rows, :], xst_[0:rows, cq * 512:(cq + 1) * 512], rs, None, ALU.mult),
                         reads=xk + [('stat', si + 1)], writes=[('xsc', cq % 2)])
                    ps, pk = nps()
                    T.group('pe', [(lambda e, j=j: e.transpose(ps[:, j * rows:(j + 1) * rows], xc[0:rows, j * 128:(j + 1) * 128], ident[0:rows, 0:rows]))
                                   for j in range(4)], reads=[('xsc', cq % 2), 'C'], writes=[pk])
                    gsl = C[:, gain_off + cq * 4:gain_off + cq * 4 + 4].unsqueeze(2).to_broadcast([128, 4, rows])
                    T.op('dve', lambda e: e.tensor_tensor(dst[:, cq * 4:cq * 4 + 4, col0:col0 + rows], ps[:, 0:4 * rows].rearrange("p (j t) -> p j t", j=4), gsl, ALU.mult),
                         reads=[pk, 'C'], writes=[dst_key])

            def fm_block(Wv, wk, c0, ncol_tile, evac, evac_s):
                for half in range(2):
                    ps, pk = nps()
                    T.group('pe', [(lambda e, kt=kt: e.matmul(ps[0:ncol_tile, :], lhsT=Wv[:, kt, c0:c0 + ncol_tile], rhs=uT[:, kt, half * 512:(half + 1) * 512],
                                                             start=(kt == 0), stop=(kt == 15))) for kt in range(16)],
                            reads=[wk] + [('uT', half * 4 + i) for i in range(4)], writes=[pk])
                    evac(ps, pk, half)
                if evac_s is not None:
                    ps, pk = nps()
                    T.group('pe', [(lambda e, kt=kt: e.matmul(ps[0:ncol_tile, 0:NS], lhsT=Wv[:, kt, c0:c0 + ncol_tile], rhs=uTs[:, kt, :],
                                                             start=(kt == 0), stop=(kt == 15))) for kt in range(16)],
                            reads=[wk, 'uTs'], writes=[pk])
                    evac_s(ps, pk)

            def tm_tile(Wv, wk, tt, ncols):
                ps, pk = nps()
                T.group('pe', [(lambda e, kt=kt: e.matmul(ps[:, 0:ncols], lhsT=uT[:, kt, tt * 128:(tt + 1) * 128], rhs=Wv[:, kt, 0:ncols],
                                                         start=(kt == 0), stop=(kt == 15))) for kt in range(16)],
                        reads=[wk, ('uT', tt)], writes=[pk])
                return ps, pk

            def tm_tile_s(Wv, wk, ncols):
                ps, pk = nps()
                T.group('pe', [(lambda e, kt=kt: e.matmul(ps[0:NS, 0:ncols], lhsT=uTs[:, kt, :], rhs=Wv[:, kt, 0:ncols],
                                                         start=(kt == 0), stop=(kt == 15))) for kt in range(16)],
                        reads=[wk, 'uTs'], writes=[pk])
                return ps, pk

            def store_state(dst, nk_free, key):
                if nk_free == 512:
                    src = St[:].rearrange("p (k v) -> p k v", k=2)
                else:
                    src = St[:].rearrange("p (h k v) -> p h k v", h=2, k=2)
                T.dma('sp', dst, src, sb='St', load=False, dram=key)

            def run_pass(pidx):
                full = True
                cur['pidx'] = pidx
                cur['first'] = (pidx == 0)
                cur['smp'] = (pidx == 1)
                x_d = xm if pidx == 1 else xp
                T.dma('sp', cosT[:], (cosm if pidx == 1 else cosp)[:, :], sb='cos', load=True)
                T.dma('sp', sinT[:], (sinm if pidx == 1 else sinp)[:, :], sb='sin', load=True)
                for tt in range(8):
                    norm_transpose(x_d[tt * 128:(tt + 1) * 128, :], 128, uT, ('uT', tt), tt * 128, C_NM)
                if cur['smp']:
                    norm_transpose(xs[:, :], NS, uTs, 'uTs', 0, C_NM)

                Wv, wk = w_next()

                def ev_r(ps, pk, half):
                    T.op('act', lambda e: e.activation(raT[0:16, half * 512:(half + 1) * 512], ps[0:16, :], AF.Copy),
                         reads=[pk], writes=[('raT', half)])

                def ev_rs(ps, pk):
                    T.op('act', lambda e: e.activation(raT[0:16, NT:NT + NS], ps[0:16, 0:NS], AF.Copy), reads=[pk], writes=[('raT', 's')])

                fm_block(Wv, wk, 0, 16, ev_r, ev_rs if cur['smp'] else None)

                for g in range(2):
                    gla_unit(g, full)
                    ret_unit(g, full)

            cvt_pending = [(i, (wout if i < 4 else wgate)[:, (i % 4) * 8192:(i % 4 + 1) * 8192]) for i in range(8)]

            coll_pending = []

            def emit_coll():
                if coll_pending and cur['pidx'] == 1:
                    coll_pending.pop(0)()

            def emit_cvt(n):
                for _ in range(n):
                    if cvt_pending and cur['pidx'] == 0:
                        i, src = cvt_pending.pop(0)
                        T.dma('pool', wcv[i][:, :], src, sb=('wcvs', i), load=True, extra=[('wcv', i)], max_dma_last_dim=8192)

            def gla_unit(h, full):
                emit_cvt(2)
                if h == 1:
                    emit_coll()
                wup = C[0:17, C_WUP + h * 256:C_WUP + (h + 1) * 256]
                for c in range(8):
                    ps, pk = nps()
                    T.group('pe', [lambda e: e.matmul(ps[:, 0:256], lhsT=raT[0:17, c * 128:(c + 1) * 128], rhs=wup, start=True, stop=True)],
                            reads=[('raT', c // 4), 'C'], writes=[pk])
                    T.op('act', lambda e: e.activation(Lx[:], ps[:, 0:256], AF.Exp, scale=-1.0), reads=[pk], writes=['Lx'])
                    T.op('act', lambda e: e.activation(Lt[:], Lx[:], AF.Ln, bias=1.0), reads=['Lx'], writes=['Lt'])
                    ps2, pk2 = nps()
                    T.group('pe', [(lambda e, kt=kt: e.matmul(ps2[:, kt * 128:(kt + 1) * 128], lhsT=Lt[:, kt * 128:(kt + 1) * 128], rhs=tri, start=True, stop=True))
                                   for kt in range(2)], reads=['Lt', 'C'], writes=[pk2])
                    cum = ps2[:, 0:256].rearrange("p (k t) -> p k t", k=2)
                    T.op('act', lambda e: e.activation(Et[:, :, c * 128:(c + 1) * 128], cum, AF.Exp, scale=-1.0 / 16.0), reads=[pk2], writes=[('E', c)])
                if cur['smp']:
                    for kt in range(2):
                        ps, pk = nps()
                        T.group('pe', [lambda e: e.matmul(ps[:, 0:NS], lhsT=C[0:17, C_WUP + h * 256 + kt * 128:C_WUP + h * 256 + (kt + 1) * 128],
                                                          rhs=raT[0:17, NT:NT + NS], start=True, stop=True)], reads=[('raT', 's'), 'C'], writes=[pk])
                        T.op('act', lambda e: e.activation(Lx[:, 0:NS], ps[:, 0:NS], AF.Exp, scale=-1.0), reads=[pk], writes=['Lx'])
                        T.op('act', lambda e: e.activation(Lx[:, 32:32 + NS], Lx[:, 0:NS], AF.Ln, bias=1.0), reads=['Lx'], writes=['Lx'])
                        T.op('act', lambda e: e.activation(aTs[:, h * 2 + kt, :], Lx[:, 32:32 + NS], AF.Exp, scale=-1.0 / 16.0), reads=['Lx'], writes=['aTs'])

                if full:
                    Wv, wk = w_next()
                    for ct in range(2):
                        def ev_q(ps, pk, half, ct=ct):
                            T.op('dve', lambda e: e.scalar_tensor_tensor(qT[:, ct, half * 512:(half + 1) * 512], ps[:], 1.0 / 16.0,
                                                                         Et[:, ct, half * 512:(half + 1) * 512], ALU.mult, ALU.mult),
                                 reads=[pk] + [('E', half * 4 + i) for i in range(4)], writes=[('qT', ct, half * 4 + i) for i in range(4)])

                        def ev_qs(ps, pk, ct=ct):
                            T.op('dve', lambda e: e.tensor_scalar(qTs_a[:, h * 2 + ct, :], ps[:, 0:NS], 1.0 / 16.0, None, ALU.mult), reads=[pk], writes=['qTs_a'])
                        fm_block(Wv, wk, ct * 128, 128, ev_q, ev_qs if cur['smp'] else None)
                Wv, wk = w_next()
                for ct in range(2):
                    def ev_k(ps, pk, half, ct=ct):
                        T.op('dve', lambda e: e.reciprocal(rtA[0][:], Et[:, ct, half * 512:(half + 1) * 512]),
                             reads=[('E', half * 4 + i) for i in range(4)], writes=[('rtA', 0)])
                        T.op('dve', lambda e: e.tensor_tensor(kT[:, ct, half * 512:(half + 1) * 512], ps[:], rtA[0][:], ALU.mult),
                             reads=[pk, ('rtA', 0)], writes=[('kT', ct, half * 4 + i) for i in range(4)])

                    def ev_ks(ps, pk, ct=ct):
                        T.op('dve', lambda e: e.tensor_copy(kTs_a[:, h * 2 + ct, :], ps[:, 0:NS]), reads=[pk], writes=['kTs_a'])
                    fm_block(Wv, wk, ct * 128, 128, ev_k, ev_ks if cur['smp'] else None)
                Wv, wk = w_next()
                for tt in range(8):
                    ps, pk = tm_tile(Wv, wk, tt, 512)
                    T.op('act', lambda e: e.activation(vt[:, tt, :], ps[:], AF.Copy), reads=[pk], writes=[('v', tt)])
                if cur['smp']:
                    ps, pk = tm_tile_s(Wv, wk, 512)
                    T.op('act', lambda e: e.activation(vs_a[:, h * 512:(h + 1) * 512], ps[0:NS, :], AF.Copy), reads=[pk], writes=['vs_a'])

                S3 = St[:].rearrange("p (k v) -> p k v", k=2)
                Sb3 = Sbf[:].rearrange("p (k v) -> p k v", k=2)
                if not cur['first']:
                    T.dma('sp', S3, spg[h].rearrange("(p k) v -> p k v", k=2), sb='St', load=True, dram=('spg', h))
                else:
                    T.op('pool', lambda e: e.memset(St[:], 0.0), writes=['St'])
                T.op('pool', lambda e: e.tensor_copy(Sbf[:], St[:]), reads=['St'], writes=['Sbf'])
                Wg, wkg = w_next()
                Wm, wkm = w_next(hold=True)

                for c in range(8):
                    csl = slice(c * 128, (c + 1) * 128)
                    if full:
                        pg, pkg = tm_tile(Wg, wkg, c, 512)
                        T.op('act', lambda e: e.activation(slt[:], pg[:], AF.Silu), reads=[pkg], writes=['sl'])
                        T.op('pool', lambda e: e.tensor_tensor(slt[:], slt[:], C[:, C_GN:C_GN + 512], ALU.mult), reads=['sl', 'C'], writes=['sl'])
                        pa, pka = nps()
                        T.group('pe', [(lambda e, kt=kt: e.matmul(pa[:, 0:128], lhsT=kT[:, kt, csl], rhs=qT[:, kt, csl], start=(kt == 0), stop=(kt == 1)))
                                       for kt in range(2)], reads=[('kT', 0, c), ('kT', 1, c), ('qT', 0, c), ('qT', 1, c)], writes=[pka])
                        ai = cnt['at'] % 2
                        cnt['at'] += 1
                        at = ATm[ai]
                        T.op('dve', lambda e: e.tensor_tensor(at[:, 0:128], pa[:, 0:128], tri, ALU.mult), reads=[pka, 'C'], writes=[('ATm', ai, 0)])
                    for kt in range(2):
                        T.op('dve', lambda e, kt=kt: e.tensor_scalar(khT[:, kt, :], kT[:, kt, csl], Et[:, kt, c * 128 + 127:c * 128 + 128], None, ALU.mult),
                             reads=[('kT', kt, c), ('E', c)], writes=['khT'])
                    pt, pkt = nps()
                    ptb = pt.bitcast(BF16)
                    T.group('pe', [(lambda e, kt=kt: e.transpose(ptb[:, kt * 128:(kt + 1) * 128], khT[:, kt, :], identB)) for kt in range(2)],
                            reads=['khT', 'CB'], writes=[pkt])
                    ki = cnt['kh'] % 2
                    cnt['kh'] += 1
                    kh = khat[ki]
                    T.op('act', lambda e: e.activation(kh[:, 0:256], ptb[:, 0:256], AF.Copy), reads=[pkt], writes=[('khat', ki, 0), ('khat', ki, 1)])
                    if full:
                        pmm, pkm = tm_tile(Wm, wkm, c, 512)
                        T.op('act', lambda e: e.activation(smt[:], pmm[:], AF.Sigmoid), reads=[pkm], writes=['sm'])
                        T.op('pool', lambda e: e.tensor_tensor(Gt[:], slt[:], smt[:], ALU.mult), reads=['sl', 'sm'], writes=['G'])
                        po, pko = nps()
                        T.group('pe', [lambda e: e.matmul(po[:], lhsT=at[:, 0:128], rhs=vt[:, c, :], start=True, stop=False)] +
                                [(lambda e, kt=kt: e.matmul(po[:], lhsT=qT[:, kt, csl], rhs=Sb3[:, kt, :], start=False, stop=(kt == 1))) for kt in range(2)],
                                reads=[('ATm', ai, 0), ('v', c), ('qT', 0, c), ('qT', 1, c), 'Sbf'], writes=[pko])
                    for kt in range(2):
                        pu, pku = nps()
                        T.group('pe', [lambda e, kt=kt: e.matmul(pu[:], lhsT=kh[:, kt * 128:(kt + 1) * 128], rhs=vt[:, c, :], start=True, stop=True)],
                                reads=[('khat', ki, 0), ('khat', ki, 1), ('v', c)], writes=[pku])
                        T.op('dve', lambda e, kt=kt: e.scalar_tensor_tensor(S3[:, kt, :], S3[:, kt, :], Et[:, kt, c * 128 + 127:c * 128 + 128], pu[:], ALU.mult, ALU.add),
                             reads=['St', ('E', c), pku], writes=['St'])
                    if full:
                        T.op('pool', lambda e: e.tensor_copy(Sbf[:], St[:]), reads=['St'], writes=['Sbf'])
                        si = new_stat(2)
                        ss = stat[:, si:si + 1]
                        rs = stat[:, si + 1:si + 2]
                        T.op('act', lambda e: e.activation(junk5[:], po[:], AF.Square, scale=float(512 ** -0.5), accum_out=ss),
                             reads=[pko], writes=['junk5', ('stat', si)])
                        rstd_op(rs, ss, [('stat', si)], [('stat', si + 1)])
                        mi = cnt['ms'] % 2
                        cnt['ms'] += 1
                        ms = mst[mi]
                        T.op('dve', lambda e: e.scalar_tensor_tensor(ms[:], po[:], rs, Gt[:], ALU.mult, ALU.mult),
                             reads=[pko, ('stat', si + 1), 'G'], writes=[('mst', mi)])
                        T.dma('sp', mrga_l[cur['pidx']][1][c * 128:(c + 1) * 128, h * 512:(h + 1) * 512], ms[:], sb=('mst', mi), load=False, dram=('mrga', cur['pidx'], c, h))
                if cur['smp']:
                    pg, pkg = tm_tile_s(Wg, wkg, 512)
                    T.op('act', lambda e: e.activation(slt[0:NS, :], pg[0:NS, :], AF.Silu), reads=[pkg], writes=['sl'])
                    T.op('pool', lambda e: e.tensor_tensor(slt[0:NS, :], slt[0:NS, :], C[0:NS, C_GN:C_GN + 512], ALU.mult), reads=['sl', 'C'], writes=['sl'])
                    pmm, pkm = tm_tile_s(Wm, wkm, 512)
                    T.op('act', lambda e: e.activation(smt[0:NS, :], pmm[0:NS, :], AF.Sigmoid), reads=[pkm], writes=['sm'])
                    T.op('pool', lambda e: e.tensor_tensor(Gs_a[:, h * 512:(h + 1) * 512], slt[0:NS, :], smt[0:NS, :], ALU.mult), reads=['sl', 'sm'], writes=['Gs_a'])
                if not cur['first']:
                    store_state(ng[h].rearrange("(p k) v -> p k v", k=2), 512, ('ng', h))
                else:
                    store_state(spg[h].rearrange("(p k) v -> p k v", k=2), 512, ('spg', h))

            def rotary_evac(Wv, wk, hh, dstT, is_q, head, full):
                for half in range(2):
                    hs = slice(half * 512, (half + 1) * 512)
                    pss = []
                    for i in range(2):
                        ps, pk = nps()
                        c0 = hh * 256 + i * 128
                        T.group('pe', [(lambda e, kt=kt: e.matmul(ps[:], lhsT=Wv[:, kt, c0:c0 + 128], rhs=uT[:, kt, hs], start=(kt == 0), stop=(kt == 15)))
                                       for kt in range(16)], reads=[wk] + [('uT', half * 4 + j) for j in range(4)], writes=[pk])
                        pss.append((ps, pk))
                    (p1, k1), (p2, k2) = pss
                    ri = 0
                    cnt['rot'] += 1
                    tA, tB, tR = rtA[ri], rtB[ri], rtR[ri]
                    wkeys = lambda t: [(('qT' if is_q else 'kT'), hh * 2 + t, half * 4 + j) for j in range(4)]
                    T.op('dve', lambda e: e.tensor_tensor(tA[:], p1[:], cosT[:, hs], ALU.mult), reads=[k1, 'cos'], writes=[('rtA', ri)])
                    T.op('dve', lambda e: e.tensor_tensor(tB[:], p2[:], sinT[:, hs], ALU.mult), reads=[k2, 'sin'], writes=[('rtB', ri)])
                    if is_q:
                        T.op('pool', lambda e: e.tensor_tensor(tR[:], tA[:], tB[:], ALU.subtract), reads=[('rtA', ri), ('rtB', ri)], writes=[('rtR', ri)])
                        for cc in range(4):
                            T.op('pool', lambda e, cc=cc: e.tensor_tensor(dstT[:, hh * 2, half * 512 + cc * 128:half * 512 + (cc + 1) * 128], tR[:, cc * 128:(cc + 1) * 128],
                                                                        C[:, C_GAM + head * 128:C_GAM + (head + 1) * 128], ALU.mult),
                                 reads=[('rtR', ri), 'C'], writes=[wkeys(0)[cc]])
                    else:
                        T.op('pool', lambda e: e.tensor_tensor(dstT[:, hh * 2, hs], tA[:], tB[:], ALU.subtract), reads=[('rtA', ri), ('rtB', ri)], writes=wkeys(0))
                    T.op('dve', lambda e: e.tensor_tensor(tA[:], p1[:], sinT[:, hs], ALU.mult), reads=[k1, 'sin'], writes=[('rtA', ri)])
                    T.op('dve', lambda e: e.tensor_tensor(tB[:], p2[:], cosT[:, hs], ALU.mult), reads=[k2, 'cos'], writes=[('rtB', ri)])
                    if is_q:
                        T.op('pool', lambda e: e.tensor_tensor(tR[:], tA[:], tB[:], ALU.add), reads=[('rtA', ri), ('rtB', ri)], writes=[('rtR', ri)])
                        for cc in range(4):
                            T.op('pool', lambda e, cc=cc: e.tensor_tensor(dstT[:, hh * 2 + 1, half * 512 + cc * 128:half * 512 + (cc + 1) * 128], tR[:, cc * 128:(cc + 1) * 128],
                                                                        C[:, C_GAM + head * 128:C_GAM + (head + 1) * 128], ALU.mult),
                                 reads=[('rtR', ri), 'C'], writes=[wkeys(1)[cc]])
                    else:
                        T.op('pool', lambda e: e.tensor_tensor(dstT[:, hh * 2 + 1, hs], tA[:], tB[:], ALU.add), reads=[('rtA', ri), ('rtB', ri)], writes=wkeys(1))
                if full and cur['smp']:
                    pss = []
                    for i in range(2):
                        ps, pk = nps()
                        c0 = hh * 256 + i * 128
                        T.group('pe', [(lambda e, kt=kt: e.matmul(ps[:, 0:NS], lhsT=Wv[:, kt, c0:c0 + 128], rhs=uTs[:, kt, :], start=(kt == 0), stop=(kt == 15)))
                                       for kt in range(16)], reads=[wk, 'uTs'], writes=[pk])
                        pss.append((ps, pk))
                    (p1, k1), (p2, k2) = pss
                    dsts = qTs_b if is_q else kTs_b
                    dkey = 'qTs_b' if is_q else 'kTs_b'
                    o = C_CS if is_q else C_CS + 2
                    cs = C[:, o:o + 1]
                    sn = C[:, o + 1:o + 2]
                    ct0 = head * 2
                    T.op('dve', lambda e: e.tensor_scalar(Lx[:, 64:64 + NS], p2[:, 0:NS], sn, None, ALU.mult), reads=[k2, 'C'], writes=['Lx'])
                    T.op('dve', lambda e: e.scalar_tensor_tensor(dsts[:, ct0, :], p1[:, 0:NS], cs, Lx[:, 64:64 + NS], ALU.mult, ALU.subtract),
                         reads=[k1, 'C', 'Lx'], writes=[dkey])
                    T.op('dve', lambda e: e.tensor_scalar(Lx[:, 96:96 + NS], p1[:, 0:NS], sn, None, ALU.mult), reads=[k1, 'C'], writes=['Lx'])
                    T.op('dve', lambda e: e.scalar_tensor_tensor(dsts[:, ct0 + 1, :], p2[:, 0:NS], cs, Lx[:, 96:96 + NS], ALU.mult, ALU.add),
                         reads=[k2, 'C', 'Lx'], writes=[dkey])

            def ret_unit(j, full):
                emit_cvt(2)
                if j == 0:
                    emit_coll()
                heads = (2 * j, 2 * j + 1)
                if full:
                    Wv, wk = w_next()
                    for hh in range(2):
                        rotary_evac(Wv, wk, hh, qT, True, heads[hh], True)
                Wv, wk = w_next()
                for hh in range(2):
                    rotary_evac(Wv, wk, hh, kT, False, heads[hh], full)
                Wv, wk = w_next()
                for tt in range(8):
                    ps, pk = tm_tile(Wv, wk, tt, 512)
                    T.op('act', lambda e: e.activation(vt[:, tt, :], ps[:], AF.Copy), reads=[pk], writes=[('v', tt)])
                if cur['smp']:
                    ps, pk = tm_tile_s(Wv, wk, 512)
                    T.op('act', lambda e: e.activation(vs_b[:, j * 512:(j + 1) * 512], ps[0:NS, :], AF.Copy), reads=[pk], writes=['vs_b'])
                S4 = St[:].rearrange("p (h k v) -> p h k v", h=2, k=2)
                Sb4 = Sbf[:].rearrange("p (h k v) -> p h k v", h=2, k=2)
                if not cur['first']:
                    T.dma('sp', S4, spr[2 * j:2 * j + 2].rearrange("h (k p) v -> p h k v", p=128), sb='St', load=True, dram=('spr', j))
                else:
                    T.op('pool', lambda e: e.memset(St[:], 0.0), writes=['St'])
                T.op('pool', lambda e: e.tensor_copy(Sbf[:], St[:]), reads=['St'], writes=['Sbf'])
                Wg, wkg = w_next()
                Wm, wkm = w_next(hold=True)
                for c in range(8):
                    csl = slice(c * 128, (c + 1) * 128)
                    if full:
                        pg, pkg = tm_tile(Wg, wkg, c, 512)
                        T.op('act', lambda e: e.activation(slt[:], pg[:], AF.Silu), reads=[pkg], writes=['sl'])
                        ai = cnt['at'] % 2
                        cnt['at'] += 1
                        at = ATm[ai]
                        for hh in range(2):
                            pa, pka = nps()
                            T.group('pe', [(lambda e, i=i: e.matmul(pa[:, 0:128], lhsT=kT[:, hh * 2 + i, csl], rhs=qT[:, hh * 2 + i, csl], start=(i == 0), stop=(i == 1)))
                                           for i in range(2)], reads=[('kT', hh * 2, c), ('kT', hh * 2 + 1, c), ('qT', hh * 2, c), ('qT', hh * 2 + 1, c)], writes=[pka])
                            hd = heads[hh]
                            T.op('dve', lambda e, hh=hh, hd=hd, pa=pa: e.tensor_tensor(at[:, hh * 128:(hh + 1) * 128], pa[:, 0:128], C[:, C_DT + hd * 128:C_DT + (hd + 1) * 128], ALU.mult),
                                 reads=[pka, 'C'], writes=[('ATm', ai, hh)])
                    ki = cnt['kh'] % 2
                    cnt['kh'] += 1
                    kh = khat[ki]
                    for hh in range(2):
                        pt, pkt = nps()
                        ptb = pt.bitcast(BF16)
                        T.group('pe', [(lambda e, i=i: e.transpose(ptb[:, i * 128:(i + 1) * 128], kT[:, hh * 2 + i, csl], identB)) for i in range(2)],
                                reads=[('kT', hh * 2, c), ('kT', hh * 2 + 1, c), 'CB'], writes=[pkt])
                        hd = heads[hh]
                        T.op('act', lambda e, hh=hh, hd=hd, ptb=ptb: e.activation(kh[:, hh * 256:(hh + 1) * 256], ptb[:, 0:256], AF.Copy, scale=C[:, C_GKH + hd:C_GKH + hd + 1]),
                             reads=[pkt, 'C'], writes=[('khat', ki, hh)])
                    if full:
                        pmm, pkm = tm_tile(Wm, wkm, c, 512)
                        T.op('act', lambda e: e.activation(smt[:], pmm[:], AF.Sigmoid), reads=[pkm], writes=['sm'])
                        T.op('pool', lambda e: e.tensor_tensor(Gt[:], slt[:], smt[:], ALU.mult), reads=['sl', 'sm'], writes=['G'])
                        po, pko = nps()
                        fns = []
                        for hh in range(2):
                            osl = slice(hh * 256, (hh + 1) * 256)
                            fns.append(lambda e, hh=hh, osl=osl: e.matmul(po[:, osl], lhsT=at[:, hh * 128:(hh + 1) * 128], rhs=vt[:, c, osl], start=True, stop=False))
                            for i in range(2):
                                fns.append(lambda e, hh=hh, osl=osl, i=i: e.matmul(po[:, osl], lhsT=qT[:, hh * 2 + i, csl], rhs=Sb4[:, hh, i, :], start=False, stop=(i == 1)))
                        T.group('pe', fns, reads=[('ATm', ai, 0), ('ATm', ai, 1), ('v', c), 'Sbf'] + [('qT', t, c) for t in range(4)], writes=[pko])
                    for hh in range(2):
                        pu, pku = nps()
                        osl = slice(hh * 256, (hh + 1) * 256)
                        T.group('pe', [(lambda e, i=i: e.matmul(pu[:, i * 256:(i + 1) * 256], lhsT=kh[:, hh * 256 + i * 128:hh * 256 + (i + 1) * 128], rhs=vt[:, c, osl], start=True, stop=True))
                                       for i in range(2)], reads=[('khat', ki, hh), ('v', c)], writes=[pku])
                        hd = heads[hh]
                        T.op('dve', lambda e, hh=hh, hd=hd, pu=pu: e.scalar_tensor_tensor(St[:, hh * 512:(hh + 1) * 512], St[:, hh * 512:(hh + 1) * 512], C[:, C_G128 + hd:C_G128 + hd + 1], pu[:], ALU.mult, ALU.add),
                             reads=['St', pku, 'C'], writes=['St'])
                    if full:
                        T.op('pool', lambda e: e.tensor_copy(Sbf[:], St[:]), reads=['St'], writes=['Sbf'])
                        si = new_stat(4)
                        for hh in range(2):
                            osl = slice(hh * 256, (hh + 1) * 256)
                            T.op('act', lambda e, hh=hh, osl=osl: e.activation(junk5[:, osl], po[:, osl], AF.Square, scale=float(256 ** -0.5), accum_out=stat[:, si + hh:si + hh + 1]),
                                 reads=[pko], writes=['junk5', ('stat', si + hh)])
                        rstd_op(stat[:, si + 2:si + 4], stat[:, si:si + 2], [('stat', si), ('stat', si + 1)], [('stat', si + 2), ('stat', si + 3)])
                        mi = cnt['ms'] % 2
                        cnt['ms'] += 1
                        ms = mst[mi]
                        for hh in range(2):
                            osl = slice(hh * 256, (hh + 1) * 256)
                            T.op('dve', lambda e, hh=hh, osl=osl: e.scalar_tensor_tensor(ms[:, osl], po[:, osl], stat[:, si + 2 + hh:si + 3 + hh], Gt[:, osl], ALU.mult, ALU.mult),
                                 reads=[pko, ('stat', si + 2 + hh), 'G'], writes=[('mst', mi)])
                        T.dma('sp', mrgb_l[cur['pidx']][1][c * 128:(c + 1) * 128, j * 512:(j + 1) * 512], ms[:], sb=('mst', mi), load=False, dram=('mrgb', cur['pidx'], c, j))
                if cur['smp']:
                    pg, pkg = tm_tile_s(Wg, wkg, 512)
                    T.op('act', lambda e: e.activation(slt[0:NS, :], pg[0:NS, :], AF.Silu), reads=[pkg], writes=['sl'])
                    pmm, pkm = tm_tile_s(Wm, wkm, 512)
                    T.op('act', lambda e: e.activation(smt[0:NS, :], pmm[0:NS, :], AF.Sigmoid), reads=[pkm], writes=['sm'])
                    T.op('pool', lambda e: e.tensor_tensor(Gs_b[:, j * 512:(j + 1) * 512], slt[0:NS, :], smt[0:NS, :], ALU.mult), reads=['sl', 'sm'], writes=['Gs_b'])
                if not cur['first']:
                    store_state(nr[2 * j:2 * j + 2].rearrange("h (k p) v -> p h k v", p=128), 256, ('nr', j))
                else:
                    store_state(spr[2 * j:2 * j + 2].rearrange("h (k p) v -> p h k v", p=128), 256, ('spr', j))

            def coll_a(p_):
                mk = [('mrga', p_, c_, h_) for c_ in range(8) for h_ in range(2)]
                T.coll(mrga_l[p_][0].ap().opt(), gat_a_l[p_][0].ap().opt(), mk, [('gat_a', p_)], PAIRS)

            def coll_b(p_):
                mk = [('mrgb', p_, c_, h_) for c_ in range(8) for h_ in range(2)]
                T.coll(mrgb_l[p_][0].ap().opt(), gat_b_l[p_][0].ap().opt(), mk, [('gat_b', p_)], PAIRS)

            run_pass(0)
            coll_pending.extend([lambda: coll_a(0), lambda: coll_b(0)])
            run_pass(1)
            T.barrier()

        es2 = contextlib.ExitStack()
        with es2:
            def sb2(name, shape, dt=F32):
                return es2.enter_context(nc.sbuf_tensor("t_" + name, list(shape), dt))
            Sin = [sb2("Sin%d" % i, [128, 1024]) for i in range(3)]
            Sn = [sb2("Sn%d" % i, [128, 1024]) for i in range(3)]
            tmpv = [sb2("tmpv%d" % i, [128, 512]) for i in range(4)]
            QZ = [sb2("QZ%d" % i, [128, 2, NS], BF16) for i in range(4)]
            selb = [sb2("selb%d" % i, [NS, 128], BF16) for i in range(2)]
            for i in range(4):
                T.op('dve', lambda e, i=i: e.memset(QZ[i][:], 0.0), writes=[('QZ', i)])
            Sb_ = [sb2("Sb%d" % i, [128, 1024], BF16) for i in range(2)]
            stat2 = sb2("stat2", [NS, 64])
            junk2 = sb2("junk2", [NS, 512], BF16)
            mss = sb2("mss", [NS, 1024], BF16)
            it = [0]

            def sample_head(is_gla, h, hidx):
                dv = 512 if is_gla else 256
                sd, so = (sg, nsg) if is_gla else (sr, nsr)
                vs_ = vs_a if is_gla else vs_b
                vkey = 'vs_a' if is_gla else 'vs_b'
                qsrc = qTs_a if is_gla else qTs_b
                qkey = 'qTs_a' if is_gla else 'qTs_b'
                ksrc = kTs_a if is_gla else kTs_b
                kkey = 'kTs_a' if is_gla else 'kTs_b'
                po, pko = PS[7], ('ps', 7)
                reserved.add(7)

                if is_gla:
                    nsl, skey, nkey = 3, 'Sin', 'Sn'
                    SinV = [Sin[i][:, 0:1024] for i in range(3)]
                    SnV = [Sn[i][:, 0:1024] for i in range(3)]
                else:
                    nsl, skey, nkey = 6, 'SinR', 'SnR'
                    SinV = [Sin[i // 2][:, (i % 2) * 512:(i % 2 + 1) * 512] for i in range(6)]
                    SnV = [Sn[i // 2][:, (i % 2) * 512:(i % 2 + 1) * 512] for i in range(6)]

                def load(b, slot):
                    T.dma('sp', SinV[slot].rearrange("p (k v) -> p k v", k=2), (sd[b, h].rearrange("(p k) v -> p k v", k=2) if is_gla else sd[b, h].rearrange("(k p) v -> p k v", p=128)),
                          sb=(skey, slot), load=True)
                base = it[0]

                def stage_a(b):
                    pv, pkv = nps()
                    sbi = (base + b) % 2
                    sl_ = selb[sbi]
                    T.op('dve', lambda e: e.tensor_copy(sl_[:], identB[0:NS, b:b + 1].to_broadcast([NS, 128])), reads=['CB'], writes=[('selb', sbi)])
                    T.group('pe', [lambda e: e.matmul(pv[:, 0:dv], lhsT=sl_[:], rhs=vs_[:, h * dv:(h + 1) * dv], start=True, stop=True)],
                            reads=[('selb', sbi), vkey], writes=[pkv])
                    for kt in range(2):
                        ct = h * 2 + kt
                        ti = ((base + b) % 2) * 2 + kt
                        tv = tmpv[ti]
                        kcol = ksrc[:, ct, b:b + 1]
                        T.op('act', lambda e, tv=tv, kcol=kcol: e.activation(tv[:, 0:dv], pv[:, 0:dv], AF.Copy, scale=kcol),
                             reads=[pkv, kkey], writes=[('tmpv', ti)])

                def stage_b(b):
                    slot = (base + b) % nsl
                    sn_ = SnV[slot]
                    for kt in range(2):
                        ct = h * 2 + kt
                        ti = ((base + b) % 2) * 2 + kt
                        tv = tmpv[ti]
                        dec = aTs[:, ct, b:b + 1] if is_gla else C[:, C_G1 + h:C_G1 + h + 1]
                        T.op('dve', lambda e, kt=kt, tv=tv, dec=dec: e.scalar_tensor_tensor(sn_[:, kt * dv:(kt + 1) * dv], SinV[slot][:, kt * dv:(kt + 1) * dv], dec, tv[:, 0:dv], ALU.mult, ALU.add),
                             reads=[(skey, slot), ('tmpv', ti), 'aTs', 'C'], writes=[(nkey, slot)])
                    sbb = Sb_[(base + b) % 2]
                    T.op('pool', lambda e: e.tensor_copy(sbb[:, 0:2 * dv], sn_[:, 0:2 * dv]), reads=[(nkey, slot)], writes=[('Sb', (base + b) % 2)])
                    T.dma('sp', (so[b, h].rearrange("(p k) v -> p k v", k=2) if is_gla else so[b, h].rearrange("(k p) v -> p k v", p=128)), sn_[:, 0:2 * dv].rearrange("p (k v) -> p k v", k=2), sb=(nkey, slot), load=False)

                for b0 in range(nsl):
                    load(b0, (base + b0) % nsl)
                stage_a(0)
                stage_a(1)
                stage_b(0)
                for b in range(NS):
                    if b + 2 < NS:
                        stage_a(b + 2)
                    if b + 1 < NS:
                        stage_b(b + 1)
                    if b + nsl < NS:
                        load(b + nsl, (base + b + nsl) % nsl)
                    sbb = Sb_[(base + b) % 2]
                    qi = (base + b) % 4
                    qz = QZ[qi]
                    T.op('dve', lambda e: e.tensor_copy(qz[:, :, b:b + 1], qsrc[:, h * 2:h * 2 + 2, b:b + 1]), reads=[qkey], writes=[('QZ', qi)])
                    T.group('pe', [(lambda e, kt=kt: e.matmul(po[0:NS, 0:dv], lhsT=qz[:, kt, :], rhs=sbb[:, kt * dv:(kt + 1) * dv],
                                                             start=(b == 0 and kt == 0), stop=(b == NS - 1 and kt == 1))) for kt in range(2)],
                            reads=[('Sb', (base + b) % 2), ('QZ', qi)], writes=[pko])
                    if b >= 2:
                        qj = (base + b - 2) % 4
                        T.op('dve', lambda e: e.memset(QZ[qj][:, :, b - 2:b - 1], 0.0), writes=[('QZ', qj)])
                    yield
                for bb in (NS - 2, NS - 1):
                    qj = (base + bb) % 4
                    T.op('dve', lambda e: e.memset(QZ[qj][:, :, bb:bb + 1], 0.0), writes=[('QZ', qj)])
                it[0] = base + NS
                si = (hidx * 2) % 60
                ss = stat2[:, si:si + 1]
                rs = stat2[:, si + 1:si + 2]
                T.op('act', lambda e: e.activation(junk2[:, 0:dv], po[0:NS, 0:dv], AF.Square, scale=float(dv ** -0.5), accum_out=ss), reads=[pko], writes=['junk2', ('stat2', si)])
                rstd_op(rs, ss, [('stat2', si)], [('stat2', si + 1)])
                G = (Gs_a if is_gla else Gs_b)[:, h * dv:(h + 1) * dv]
                T.op('dve', lambda e: e.scalar_tensor_tensor(mss[:, h * dv:(h + 1) * dv], po[0:NS, 0:dv], rs, G, ALU.mult, ALU.mult),
                     reads=[pko, ('stat2', si + 1), 'Gs_a', 'Gs_b'], writes=['mss'])
                reserved.discard(7)

            def sample_gen():
                for h in range(2):
                    yield from sample_head(True, h, h)
                T.dma('sp', mrgsa[:, :], mss[:], sb='mss', load=False, dram='mrgsa')
                T.coll(mrgsa_f.ap().opt(), gat_sa_f.ap().opt(), ['mrgsa'], ['gat_sa'], PAIRS)
                T.barrier()
                it[0] = 0
                for h in range(4):
                    yield from sample_head(False, h, 2 + h)
                T.dma('sp', mrgsb[:, :], mss[:], sb='mss', load=False, dram='mrgsb')
                T.coll(mrgsb_f.ap().opt(), gat_sb_f.ap().opt(), ['mrgsb'], ['gat_sb'], PAIRS)

            NFB = sb2("nfb", [128, 2048])
            T.dma('sp', NFB[:], nfb[:, :], sb='NFB', load=True)
            WP = sb2("WP", [128, 2, 2048], BF16)
            T.dma('pool', WP[:].rearrange("p k c -> p (k c)"), wproj[:, :], sb='WP', load=True, max_dma_last_dim=8192)
            hbuf = sb2("h", [128, 2, 2048])
            mT = sb2("mT", [128, 16, 256], BF16)
            hnT = sb2("hnT", [128, 16, 256], BF16)
            ma_t = sb2("ma_t", [128, 2048], BF16); mb_t = sb2("mb_t", [128, 2048], BF16)
            ma2_t = sb2("ma2_t", [128, 2048], BF16); mb2_t = sb2("mb2_t", [128, 2048], BF16)
            pst = sb2("pst", [128, 256]); psc = sb2("psc", [128, 256], BF16); pTg = sb2("pTg", [128, 2, 256], BF16)
            hn = sb2("hn", [128, 2048], BF16)
            gsig = [sb2("gsig%d" % i, [128, 512]) for i in range(2)]
            stat3 = sb2("stat3", [128, 64])
            c3 = dict(st=0, g=0)

            def st3(n):
                i = c3['st']
                if i + n > 64:
                    i = 0
                c3['st'] = i + n
                return i

            def post_group(grp):
                for li, (tt, rows) in enumerate(grp):
                    for hf, (ta, tb) in enumerate(((ma_t, mb_t), (ma2_t, mb2_t))):
                        for rk in range(2):
                            if tt < 8:
                                ra = rk * NT + tt * 128
                                srca, srcb, dka, dkb = gat_a_l[hf][1][ra:ra + rows, :], gat_b_l[hf][1][ra:ra + rows, :], ('gat_a', hf), ('gat_b', hf)
                            else:
                                ra = rk * NS + hf * NS1
                                srca, srcb, dka, dkb = gat_sa[ra:ra + rows, :], gat_sb[ra:ra + rows, :], 'gat_sa', 'gat_sb'
                            T.dma('pool', ta[0:rows, rk * 1024:(rk + 1) * 1024], srca, sb=('mld', hf, 0, rk), load=True, dram=dka)
                            T.dma('pool', tb[0:rows, rk * 1024:(rk + 1) * 1024], srcb, sb=('mld', hf, 1, rk), load=True, dram=dkb)
                    mkeys = [('mld', hf_, ab_, rk_) for hf_ in range(2) for ab_ in range(2) for rk_ in range(2)]
                    T.op('dve', lambda e: e.tensor_tensor(ma_t[0:rows, :], ma_t[0:rows, :], mb_t[0:rows, :], ALU.add), reads=mkeys, writes=mkeys)
                    T.op('dve', lambda e: e.tensor_tensor(ma2_t[0:rows, :], ma2_t[0:rows, :], mb2_t[0:rows, :], ALU.add), reads=mkeys, writes=mkeys)
                    T.op('dve', lambda e: e.tensor_scalar(ma_t[0:rows, :], ma_t[0:rows, :], C[0:rows, C_MASK:C_MASK + 1], None, ALU.mult), reads=mkeys + ['C'], writes=mkeys)
                    T.op('dve', lambda e: e.scalar_tensor_tensor(ma_t[0:rows, :], ma2_t[0:rows, :], C[0:rows, C_MASK + 1:C_MASK + 2], ma_t[0:rows, :], ALU.mult, ALU.add),
                         reads=mkeys + ['C'], writes=mkeys + ['ma_t'])
                    for q in range(2):
                        pt, pkt = nps()
                        ptb = pt.bitcast(BF16)
                        T.group('pe', [(lambda e, j=j: e.transpose(ptb[:, j * 128:j * 128 + rows], ma_t[0:rows, (q * 8 + j) * 128:(q * 8 + j + 1) * 128], identB[0:rows, 0:rows]))
                                       for j in range(8)], reads=['ma_t', 'CB'] + mkeys, writes=[pkt])
                        src = ptb[:, 0:1024].rearrange("p (j t) -> p j t", j=8)[:, :, 0:rows]
                        T.op('act', lambda e, src=src, q=q: e.activation(mT[:, q * 8:(q + 1) * 8, li * 128:li * 128 + rows], src, AF.Copy), reads=[pkt], writes=[('mT', li)])
                    yield
                for cb in range(4):
                    Wv, wk = w_next()
                    for li, (tt, rows) in enumerate(grp):
                        ps, pk = nps()
                        for k0 in range(0, 16, 4):
                            T.group('pe', [(lambda e, kt=kt: e.matmul(ps[0:rows, :], lhsT=mT[:, kt, li * 128:li * 128 + rows], rhs=Wv[:, kt, :], start=(kt == 0), stop=(kt == 15)))
                                           for kt in range(k0, k0 + 4)], reads=[wk, ('mT', li)], writes=[pk])
                            if k0 < 12:
                                yield
                        xsrc = (xpost[tt * 128:tt * 128 + rows, cb * 512:(cb + 1) * 512] if tt < 8 else xs16[:, cb * 512:(cb + 1) * 512])
                        T.dma('pool', hbuf[0:rows, li, cb * 512:(cb + 1) * 512], xsrc, sb=('h', li, cb), load=True)
                        T.op('dve', lambda e, li=li, rows=rows, ps=ps: e.tensor_tensor(hbuf[0:rows, li, cb * 512:(cb + 1) * 512], ps[0:rows, :], hbuf[0:rows, li, cb * 512:(cb + 1) * 512], ALU.add),
                             reads=[pk, ('h', li, cb)], writes=[('h', li, cb)])
                        yield
                for li, (tt, rows) in enumerate(grp):
                    si = st3(2)
                    ss = stat3[0:rows, si:si + 1]
                    rs = stat3[0:rows, si + 1:si + 2]
                    T.op('act', lambda e: e.activation(hn[0:rows, :], hbuf[0:rows, li, :], AF.Square, scale=float(2048 ** -0.5), accum_out=ss),
                         reads=[('h', li, cb) for cb in range(4)], writes=['hn', ('stat3', si)])
                    rstd_op(rs, ss, [('stat3', si)], [('stat3', si + 1)])
                    T.op('dve', lambda e: e.tensor_scalar(hn[0:rows, :], hbuf[0:rows, li, :], rs, None, ALU.mult),
                         reads=[('h', li, cb) for cb in range(4)] + [('stat3', si + 1)], writes=['hn'])
                    for q in range(2):
                        pt, pkt = nps()
                        ptb = pt.bitcast(BF16)
                        T.group('pe', [(lambda e, j=j: e.transpose(ptb[:, j * 128:j * 128 + rows], hn[0:rows, (q * 8 + j) * 128:(q * 8 + j + 1) * 128], identB[0:rows, 0:rows]))
                                       for j in range(8)], reads=['hn', 'CB'], writes=[pkt])
                        src = ptb[:, 0:1024].rearrange("p (j t) -> p j t", j=8)[:, :, 0:rows]
                        gn_ = C[:, C_NP + q * 8:C_NP + (q + 1) * 8].unsqueeze(2).to_broadcast([128, 8, rows])
                        T.op('dve', lambda e, src=src, q=q, gn_=gn_: e.tensor_tensor(hnT[:, q * 8:(q + 1) * 8, li * 128:li * 128 + rows], src, gn_, ALU.mult),
                             reads=[pkt, 'C'], writes=[('hnT', li)])
                    psrc = pm[tt * 128:tt * 128 + rows, :] if tt < 8 else psd[:, :]
                    T.dma('pool', pst[0:rows, :], psrc, sb='pst', load=True)
                    T.op('pool', lambda e, rows=rows: e.tensor_copy(psc[0:rows, :], pst[0:rows, :]), reads=['pst'], writes=['psc'])
                    pt, pkt = nps()
                    ptb = pt.bitcast(BF16)
                    T.group('pe', [(lambda e, j=j, rows=rows: e.transpose(ptb[:, j * 128:j * 128 + rows], psc[0:rows, j * 128:(j + 1) * 128], identB[0:rows, 0:rows]))
                                   for j in range(2)], reads=['psc', 'CB'], writes=[pkt])
                    T.op('act', lambda e, rows=rows, ptb=ptb, li=li: e.activation(pTg[:, :, li * 128:li * 128 + rows],
                                                                               ptb[:, 0:256].rearrange("p (j t) -> p j t", j=2)[:, :, 0:rows], AF.Copy),
                         reads=[pkt], writes=[('pTg', li)])
                    yield
                for cb in range(4):
                    Wv, wk = w_next()
                    for li, (tt, rows) in enumerate(grp):
                        ps, pk = nps()
                        for k0 in range(0, 16, 4):
                            T.group('pe', [(lambda e, kt=kt: e.matmul(ps[0:rows, :], lhsT=hnT[:, kt, li * 128:li * 128 + rows], rhs=Wv[:, kt, :], start=(kt == 0), stop=(kt == 15)))
                                           for kt in range(k0, k0 + 4)], reads=[wk, ('hnT', li)], writes=[pk])
                            if k0 < 12:
                                yield
                        gi = c3['g'] % 2
                        c3['g'] += 1
                        gs = gsig[gi]
                        T.op('act', lambda e, rows=rows, ps=ps: e.activation(gs[0:rows, :], ps[0:rows, :], AF.Sigmoid), reads=[pk], writes=[('gsig', gi)])
                        pp, pkp = nps()
                        T.group('pe', [(lambda e, k2=k2: e.matmul(pp[0:rows, :], lhsT=pTg[:, k2, li * 128:li * 128 + rows], rhs=WP[:, k2, cb * 512:(cb + 1) * 512], start=(k2 == 0), stop=(k2 == 1)))
                                       for k2 in range(2)], reads=['WP', ('pTg', li)], writes=[pkp])
                        T.op('dve', lambda e, rows=rows, pp=pp: e.tensor_tensor(gs[0:rows, :], gs[0:rows, :], pp[0:rows, :], ALU.mult), reads=[('gsig', gi), pkp], writes=[('gsig', gi)])
                        T.op('pool', lambda e, rows=rows, li=li: e.tensor_tensor(hbuf[0:rows, li, cb * 512:(cb + 1) * 512], hbuf[0:rows, li, cb * 512:(cb + 1) * 512], gs[0:rows, :], ALU.add),
                             reads=[('gsig', gi), ('h', li, cb)], writes=[('h', li, cb)])
                        yield
                for li, (tt, rows) in enumerate(grp):
                    si = st3(2)
                    ss = stat3[0:rows, si:si + 1]
                    rs = stat3[0:rows, si + 1:si + 2]
                    T.op('act', lambda e: e.activation(hn[0:rows, :], hbuf[0:rows, li, :], AF.Square, scale=float(2048 ** -0.5), accum_out=ss),
                         reads=[('h', li, cb) for cb in range(4)], writes=['hn', ('stat3', si)])
                    rstd_op(rs, ss, [('stat3', si)], [('stat3', si + 1)])
                    dst = y_m[tt * 128:tt * 128 + rows, :] if tt < 8 else y_s[:, :]
                    for cb in range(4):
                        T.op('dve', lambda e, cb=cb: e.scalar_tensor_tensor(hbuf[0:rows, li, cb * 512:(cb + 1) * 512], hbuf[0:rows, li, cb * 512:(cb + 1) * 512], rs,
                                                                            NFB[0:rows, cb * 512:(cb + 1) * 512], ALU.mult, ALU.mult),
                             reads=[('h', li, cb), ('stat3', si + 1), 'NFB'], writes=[('h', li, cb)])
                        T.dma('pool', dst[:, cb * 512:(cb + 1) * 512], hbuf[0:rows, li, cb * 512:(cb + 1) * 512], sb=('h', li, cb), load=False)
                    yield

            def post_gen():
                for grp in ([(0, 128), (1, 128)], [(2, 128), (3, 128)], [(4, 128), (5, 128)], [(6, 128), (7, 128)]):
                    yield from post_group(grp)

            coll_a(1)
            coll_b(1)
            sg_ = sample_gen()
            pg_ = post_gen()
            alive_s, alive_p = True, True
            acc = 0.0
            while alive_s or alive_p:
                if alive_s:
                    acc += SAMPLE_PER_POST if alive_p else 1.0
                    while acc >= 1.0 and alive_s:
                        acc -= 1.0
                        try:
                            next(sg_)
                        except StopIteration:
                            alive_s = False
                if alive_p:
                    try:
                        next(pg_)
                    except StopIteration:
                        alive_p = False
            for _ in post_group([(8, NS1)]):
                pass
            T.barrier()
        esB.__exit__(None, None, None)
    return nc


_NC_CACHE = {}


def _host_consts(r, norm_mix, norm_ple, gla_norm, w_gla_up, b_gla):
    lg = _gammas()
    cst = np.zeros((128, NCST), np.float32)
    cst[:, C_ID:C_ID + 128] = np.eye(128, dtype=np.float32)
    i = np.arange(128)
    U = (i[:, None] <= i[None, :]).astype(np.float32)
    cst[:, C_TRI:C_TRI + 128] = U
    for hl in range(4):
        h = 4 * r + hl
        d = np.exp(-(i[:, None] + 1.0) * lg[h]) / 16.0 * U
        cst[:, C_DT + hl * 128:C_DT + (hl + 1) * 128] = d.astype(np.float32)
        cst[:, C_GAM + hl * 128:C_GAM + (hl + 1) * 128] = np.exp((i[None, :] + 1.0) * lg[h]).astype(np.float32)
        cst[:, C_GKH + hl] = (np.exp((127.0 - i) * lg[h]) / 16.0).astype(np.float32)
        cst[:, C_G128 + hl] = np.float32(np.exp(128.0 * lg[h]))
        cst[:, C_G1 + hl] = np.float32(np.exp(lg[h]))
    cst[:, C_MASK] = 1.0 if r == 0 else 0.0
    cst[:, C_MASK + 1] = 1.0 if r == 1 else 0.0
    cst[:, C_NM:C_NM + 16] = norm_mix.reshape(16, 128).T
    cst[:, C_NP:C_NP + 16] = norm_ple.reshape(16, 128).T
    cst[:, C_GN:C_GN + 512] = gla_norm.reshape(1, 512)
    inv = (1.0 / (np.float32(10000.0) ** np.linspace(0.0, 1.0, 128, dtype=np.float32))).astype(np.float32)
    ang = (np.float32(16384.0) * inv).astype(np.float32)
    cst[:, C_CS] = np.cos(ang); cst[:, C_CS + 1] = np.sin(ang)
    cst[:, C_CS + 2] = np.cos(ang) / 16.0; cst[:, C_CS + 3] = np.sin(ang) / 16.0
    for hl in range(2):
        cols = (2 * r + hl) * 256 + GLA_PERM
        cst[0:16, C_WUP + hl * 256:C_WUP + (hl + 1) * 256] = w_gla_up[:, cols]
        cst[16, C_WUP + hl * 256:C_WUP + (hl + 1) * 256] = b_gla[cols]
    cbf = np.zeros((128, NCBF), np.float32)
    cbf[:, B_ID:B_ID + 128] = np.eye(128, dtype=np.float32)
    return cst, cbf.astype(ml_dtypes.bfloat16), inv


def _rot_tables(inv, pos0):
    pos = (pos0 + np.arange(NT)).astype(np.float32)
    ang = pos[:, None] * inv[None, :]
    return np.ascontiguousarray(np.cos(ang).T.astype(np.float32)), np.ascontiguousarray(np.sin(ang).T.astype(np.float32))


def _pack_w(w, ncb):
    K, N = w.shape
    nk = K // 128
    a = w.reshape(nk, 128, N // ncb, ncb).transpose(1, 2, 0, 3)
    return np.ascontiguousarray(a.reshape(128, -1))


def kernel(x_prompt, x_sample, state_gla, state_ret, p_prompt, p_sample, norm_mix, w_in, w_gla_up, b_gla, gla_norm,
           w_out, norm_ple, w_ple_gate, w_ple_proj, norm_final):
    f = lambda a: np.asarray(a, dtype=np.float32)
    x_prompt, x_sample, state_gla, state_ret, p_prompt, p_sample = map(f, (x_prompt, x_sample, state_gla, state_ret, p_prompt, p_sample))
    w_in = f(w_in)[0]
    w3 = w_in.reshape(16, 128, NIN)
    per_rank = []
    for r in range(2):
        cst, cbf, inv = _host_consts(r, f(norm_mix)[0], f(norm_ple)[0], f(gla_norm)[0], f(w_gla_up)[0], f(b_gla)[0])
        wpk = np.empty((128, WTOT), np.float32)
        for lk in LKEYS:
            gk = lk if lk == 'r' else '%s%d' % (lk[:2], 2 * r + int(lk[2:]))
            c0, n = WBL[gk]
            cols = (c0 + GLA_PERM) if lk[:2] in ('qa', 'ka') else np.arange(c0, c0 + n)
            wpk[:, WOFF[lk]:WOFF[lk] + 16 * n] = w3[:, :, cols].transpose(1, 0, 2).reshape(128, 16 * n)
        per_rank.append((cst, cbf, wpk))
    woutp = _pack_w(f(w_out)[0], 512)
    wgatep = _pack_w(f(w_ple_gate)[0], 512)
    wprojp = np.ascontiguousarray(f(w_ple_proj)[0].reshape(2, 128, 2048).transpose(1, 0, 2).reshape(128, 4096))
    nfb = np.ascontiguousarray(np.broadcast_to(f(norm_final).reshape(1, 2048), (128, 2048)))
    cos0, sin0 = _rot_tables(inv, 0)
    cos1, sin1 = _rot_tables(inv, 1024)
    in_maps = []
    for c in range(8):
        b, r = c // 2, c % 2
        cst, cbf, wpk = per_rank[r]
        sl = slice(r * 1024, (r + 1) * 1024)
        s0 = 32 * b
        in_maps.append(dict(
            xp=np.ascontiguousarray(x_prompt[b, 0:1024]), xm=np.ascontiguousarray(x_prompt[b, 1024:2048]),
            xpost=np.ascontiguousarray(x_prompt[b, sl]),
            xs=np.ascontiguousarray(x_sample[s0:s0 + 32, 0]), xs16=np.ascontiguousarray(x_sample[s0 + 16 * r:s0 + 16 * r + 16, 0]),
            pm=np.ascontiguousarray(p_prompt[0, b, sl]), psd=np.ascontiguousarray(p_sample[0, s0 + 16 * r:s0 + 16 * r + 16, 0]),
            sg=np.ascontiguousarray(state_gla[0, s0:s0 + 32, 2 * r:2 * r + 2]), sr=np.ascontiguousarray(state_ret[0, s0:s0 + 32, 4 * r:4 * r + 4]),
            wpk=wpk, wout=woutp, wgate=wgatep, wproj=wprojp, cst=cst, cbf=cbf,
            cosm=cos1, sinm=sin1, cosp=cos0, sinp=sin0, nfb=nfb))
    if 'nc' not in _NC_CACHE:
        _NC_CACHE['nc'] = build_nc()
    res = run_bass_kernel_spmd(_NC_CACHE['nc'], in_maps, core_ids=list(range(8)))
    R = res.results
    y_prompt = np.empty((4, 2048, 2048), np.float32)
    y_sample = np.empty((128, 1, 2048), np.float32)
    ngp = np.empty((1, 4, 4, 256, 512), np.float32)
    nrp = np.empty((1, 4, 8, 256, 256), np.float32)
    ngs = np.empty((1, 128, 4, 256, 512), np.float32)
    nrs = np.empty((1, 128, 8, 256, 256), np.float32)
    for c in range(8):
        b, r = c // 2, c % 2
        s0 = 32 * b
        y_prompt[b, r * 1024:(r + 1) * 1024] = R[c]["y_m"]
        y_sample[s0 + 16 * r:s0 + 16 * r + 16, 0] = R[c]["y_s"]
        ngp[0, b, 2 * r:2 * r + 2] = R[c]["ng"]
        nrp[0, b, 4 * r:4 * r + 4] = R[c]["nr"]
        ngs[0, s0:s0 + 32, 2 * r:2 * r + 2] = R[c]["nsg"]
        nrs[0, s0:s0 + 32, 4 * r:4 * r + 4] = R[c]["nsr"]
    return (y_prompt, y_sample, ngp, nrp, ngs, nrs)
```

```python
import contextlib
import numpy as np
import ml_dtypes
import concourse.bass as bass
import concourse.mybir as mybir
from concourse.bass_utils import run_bass_kernel_spmd

F32 = mybir.dt.float32
BF16 = mybir.dt.bfloat16
AF = mybir.ActivationFunctionType
ALU = mybir.AluOpType

NIN = 18448
OFF = dict(qa=0, ka=1024, va=2048, ga=4096, ra=6144, qb=6160, kb=8208, vb=10256, gb=12304, ma=14352, mb=16400)
EPS = 1e-6
NT = 1024
NS = 32
NS1 = 16
SAMPLE_PER_POST = 0.7
STRICT_SAME_ENGINE = True

C_ID = 0
C_TRI = 128
C_DT = 256
C_GAM = C_DT + 512
C_GKH = C_GAM + 512
C_G128 = C_GKH + 4
C_G1 = C_G128 + 4
C_MASK = C_G1 + 4
C_NM = C_MASK + 2
C_NP = C_NM + 16
C_GN = C_NP + 16
C_CS = C_GN + 512
C_WUP = C_CS + 4
NCST = C_WUP + 512
B_ID = 0
NCBF = 128


GLA_PERM = (2 * np.arange(128)[None, :] + np.arange(2)[:, None]).reshape(256)


def _gammas():
    h = np.arange(8, dtype=np.float64)
    return np.log(1.0 - np.exp2(-5.0 - h))


def wblocks():
    bl = {}
    bl['r'] = (OFF['ra'], 16)
    for h in range(4):
        bl['qa%d' % h] = (OFF['qa'] + h * 256, 256)
        bl['ka%d' % h] = (OFF['ka'] + h * 256, 256)
        bl['va%d' % h] = (OFF['va'] + h * 512, 512)
        bl['ga%d' % h] = (OFF['ga'] + h * 512, 512)
        bl['ma%d' % h] = (OFF['ma'] + h * 512, 512)
        bl['qb%d' % h] = (OFF['qb'] + h * 512, 512)
        bl['kb%d' % h] = (OFF['kb'] + h * 512, 512)
        bl['vb%d' % h] = (OFF['vb'] + h * 512, 512)
        bl['gb%d' % h] = (OFF['gb'] + h * 512, 512)
        bl['mb%d' % h] = (OFF['mb'] + h * 512, 512)
    offs = {}
    o = 0
    for k, (c0, n) in bl.items():
        offs[k] = o
        o += 16 * n
    return bl, offs, o


WBL, _WOFF_G, _WTOT_G = wblocks()
LKEYS = ['r'] + ['%s%d' % (k, g) for g in range(2) for k in ('qa', 'ka', 'va', 'ga', 'ma', 'qb', 'kb', 'vb', 'gb', 'mb')]
WOFF = {}
_o = 0
for _k in LKEYS:
    WOFF[_k] = _o
    _o += 16 * WBL[_k][1]
WTOT = _o


class Res:
    __slots__ = ("lw", "rd", "dsem", "dcnt")

    def __init__(self):
        self.lw = None
        self.rd = {}
        self.dsem = None
        self.dcnt = 0


class Trk:
    def __init__(self, nc, es):
        self.nc = nc
        self.es = es
        self.E = {}
        for n, e in (("pe", nc.tensor), ("act", nc.scalar), ("dve", nc.vector), ("pool", nc.gpsimd), ("sp", nc.sync)):
            sem = es.enter_context(nc.semaphore("s_" + n))
            self.E[n] = dict(e=e, sem=sem, cnt=0, seen={})
        self.R = {}
        self.nd = 0

    def res(self, key):
        r = self.R.get(key)
        if r is None:
            r = self.R[key] = Res()
        return r

    def _deps(self, reads, writes):
        d = []
        for k in reads:
            r = self.res(k)
            if r.lw is not None:
                d.append((r.lw, 'raw'))
        for k in writes:
            r = self.res(k)
            if r.lw is not None:
                d.append((r.lw, 'waw'))
            for x in r.rd.values():
                d.append((x, 'war'))
        return d

    def _wait(self, en, deps):
        E = self.E[en]
        for (src, kind) in deps:
            tag, ref, val = src
            if tag == 'E':
                if ref == en and (en == 'pe' or (kind != 'raw' and not STRICT_SAME_ENGINE)):
                    continue
                key = ('E', ref)
                sem = self.E[ref]['sem']
            else:
                key = ('D', id(ref))
                sem = ref.dsem
            if E['seen'].get(key, 0) >= val:
                continue
            E['e'].wait_ge(sem, val)
            E['seen'][key] = val

    def _commit(self, me, mkey, reads, writes):
        for k in reads:
            self.res(k).rd[mkey] = me
        for k in writes:
            r = self.res(k)
            r.lw = me
            r.rd = {}

    def op(self, en, fn, reads=(), writes=()):
        self._wait(en, self._deps(reads, writes))
        E = self.E[en]
        ins = fn(E['e'])
        E['cnt'] += 1
        ins.then_inc(E['sem'], 1)
        self._commit(('E', en, E['cnt']), ('E', en), reads, writes)

    def group(self, en, fns, reads=(), writes=()):
        self._wait(en, self._deps(reads, writes))
        E = self.E[en]
        ins = None
        for f in fns:
            ins = f(E['e'])
        E['cnt'] += 1
        ins.then_inc(E['sem'], 1)
        self._commit(('E', en, E['cnt']), ('E', en), reads, writes)

    def dma(self, en, out, in_, sb, load, dram=None, extra=(), **kw):
        r = self.res(sb)
        if r.dsem is None:
            r.dsem = self.es.enter_context(self.nc.semaphore("d%d" % self.nd))
            self.nd += 1
        if load:
            reads = [dram] if dram is not None else []
            writes = [sb] + list(extra)
        else:
            reads = [sb]
            writes = [dram] if dram is not None else []
        self._wait(en, self._deps(reads, writes))
        ins = self.E[en]['e'].dma_start(out=out, in_=in_, **kw)
        r.dcnt += 16
        ins.then_inc(r.dsem, 16)
        self._commit(('D', r, r.dcnt), ('D', id(r)), reads, writes)

    def coll(self, ins_ap, outs_ap, reads, writes, groups):
        r = self.res(('coll', self.nd))
        r.dsem = self.es.enter_context(self.nc.semaphore("c%d" % self.nd))
        self.nd += 1
        self._wait('pool', self._deps(reads, writes))
        ins = self.nc.gpsimd.collective_compute("AllGather", ALU.bypass, replica_groups=groups, ins=[ins_ap], outs=[outs_ap])
        r.dcnt += 1
        ins.then_inc(r.dsem, 1)
        self._commit(('D', r, r.dcnt), ('D', id(r)), reads, writes)

    def barrier(self):
        for en, E in self.E.items():
            for e2, E2 in self.E.items():
                if e2 != en and E2['cnt'] > 0 and E['seen'].get(('E', e2), 0) < E2['cnt']:
                    E['e'].wait_ge(E2['sem'], E2['cnt'])
                    E['seen'][('E', e2)] = E2['cnt']
            for r in self.R.values():
                if r.dsem is not None and r.dcnt > 0 and E['seen'].get(('D', id(r)), 0) < r.dcnt:
                    E['e'].wait_ge(r.dsem, r.dcnt)
                    E['seen'][('D', id(r))] = r.dcnt


def build_nc(stop_after=None):
    nc = bass.Bass("TRN2", target_bir_lowering=False)

    def din(name, shape, dt=F32):
        return nc.dram_tensor(name, list(shape), dt, kind="ExternalInput").ap()

    def dout(name, shape, dt=F32):
        return nc.dram_tensor(name, list(shape), dt, kind="ExternalOutput").ap()

    def dint(name, shape, dt=F32):
        return nc.dram_tensor(name, list(shape), dt, kind="Internal").ap()

    xm = din("xm", [NT, 2048]); xp = din("xp", [NT, 2048]); xs = din("xs", [NS, 2048])
    xpost = din("xpost", [NT, 2048]); xs16 = din("xs16", [NS1, 2048])
    pm = din("pm", [NT, 256]); psd = din("psd", [NS1, 256])
    sg = din("sg", [NS, 2, 256, 512]); sr = din("sr", [NS, 4, 256, 256])
    wpk = din("wpk", [128, WTOT]); wout = din("wout", [128, 16 * 2048]); wgate = din("wgate", [128, 16 * 2048])
    wproj = din("wproj", [128, 2 * 2048])
    cst = din("cst", [128, NCST]); cbf = din("cbf", [128, NCBF], BF16)
    cosm = din("cosm", [128, NT]); sinm = din("sinm", [128, NT]); cosp = din("cosp", [128, NT]); sinp = din("sinp", [128, NT])
    nfb = din("nfb", [128, 2048])
    y_m = dout("y_m", [NT, 2048]); y_s = dout("y_s", [NS1, 2048])
    ng = dout("ng", [2, 256, 512]); nr = dout("nr", [4, 256, 256])
    nsg = dout("nsg", [NS, 2, 256, 512]); nsr = dout("nsr", [NS, 4, 256, 256])
    def cbuf(name, p, n):
        t = nc.dram_tensor(name, [p, n], F32, kind="Internal")
        return t, t.bitcast(BF16).ap().rearrange("p (a c) -> (p a) c", c=1024)
    mrga_l = [cbuf("mrga%d" % i, 128, 4096) for i in range(2)]; mrgb_l = [cbuf("mrgb%d" % i, 128, 4096) for i in range(2)]
    gat_a_l = [cbuf("gat_a%d" % i, 256, 4096) for i in range(2)]; gat_b_l = [cbuf("gat_b%d" % i, 256, 4096) for i in range(2)]
    mrgsa_f, mrgsa = cbuf("mrgsa", NS, 512); mrgsb_f, mrgsb = cbuf("mrgsb", NS, 512)
    gat_sa_f, gat_sa = cbuf("gat_sa", 2 * NS, 512); gat_sb_f, gat_sb = cbuf("gat_sb", 2 * NS, 512)
    spg = dint("spg", [2, 256, 512]); spr = dint("spr", [4, 256, 256])
    PAIRS = [[0, 1], [2, 3], [4, 5], [6, 7]]
    cur = dict(first=True, smp=False, pidx=0)

    es = contextlib.ExitStack()
    with es:
        T = Trk(nc, es)

        def sb(name, shape, dt=F32):
            return es.enter_context(nc.sbuf_tensor("t_" + name, list(shape), dt))

        PS = [es.enter_context(nc.psum_tensor("ps%d" % i, [128, 512], F32)) for i in range(8)]
        psi = [0]

        reserved = set()

        def rstd_op(o, i, reads, writes):
            T.op('act', lambda e: e.activation(o, i, AF.Ln, bias=EPS), reads=reads, writes=writes)
            T.op('act', lambda e: e.activation(o, o, AF.Exp, scale=-0.5), reads=writes, writes=writes)

        def nps():
            i = psi[0]
            while i in reserved:
                i = (i + 1) % 8
            psi[0] = (i + 1) % 8
            return PS[i], ('ps', i)

        W = [sb("W%d" % i, [128, 8192], BF16) for i in range(3)]
        C = sb("cst", [128, NCST])
        CB = sb("cbf", [128, NCBF], BF16)
        esB = contextlib.ExitStack()
        esB.__enter__()

        def sbB(name, shape, dt=F32):
            return esB.enter_context(nc.sbuf_tensor("t_" + name, list(shape), dt))
        qTs_a = sbB("qTs_a", [128, 4, NS]); kTs_a = sbB("kTs_a", [128, 4, NS]); aTs = sbB("aTs", [128, 4, NS])
        qTs_b = sbB("qTs_b", [128, 8, NS]); kTs_b = sbB("kTs_b", [128, 8, NS])
        vs_a = sbB("vs_a", [NS, 1024], BF16); vs_b = sbB("vs_b", [NS, 1024], BF16)
        Gs_a = sbB("Gs_a", [NS, 1024], BF16); Gs_b = sbB("Gs_b", [NS, 1024], BF16)

        T.dma('sp', C[:], cst[:, :], sb='C', load=True)
        T.dma('sp', CB[:], cbf[:, :], sb='CB', load=True)
        ident = C[:, C_ID:C_ID + 128]
        tri = C[:, C_TRI:C_TRI + 128]
        identB = CB[:, B_ID:B_ID + 128]

        wsched = []
        wstate = dict(issued=0, cur=-1)

        def w_issue_upto(n):
            while wstate['issued'] < min(n, len(wsched)):
                j = wstate['issued']
                src, ncols, nk = wsched[j][0:3]
                dk = wsched[j][3] if len(wsched[j]) > 3 else None
                T.dma('pool', W[j % 3][:, 0:nk * ncols], src, sb=('W', j % 3), load=True, dram=dk, max_dma_last_dim=8192)
                wstate['issued'] += 1

        def w_next(hold=False):
            wstate['cur'] += 1
            j = wstate['cur']
            w_issue_upto(j + 2 if hold else j + 3)
            src, ncols, nk = wsched[j][0:3]
            return W[j % 3][:, 0:nk * ncols].rearrange("p (k c) -> p k c", k=nk), ('W', j % 3)

        def sched_in(key):
            c0, n = WBL[key]
            wsched.append((wpk[:, WOFF[key]:WOFF[key] + 16 * n], n, 16))

        for pidx_ in range(2):
            sched_in('r')
            for g in range(2):
                for k in ('qa', 'ka', 'va', 'ga', 'ma'):
                    sched_in('%s%d' % (k, g))
                for k in ('qb', 'kb', 'vb', 'gb', 'mb'):
                    sched_in('%s%d' % (k, g))
        n_in_blocks = len(wsched)
        wcv = [dint("wcv%d" % i, [128, 8192], BF16) for i in range(8)]
        for grp in range(5):
            for cb in range(4):
                wsched.append((wcv[cb][:, :], 512, 16, ('wcv', cb)))
            for cb in range(4):
                wsched.append((wcv[4 + cb][:, :], 512, 16, ('wcv', 4 + cb)))

        es1 = contextlib.ExitStack()
        with es1:
            def sb1(name, shape, dt=F32):
                return es1.enter_context(nc.sbuf_tensor("t_" + name, list(shape), dt))

            uT = sb1("uT", [128, 16, NT], BF16)
            uTs = sb1("uTs", [128, 16, NS], BF16)
            Et = sb1("E", [128, 2, NT])
            qT = sb1("qT", [128, 4, NT], BF16); kT = sb1("kT", [128, 4, NT], BF16)
            vt = sb1("v", [128, 8, 512], BF16)
            cosT = sb1("cos", [128, NT]); sinT = sb1("sin", [128, NT])
            raT = sb1("raT", [17, NT + NS])
            xst = sb1("xst", [128, 2048]); junk5 = sb1("junk5", [128, 512], BF16)
            xsc = [sb1("xsc%d" % i, [128, 512]) for i in range(2)]
            stat = sb1("stat", [128, 64])
            rtA = [sb1("rtA%d" % i, [128, 512]) for i in range(1)]
            rtB = [sb1("rtB%d" % i, [128, 512]) for i in range(1)]
            rtR = [sb1("rtR%d" % i, [128, 512]) for i in range(1)]
            Lx = sb1("Lx", [128, 256]); Lt = sb1("Lt", [128, 256])
            ATm = [sb1("ATm%d" % i, [128, 256], BF16) for i in range(2)]
            khT = sb1("khT", [128, 2, 128], BF16)
            khat = [sb1("khat%d" % i, [128, 512], BF16) for i in range(2)]
            slt = sb1("sl", [128, 512]); smt = sb1("sm", [128, 512]); Gt = sb1("G", [128, 512])
            St = sb1("S", [128, 1024]); Sbf = sb1("Sbf", [128, 1024], BF16)
            mst = [sb1("mst%d" % i, [128, 512], BF16) for i in range(2)]
            cnt = dict(st=0, rot=0, at=0, kh=0, ms=0)

            def new_stat(n=1):
                i = cnt['st']
                if i + n > 64:
                    i = 0
                cnt['st'] = i + n
                return i

            T.op('pool', lambda e: e.memset(raT[:], 1.0), writes=[('raT', 0), ('raT', 1), ('raT', 's')])

            qTf = qT.bitcast(F32)[:].rearrange("p a b -> p (a b)")
            qT_keys = [('qT', ct_, c_) for ct_ in range(4) for c_ in range(8)]
            kTf = kT.bitcast(F32)[:].rearrange("p a b -> p (a b)")
            kT_keys = [('kT', ct_, c_) for ct_ in range(4) for c_ in range(8)]
            Ef = Et[:].rearrange("p a b -> p (a b)")
            E_keys = [('E', c_) for c_ in range(8)]
            nt_cnt = [0]

            def norm_transpose(x_ap, rows, dst, dst_key, col0, gain_off):
                alt = nt_cnt[0] % 4
                nt_cnt[0] += 1
                if alt == 0:
                    xst_, xk = xst, ['xst']
                    T.dma('sp', xst_[0:rows, :], x_ap, sb='xst', load=True)
                elif alt == 1:
                    xst_, xk = qTf, ['qTf'] + qT_keys
                    T.dma('sp', xst_[0:rows, :], x_ap, sb='qTf', load=True, extra=qT_keys)
                elif alt == 2:
                    xst_, xk = kTf, ['kTf'] + kT_keys
                    T.dma('sp', xst_[0:rows, :], x_ap, sb='kTf', load=True, extra=kT_keys)
                else:
                    xst_, xk = Ef, ['Ef'] + E_keys
                    T.dma('sp', xst_[0:rows, :], x_ap, sb='Ef', load=True, extra=E_keys)
                si = new_stat(2)
                ss = stat[0:rows, si:si + 1]
                rs = stat[0:rows, si + 1:si + 2]
                vj = vt[:, 0:4, :].rearrange("p a b -> p (a b)")
                T.op('act', lambda e: e.activation(vj[0:rows, :], xst_[0:rows, :], AF.Square, scale=float(2048 ** -0.5), accum_out=ss),
                     reads=xk, writes=[('v', 0), ('v', 1), ('v', 2), ('v', 3), ('stat', si)])
                rstd_op(rs, ss, [('stat', si)], [('stat', si + 1)])
                for cq in range(4):
                    xc = xsc[cq % 2]
                    T.op('dve', lambda e: e.tensor_scalar(xc[0:rows, :], xst_[0:rows, cq * 512:(cq + 1) * 512], rs, None, ALU.mult),
                         reads=xk + [('stat', si + 1)], writes=[('xsc', cq % 2)])
                    ps, pk = nps()
                    T.group('pe', [(lambda e, j=j: e.transpose(ps[:, j * rows:(j + 1) * rows], xc[0:rows, j * 128:(j + 1) * 128], ident[0:rows, 0:rows]))
                                   for j in range(4)], reads=[('xsc', cq % 2), 'C'], writes=[pk])
                    gsl = C[:, gain_off + cq * 4:gain_off + cq * 4 + 4].unsqueeze(2).to_broadcast([128, 4, rows])
                    T.op('dve', lambda e: e.tensor_tensor(dst[:, cq * 4:cq * 4 + 4, col0:col0 + rows], ps[:, 0:4 * rows].rearrange("p (j t) -> p j t", j=4), gsl, ALU.mult),
                         reads=[pk, 'C'], writes=[dst_key])

            def fm_block(Wv, wk, c0, ncol_tile, evac, evac_s):
                for half in range(2):
                    ps, pk = nps()
                    T.group('pe', [(lambda e, kt=kt: e.matmul(ps[0:ncol_tile, :], lhsT=Wv[:, kt, c0:c0 + ncol_tile], rhs=uT[:, kt, half * 512:(half + 1) * 512],
                                                             start=(kt == 0), stop=(kt == 15))) for kt in range(16)],
                            reads=[wk] + [('uT', half * 4 + i) for i in range(4)], writes=[pk])
                    evac(ps, pk, half)
                if evac_s is not None:
                    ps, pk = nps()
                    T.group('pe', [(lambda e, kt=kt: e.matmul(ps[0:ncol_tile, 0:NS], lhsT=Wv[:, kt, c0:c0 + ncol_tile], rhs=uTs[:, kt, :],
                                                             start=(kt == 0), stop=(kt == 15))) for kt in range(16)],
                            reads=[wk, 'uTs'], writes=[pk])
                    evac_s(ps, pk)

            def tm_tile(Wv, wk, tt, ncols):
                ps, pk = nps()
                T.group('pe', [(lambda e, kt=kt: e.matmul(ps[:, 0:ncols], lhsT=uT[:, kt, tt * 128:(tt + 1) * 128], rhs=Wv[:, kt, 0:ncols],
                                                         start=(kt == 0), stop=(kt == 15))) for kt in range(16)],
                        reads=[wk, ('uT', tt)], writes=[pk])
                return ps, pk

            def tm_tile_s(Wv, wk, ncols):
                ps, pk = nps()
                T.group('pe', [(lambda e, kt=kt: e.matmul(ps[0:NS, 0:ncols], lhsT=uTs[:, kt, :], rhs=Wv[:, kt, 0:ncols],
                                                         start=(kt == 0), stop=(kt == 15))) for kt in range(16)],
                        reads=[wk, 'uTs'], writes=[pk])
                return ps, pk

            def store_state(dst, nk_free, key):
                if nk_free == 512:
                    src = St[:].rearrange("p (k v) -> p k v", k=2)
                else:
                    src = St[:].rearrange("p (h k v) -> p h k v", h=2, k=2)
                T.dma('sp', dst, src, sb='St', load=False, dram=key)

            def run_pass(pidx):
                full = True
                cur['pidx'] = pidx
                cur['first'] = (pidx == 0)
                cur['smp'] = (pidx == 1)
                x_d = xm if pidx == 1 else xp
                T.dma('sp', cosT[:], (cosm if pidx == 1 else cosp)[:, :], sb='cos', load=True)
                T.dma('sp', sinT[:], (sinm if pidx == 1 else sinp)[:, :], sb='sin', load=True)
                for tt in range(8):
                    norm_transpose(x_d[tt * 128:(tt + 1) * 128, :], 128, uT, ('uT', tt), tt * 128, C_NM)
                if cur['smp']:
                    norm_transpose(xs[:, :], NS, uTs, 'uTs', 0, C_NM)

                Wv, wk = w_next()

                def ev_r(ps, pk, half):
                    T.op('act', lambda e: e.activation(raT[0:16, half * 512:(half + 1) * 512], ps[0:16, :], AF.Copy),
                         reads=[pk], writes=[('raT', half)])

                def ev_rs(ps, pk):
                    T.op('act', lambda e: e.activation(raT[0:16, NT:NT + NS], ps[0:16, 0:NS], AF.Copy), reads=[pk], writes=[('raT', 's')])

                fm_block(Wv, wk, 0, 16, ev_r, ev_rs if cur['smp'] else None)

                for g in range(2):
                    gla_unit(g, full)
                    ret_unit(g, full)

            cvt_pending = [(i, (wout if i < 4 else wgate)[:, (i % 4) * 8192:(i % 4 + 1) * 8192]) for i in range(8)]

            coll_pending = []

            def emit_coll():
                if coll_pending and cur['pidx'] == 1:
                    coll_pending.pop(0)()

            def emit_cvt(n):
                for _ in range(n):
                    if cvt_pending and cur['pidx'] == 0:
                        i, src = cvt_pending.pop(0)
                        T.dma('pool', wcv[i][:, :], src, sb=('wcvs', i), load=True, extra=[('wcv', i)], max_dma_last_dim=8192)

            def gla_unit(h, full):
                emit_cvt(2)
                if h == 1:
                    emit_coll()
                wup = C[0:17, C_WUP + h * 256:C_WUP + (h + 1) * 256]
                for c in range(8):
                    ps, pk = nps()
                    T.group('pe', [lambda e: e.matmul(ps[:, 0:256], lhsT=raT[0:17, c * 128:(c + 1) * 128], rhs=wup, start=True, stop=True)],
                            reads=[('raT', c // 4), 'C'], writes=[pk])
                    T.op('act', lambda e: e.activation(Lx[:], ps[:, 0:256], AF.Exp, scale=-1.0), reads=[pk], writes=['Lx'])
                    T.op('act', lambda e: e.activation(Lt[:], Lx[:], AF.Ln, bias=1.0), reads=['Lx'], writes=['Lt'])
                    ps2, pk2 = nps()
                    T.group('pe', [(lambda e, kt=kt: e.matmul(ps2[:, kt * 128:(kt + 1) * 128], lhsT=Lt[:, kt * 128:(kt + 1) * 128], rhs=tri, start=True, stop=True))
                                   for kt in range(2)], reads=['Lt', 'C'], writes=[pk2])
                    cum = ps2[:, 0:256].rearrange("p (k t) -> p k t", k=2)
                    T.op('act', lambda e: e.activation(Et[:, :, c * 128:(c + 1) * 128], cum, AF.Exp, scale=-1.0 / 16.0), reads=[pk2], writes=[('E', c)])
                if cur['smp']:
                    for kt in range(2):
                        ps, pk = nps()
                        T.group('pe', [lambda e: e.matmul(ps[:, 0:NS], lhsT=C[0:17, C_WUP + h * 256 + kt * 128:C_WUP + h * 256 + (kt + 1) * 128],
                                                          rhs=raT[0:17, NT:NT + NS], start=True, stop=True)], reads=[('raT', 's'), 'C'], writes=[pk])
                        T.op('act', lambda e: e.activation(Lx[:, 0:NS], ps[:, 0:NS], AF.Exp, scale=-1.0), reads=[pk], writes=['Lx'])
                        T.op('act', lambda e: e.activation(Lx[:, 32:32 + NS], Lx[:, 0:NS], AF.Ln, bias=1.0), reads=['Lx'], writes=['Lx'])
                        T.op('act', lambda e: e.activation(aTs[:, h * 2 + kt, :], Lx[:, 32:32 + NS], AF.Exp, scale=-1.0 / 16.0), reads=['Lx'], writes=['aTs'])

                if full:
                    Wv, wk = w_next()
                    for ct in range(2):
                        def ev_q(ps, pk, half, ct=ct):
                            T.op('dve', lambda e: e.scalar_tensor_tensor(qT[:, ct, half * 512:(half + 1) * 512], ps[:], 1.0 / 16.0,
                                                                         Et[:, ct, half * 512:(half + 1) * 512], ALU.mult, ALU.mult),
                                 reads=[pk] + [('E', half * 4 + i) for i in range(4)], writes=[('qT', ct, half * 4 + i) for i in range(4)])

                        def ev_qs(ps, pk, ct=ct):
                            T.op('dve', lambda e: e.tensor_scalar(qTs_a[:, h * 2 + ct, :], ps[:, 0:NS], 1.0 / 16.0, None, ALU.mult), reads=[pk], writes=['qTs_a'])
                        fm_block(Wv, wk, ct * 128, 128, ev_q, ev_qs if cur['smp'] else None)
                Wv, wk = w_next()
                for ct in range(2):
                    def ev_k(ps, pk, half, ct=ct):
                        T.op('dve', lambda e: e.reciprocal(rtA[0][:], Et[:, ct, half * 512:(half + 1) * 512]),
                             reads=[('E', half * 4 + i) for i in range(4)], writes=[('rtA', 0)])
                        T.op('dve', lambda e: e.tensor_tensor(kT[:, ct, half * 512:(half + 1) * 512], ps[:], rtA[0][:], ALU.mult),
                             reads=[pk, ('rtA', 0)], writes=[('kT', ct, half * 4 + i) for i in range(4)])

                    def ev_ks(ps, pk, ct=ct):
                        T.op('dve', lambda e: e.tensor_copy(kTs_a[:, h * 2 + ct, :], ps[:, 0:NS]), reads=[pk], writes=['kTs_a'])
                    fm_block(Wv, wk, ct * 128, 128, ev_k, ev_ks if cur['smp'] else None)
                Wv, wk = w_next()
                for tt in range(8):
                    ps, pk = tm_tile(Wv, wk, tt, 512)
                    T.op('act', lambda e: e.activation(vt[:, tt, :], ps[:], AF.Copy), reads=[pk], writes=[('v', tt)])
                if cur['smp']:
                    ps, pk = tm_tile_s(Wv, wk, 512)
                    T.op('act', lambda e: e.activation(vs_a[:, h * 512:(h + 1) * 512], ps[0:NS, :], AF.Copy), reads=[pk], writes=['vs_a'])

                S3 = St[:].rearrange("p (k v) -> p k v", k=2)
                Sb3 = Sbf[:].rearrange("p (k v) -> p k v", k=2)
                if not cur['first']:
                    T.dma('sp', S3, spg[h].rearrange("(p k) v -> p k v", k=2), sb='St', load=True, dram=('spg', h))
                else:
                    T.op('pool', lambda e: e.memset(St[:], 0.0), writes=['St'])
                T.op('pool', lambda e: e.tensor_copy(Sbf[:], St[:]), reads=['St'], writes=['Sbf'])
                Wg, wkg = w_next()
                Wm, wkm = w_next(hold=True)

                for c in range(8):
                    csl = slice(c * 128, (c + 1) * 128)
                    if full:
                        pg, pkg = tm_tile(Wg, wkg, c, 512)
                        T.op('act', lambda e: e.activation(slt[:], pg[:], AF.Sigmoid), reads=[pkg], writes=['sl'])
                        T.op('dve', lambda e: e.tensor_tensor(slt[:], pg[:], slt[:], ALU.mult), reads=[pkg, 'sl'], writes=['sl'])
                        T.op('pool', lambda e: e.tensor_tensor(slt[:], slt[:], C[:, C_GN:C_GN + 512], ALU.mult), reads=['sl', 'C'], writes=['sl'])
                        pa, pka = nps()
                        T.group('pe', [(lambda e, kt=kt: e.matmul(pa[:, 0:128], lhsT=kT[:, kt, csl], rhs=qT[:, kt, csl], start=(kt == 0), stop=(kt == 1)))
                                       for kt in range(2)], reads=[('kT', 0, c), ('kT', 1, c), ('qT', 0, c), ('qT', 1, c)], writes=[pka])
                        ai = cnt['at'] % 2
                        cnt['at'] += 1
                        at = ATm[ai]
                        T.op('dve', lambda e: e.tensor_tensor(at[:, 0:128], pa[:, 0:128], tri, ALU.mult), reads=[pka, 'C'], writes=[('ATm', ai, 0)])
                    for kt in range(2):
                        T.op('dve', lambda e, kt=kt: e.tensor_scalar(khT[:, kt, :], kT[:, kt, csl], Et[:, kt, c * 128 + 127:c * 128 + 128], None, ALU.mult),
                             reads=[('kT', kt, c), ('E', c)], writes=['khT'])
                    pt, pkt = nps()
                    ptb = pt.bitcast(BF16)
                    T.group('pe', [(lambda e, kt=kt: e.transpose(ptb[:, kt * 128:(kt + 1) * 128], khT[:, kt, :], identB)) for kt in range(2)],
                            reads=['khT', 'CB'], writes=[pkt])
                    ki = cnt['kh'] % 2
                    cnt['kh'] += 1
                    kh = khat[ki]
                    T.op('act', lambda e: e.activation(kh[:, 0:256], ptb[:, 0:256], AF.Copy), reads=[pkt], writes=[('khat', ki, 0), ('khat', ki, 1)])
                    if full:
                        pmm, pkm = tm_tile(Wm, wkm, c, 512)
                        T.op('act', lambda e: e.activation(smt[:], pmm[:], AF.Sigmoid), reads=[pkm], writes=['sm'])
                        T.op('pool', lambda e: e.tensor_tensor(Gt[:], slt[:], smt[:], ALU.mult), reads=['sl', 'sm'], writes=['G'])
                        po, pko = nps()
                        T.group('pe', [lambda e: e.matmul(po[:], lhsT=at[:, 0:128], rhs=vt[:, c, :], start=True, stop=False)] +
                                [(lambda e, kt=kt: e.matmul(po[:], lhsT=qT[:, kt, csl], rhs=Sb3[:, kt, :], start=False, stop=(kt == 1))) for kt in range(2)],
                                reads=[('ATm', ai, 0), ('v', c), ('qT', 0, c), ('qT', 1, c), 'Sbf'], writes=[pko])
                    for kt in range(2):
                        pu, pku = nps()
                        T.group('pe', [lambda e, kt=kt: e.matmul(pu[:], lhsT=kh[:, kt * 128:(kt + 1) * 128], rhs=vt[:, c, :], start=True, stop=True)],
                                reads=[('khat', ki, 0), ('khat', ki, 1), ('v', c)], writes=[pku])
                        T.op('dve', lambda e, kt=kt: e.scalar_tensor_tensor(S3[:, kt, :], S3[:, kt, :], Et[:, kt, c * 128 + 127:c * 128 + 128], pu[:], ALU.mult, ALU.add),
                             reads=['St', ('E', c), pku], writes=['St'])
                    if full:
                        T.op('pool', lambda e: e.tensor_copy(Sbf[:], St[:]), reads=['St'], writes=['Sbf'])
                        si = new_stat(2)
                        ss = stat[:, si:si + 1]
                        rs = stat[:, si + 1:si + 2]
                        T.op('act', lambda e: e.activation(junk5[:], po[:], AF.Square, scale=float(512 ** -0.5), accum_out=ss),
                             reads=[pko], writes=['junk5', ('stat', si)])
                        rstd_op(rs, ss, [('stat', si)], [('stat', si + 1)])
                        mi = cnt['ms'] % 2
                        cnt['ms'] += 1
                        ms = mst[mi]
                        T.op('dve', lambda e: e.scalar_tensor_tensor(ms[:], po[:], rs, Gt[:], ALU.mult, ALU.mult),
                             reads=[pko, ('stat', si + 1), 'G'], writes=[('mst', mi)])
                        T.dma('sp', mrga_l[cur['pidx']][1][c * 128:(c + 1) * 128, h * 512:(h + 1) * 512], ms[:], sb=('mst', mi), load=False, dram=('mrga', cur['pidx'], c, h))
                if cur['smp']:
                    pg, pkg = tm_tile_s(Wg, wkg, 512)
                    T.op('act', lambda e: e.activation(slt[0:NS, :], pg[0:NS, :], AF.Silu), reads=[pkg], writes=['sl'])
                    T.op('pool', lambda e: e.tensor_tensor(slt[0:NS, :], slt[0:NS, :], C[0:NS, C_GN:C_GN + 512], ALU.mult), reads=['sl', 'C'], writes=['sl'])
                    pmm, pkm = tm_tile_s(Wm, wkm, 512)
                    T.op('act', lambda e: e.activation(smt[0:NS, :], pmm[0:NS, :], AF.Sigmoid), reads=[pkm], writes=['sm'])
                    T.op('pool', lambda e: e.tensor_tensor(Gs_a[:, h * 512:(h + 1) * 512], slt[0:NS, :], smt[0:NS, :], ALU.mult), reads=['sl', 'sm'], writes=['Gs_a'])
                if not cur['first']:
                    store_state(ng[h].rearrange("(p k) v -> p k v", k=2), 512, ('ng', h))
                else:
                    store_state(spg[h].rearrange("(p k) v -> p k v", k=2), 512, ('spg', h))

            def rotary_evac(Wv, wk, hh, dstT, is_q, head, full):
                for half in range(2):
                    hs = slice(half * 512, (half + 1) * 512)
                    pss = []
                    for i in range(2):
                        ps, pk = nps()
                        c0 = hh * 256 + i * 128
                        T.group('pe', [(lambda e, kt=kt: e.matmul(ps[:], lhsT=Wv[:, kt, c0:c0 + 128], rhs=uT[:, kt, hs], start=(kt == 0), stop=(kt == 15)))
                                       for kt in range(16)], reads=[wk] + [('uT', half * 4 + j) for j in range(4)], writes=[pk])
                        pss.append((ps, pk))
                    (p1, k1), (p2, k2) = pss
                    ri = 0
                    cnt['rot'] += 1
                    tA, tB, tR = rtA[ri], rtB[ri], rtR[ri]
                    wkeys = lambda t: [(('qT' if is_q else 'kT'), hh * 2 + t, half * 4 + j) for j in range(4)]
                    T.op('dve', lambda e: e.tensor_tensor(tA[:], p1[:], cosT[:, hs], ALU.mult), reads=[k1, 'cos'], writes=[('rtA', ri)])
                    T.op('dve', lambda e: e.tensor_tensor(tB[:], p2[:], sinT[:, hs], ALU.mult), reads=[k2, 'sin'], writes=[('rtB', ri)])
                    if is_q:
                        T.op('pool', lambda e: e.tensor_tensor(tR[:], tA[:], tB[:], ALU.subtract), reads=[('rtA', ri), ('rtB', ri)], writes=[('rtR', ri)])
                        for cc in range(4):
                            T.op('pool', lambda e, cc=cc: e.tensor_tensor(dstT[:, hh * 2, half * 512 + cc * 128:half * 512 + (cc + 1) * 128], tR[:, cc * 128:(cc + 1) * 128],
                                                                        C[:, C_GAM + head * 128:C_GAM + (head + 1) * 128], ALU.mult),
                                 reads=[('rtR', ri), 'C'], writes=[wkeys(0)[cc]])
                    else:
                        T.op('pool', lambda e: e.tensor_tensor(dstT[:, hh * 2, hs], tA[:], tB[:], ALU.subtract), reads=[('rtA', ri), ('rtB', ri)], writes=wkeys(0))
                    T.op('dve', lambda e: e.tensor_tensor(tA[:], p1[:], sinT[:, hs], ALU.mult), reads=[k1, 'sin'], writes=[('rtA', ri)])
                    T.op('dve', lambda e: e.tensor_tensor(tB[:], p2[:], cosT[:, hs], ALU.mult), reads=[k2, 'cos'], writes=[('rtB', ri)])
                    if is_q:
                        T.op('pool', lambda e: e.tensor_tensor(tR[:], tA[:], tB[:], ALU.add), reads=[('rtA', ri), ('rtB', ri)], writes=[('rtR', ri)])
                        for cc in range(4):
                            T.op('pool', lambda e, cc=cc: e.tensor_tensor(dstT[:, hh * 2 + 1, half * 512 + cc * 128:half * 512 + (cc + 1) * 128], tR[:, cc * 128:(cc + 1) * 128],
                                                                        C[:, C_GAM + head * 128:C_GAM + (head + 1) * 128], ALU.mult),
                                 reads=[('rtR', ri), 'C'], writes=[wkeys(1)[cc]])
                    else:
                        T.op('pool', lambda e: e.tensor_tensor(dstT[:, hh * 2 + 1, hs], tA[:], tB[:], ALU.add), reads=[('rtA', ri), ('rtB', ri)], writes=wkeys(1))
                if full and cur['smp']:
                    pss = []
                    for i in range(2):
                        ps, pk = nps()
                        c0 = hh * 256 + i * 128
                        T.group('pe', [(lambda e, kt=kt: e.matmul(ps[:, 0:NS], lhsT=Wv[:, kt, c0:c0 + 128], rhs=uTs[:, kt, :], start=(kt == 0), stop=(kt == 15)))
                                       for kt in range(16)], reads=[wk, 'uTs'], writes=[pk])
                        pss.append((ps, pk))
                    (p1, k1), (p2, k2) = pss
                    dsts = qTs_b if is_q else kTs_b
                    dkey = 'qTs_b' if is_q else 'kTs_b'
                    o = C_CS if is_q else C_CS + 2
                    cs = C[:, o:o + 1]
                    sn = C[:, o + 1:o + 2]
                    ct0 = head * 2
                    T.op('dve', lambda e: e.tensor_scalar(Lx[:, 64:64 + NS], p2[:, 0:NS], sn, None, ALU.mult), reads=[k2, 'C'], writes=['Lx'])
                    T.op('dve', lambda e: e.scalar_tensor_tensor(dsts[:, ct0, :], p1[:, 0:NS], cs, Lx[:, 64:64 + NS], ALU.mult, ALU.subtract),
                         reads=[k1, 'C', 'Lx'], writes=[dkey])
                    T.op('dve', lambda e: e.tensor_scalar(Lx[:, 96:96 + NS], p1[:, 0:NS], sn, None, ALU.mult), reads=[k1, 'C'], writes=['Lx'])
                    T.op('dve', lambda e: e.scalar_tensor_tensor(dsts[:, ct0 + 1, :], p2[:, 0:NS], cs, Lx[:, 96:96 + NS], ALU.mult, ALU.add),
                         reads=[k2, 'C', 'Lx'], writes=[dkey])

            def ret_unit(j, full):
                emit_cvt(2)
                if j == 0:
                    emit_coll()
                heads = (2 * j, 2 * j + 1)
                if full:
                    Wv, wk = w_next()
                    for hh in range(2):
                        rotary_evac(Wv, wk, hh, qT, True, heads[hh], True)
                Wv, wk = w_next()
                for hh in range(2):
                    rotary_evac(Wv, wk, hh, kT, False, heads[hh], full)
                Wv, wk = w_next()
                for tt in range(8):
                    ps, pk = tm_tile(Wv, wk, tt, 512)
                    T.op('act', lambda e: e.activation(vt[:, tt, :], ps[:], AF.Copy), reads=[pk], writes=[('v', tt)])
                if cur['smp']:
                    ps, pk = tm_tile_s(Wv, wk, 512)
                    T.op('act', lambda e: e.activation(vs_b[:, j * 512:(j + 1) * 512], ps[0:NS, :], AF.Copy), reads=[pk], writes=['vs_b'])
                S4 = St[:].rearrange("p (h k v) -> p h k v", h=2, k=2)
                Sb4 = Sbf[:].rearrange("p (h k v) -> p h k v", h=2, k=2)
                if not cur['first']:
                    T.dma('sp', S4, spr[2 * j:2 * j + 2].rearrange("h (k p) v -> p h k v", p=128), sb='St', load=True, dram=('spr', j))
                else:
                    T.op('pool', lambda e: e.memset(St[:], 0.0), writes=['St'])
                T.op('pool', lambda e: e.tensor_copy(Sbf[:], St[:]), reads=['St'], writes=['Sbf'])
                Wg, wkg = w_next()
                Wm, wkm = w_next(hold=True)
                for c in range(8):
                    csl = slice(c * 128, (c + 1) * 128)
                    if full:
                        pg, pkg = tm_tile(Wg, wkg, c, 512)
                        T.op('act', lambda e: e.activation(slt[:], pg[:], AF.Sigmoid), reads=[pkg], writes=['sl'])
                        T.op('dve', lambda e: e.tensor_tensor(slt[:], pg[:], slt[:], ALU.mult), reads=[pkg, 'sl'], writes=['sl'])
                        ai = cnt['at'] % 2
                        cnt['at'] += 1
                        at = ATm[ai]
                        for hh in range(2):
                            pa, pka = nps()
                            T.group('pe', [(lambda e, i=i: e.matmul(pa[:, 0:128], lhsT=kT[:, hh * 2 + i, csl], rhs=qT[:, hh * 2 + i, csl], start=(i == 0), stop=(i == 1)))
                                           for i in range(2)], reads=[('kT', hh * 2, c), ('kT', hh * 2 + 1, c), ('qT', hh * 2, c), ('qT', hh * 2 + 1, c)], writes=[pka])
                            hd = heads[hh]
                            T.op('dve', lambda e, hh=hh, hd=hd, pa=pa: e.tensor_tensor(at[:, hh * 128:(hh + 1) * 128], pa[:, 0:128], C[:, C_DT + hd * 128:C_DT + (hd + 1) * 128], ALU.mult),
                                 reads=[pka, 'C'], writes=[('ATm', ai, hh)])
                    ki = cnt['kh'] % 2
                    cnt['kh'] += 1
                    kh = khat[ki]
                    for hh in range(2):
                        pt, pkt = nps()
                        ptb = pt.bitcast(BF16)
                        T.group('pe', [(lambda e, i=i: e.transpose(ptb[:, i * 128:(i + 1) * 128], kT[:, hh * 2 + i, csl], identB)) for i in range(2)],
                                reads=[('kT', hh * 2, c), ('kT', hh * 2 + 1, c), 'CB'], writes=[pkt])
                        hd = heads[hh]
                        T.op('act', lambda e, hh=hh, hd=hd, ptb=ptb: e.activation(kh[:, hh * 256:(hh + 1) * 256], ptb[:, 0:256], AF.Copy, scale=C[:, C_GKH + hd:C_GKH + hd + 1]),
                             reads=[pkt, 'C'], writes=[('khat', ki, hh)])
                    if full:
                        pmm, pkm = tm_tile(Wm, wkm, c, 512)
                        T.op('act', lambda e: e.activation(smt[:], pmm[:], AF.Sigmoid), reads=[pkm], writes=['sm'])
                        T.op('pool', lambda e: e.tensor_tensor(Gt[:], slt[:], smt[:], ALU.mult), reads=['sl', 'sm'], writes=['G'])
                        po, pko = nps()
                        fns = []
                        for hh in range(2):
                            osl = slice(hh * 256, (hh + 1) * 256)
                            fns.append(lambda e, hh=hh, osl=osl: e.matmul(po[:, osl], lhsT=at[:, hh * 128:(hh + 1) * 128], rhs=vt[:, c, osl], start=True, stop=False))
                            for i in range(2):
                                fns.append(lambda e, hh=hh, osl=osl, i=i: e.matmul(po[:, osl], lhsT=qT[:, hh * 2 + i, csl], rhs=Sb4[:, hh, i, :], start=False, stop=(i == 1)))
                        T.group('pe', fns, reads=[('ATm', ai, 0), ('ATm', ai, 1), ('v', c), 'Sbf'] + [('qT', t, c) for t in range(4)], writes=[pko])
                    for hh in range(2):
                        pu, pku = nps()
                        osl = slice(hh * 256, (hh + 1) * 256)
                        T.group('pe', [(lambda e, i=i: e.matmul(pu[:, i * 256:(i + 1) * 256], lhsT=kh[:, hh * 256 + i * 128:hh * 256 + (i + 1) * 128], rhs=vt[:, c, osl], start=True, stop=True))
                                       for i in range(2)], reads=[('khat', ki, hh), ('v', c)], writes=[pku])
                        hd = heads[hh]
                        T.op('dve', lambda e, hh=hh, hd=hd, pu=pu: e.scalar_tensor_tensor(St[:, hh * 512:(hh + 1) * 512], St[:, hh * 512:(hh + 1) * 512], C[:, C_G128 + hd:C_G128 + hd + 1], pu[:], ALU.mult, ALU.add),
                             reads=['St', pku, 'C'], writes=['St'])
                    if full:
                        T.op('pool', lambda e: e.tensor_copy(Sbf[:], St[:]), reads=['St'], writes=['Sbf'])
                        si = new_stat(4)
                        for hh in range(2):
                            osl = slice(hh * 256, (hh + 1) * 256)
                            T.op('act', lambda e, hh=hh, osl=osl: e.activation(junk5[:, osl], po[:, osl], AF.Square, scale=float(256 ** -0.5), accum_out=stat[:, si + hh:si + hh + 1]),
                                 reads=[pko], writes=['junk5', ('stat', si + hh)])
                        rstd_op(stat[:, si + 2:si + 4], stat[:, si:si + 2], [('stat', si), ('stat', si + 1)], [('stat', si + 2), ('stat', si + 3)])
                        mi = cnt['ms'] % 2
                        cnt['ms'] += 1
                        ms = mst[mi]
                        for hh in range(2):
                            osl = slice(hh * 256, (hh + 1) * 256)
                            T.op('dve', lambda e, hh=hh, osl=osl: e.scalar_tensor_tensor(ms[:, osl], po[:, osl], stat[:, si + 2 + hh:si + 3 + hh], Gt[:, osl], ALU.mult, ALU.mult),
                                 reads=[pko, ('stat', si + 2 + hh), 'G'], writes=[('mst', mi)])
                        T.dma('sp', mrgb_l[cur['pidx']][1][c * 128:(c + 1) * 128, j * 512:(j + 1) * 512], ms[:], sb=('mst', mi), load=False, dram=('mrgb', cur['pidx'], c, j))
                if cur['smp']:
                    pg, pkg = tm_tile_s(Wg, wkg, 512)
                    T.op('act', lambda e: e.activation(slt[0:NS, :], pg[0:NS, :], AF.Silu), reads=[pkg], writes=['sl'])
                    pmm, pkm = tm_tile_s(Wm, wkm, 512)
                    T.op('act', lambda e: e.activation(smt[0:NS, :], pmm[0:NS, :], AF.Sigmoid), reads=[pkm], writes=['sm'])
                    T.op('pool', lambda e: e.tensor_tensor(Gs_b[:, j * 512:(j + 1) * 512], slt[0:NS, :], smt[0:NS, :], ALU.mult), reads=['sl', 'sm'], writes=['Gs_b'])
                if not cur['first']:
                    store_state(nr[2 * j:2 * j + 2].rearrange("h (k p) v -> p h k v", p=128), 256, ('nr', j))
                else:
                    store_state(spr[2 * j:2 * j + 2].rearrange("h (k p) v -> p h k v", p=128), 256, ('spr', j))

            def coll_a(p_):
                mk = [('mrga', p_, c_, h_) for c_ in range(8) for h_ in range(2)]
                T.coll(mrga_l[p_][0].ap().opt(), gat_a_l[p_][0].ap().opt(), mk, [('gat_a', p_)], PAIRS)

            def coll_b(p_):
                mk = [('mrgb', p_, c_, h_) for c_ in range(8) for h_ in range(2)]
                T.coll(mrgb_l[p_][0].ap().opt(), gat_b_l[p_][0].ap().opt(), mk, [('gat_b', p_)], PAIRS)

            run_pass(0)
            coll_pending.extend([lambda: coll_a(0), lambda: coll_b(0)])
            run_pass(1)
            T.barrier()

        es2 = contextlib.ExitStack()
        with es2:
            def sb2(name, shape, dt=F32):
                return es2.enter_context(nc.sbuf_tensor("t_" + name, list(shape), dt))
            Sin = [sb2("Sin%d" % i, [128, 1024]) for i in range(3)]
            Sn = [sb2("Sn%d" % i, [128, 1024]) for i in range(3)]
            tmpv = [sb2("tmpv%d" % i, [128, 512]) for i in range(4)]
            QZ = [sb2("QZ%d" % i, [128, 2, NS], BF16) for i in range(4)]
            selb = [sb2("selb%d" % i, [NS, 128], BF16) for i in range(2)]
            for i in range(4):
                T.op('dve', lambda e, i=i: e.memset(QZ[i][:], 0.0), writes=[('QZ', i)])
            Sb_ = [sb2("Sb%d" % i, [128, 1024], BF16) for i in range(2)]
            stat2 = sb2("stat2", [NS, 64])
            junk2 = sb2("junk2", [NS, 512], BF16)
            mss = sb2("mss", [NS, 1024], BF16)
            it = [0]

            def sample_head(is_gla, h, hidx):
                dv = 512 if is_gla else 256
                sd, so = (sg, nsg) if is_gla else (sr, nsr)
                vs_ = vs_a if is_gla else vs_b
                vkey = 'vs_a' if is_gla else 'vs_b'
                qsrc = qTs_a if is_gla else qTs_b
                qkey = 'qTs_a' if is_gla else 'qTs_b'
                ksrc = kTs_a if is_gla else kTs_b
                kkey = 'kTs_a' if is_gla else 'kTs_b'
                po, pko = PS[7], ('ps', 7)
                reserved.add(7)

                if is_gla:
                    nsl, skey, nkey = 3, 'Sin', 'Sn'
                    SinV = [Sin[i][:, 0:1024] for i in range(3)]
                    SnV = [Sn[i][:, 0:1024] for i in range(3)]
                else:
                    nsl, skey, nkey = 6, 'SinR', 'SnR'
                    SinV = [Sin[i // 2][:, (i % 2) * 512:(i % 2 + 1) * 512] for i in range(6)]
                    SnV = [Sn[i // 2][:, (i % 2) * 512:(i % 2 + 1) * 512] for i in range(6)]

                def load(b, slot):
                    T.dma('sp', SinV[slot].rearrange("p (k v) -> p k v", k=2), (sd[b, h].rearrange("(p k) v -> p k v", k=2) if is_gla else sd[b, h].rearrange("(k p) v -> p k v", p=128)),
                          sb=(skey, slot), load=True)
                base = it[0]

                def stage_a(b):
                    pv, pkv = nps()
                    sbi = (base + b) % 2
                    sl_ = selb[sbi]
                    T.op('dve', lambda e: e.tensor_copy(sl_[:], identB[0:NS, b:b + 1].to_broadcast([NS, 128])), reads=['CB'], writes=[('selb', sbi)])
                    T.group('pe', [lambda e: e.matmul(pv[:, 0:dv], lhsT=sl_[:], rhs=vs_[:, h * dv:(h + 1) * dv], start=True, stop=True)],
                            reads=[('selb', sbi), vkey], writes=[pkv])
                    for kt in range(2):
                        ct = h * 2 + kt
                        ti = ((base + b) % 2) * 2 + kt
                        tv = tmpv[ti]
                        kcol = ksrc[:, ct, b:b + 1]
                        T.op('act', lambda e, tv=tv, kcol=kcol: e.activation(tv[:, 0:dv], pv[:, 0:dv], AF.Copy, scale=kcol),
                             reads=[pkv, kkey], writes=[('tmpv', ti)])

                def stage_b(b):
                    slot = (base + b) % nsl
                    sn_ = SnV[slot]
                    for kt in range(2):
                        ct = h * 2 + kt
                        ti = ((base + b) % 2) * 2 + kt
                        tv = tmpv[ti]
                        dec = aTs[:, ct, b:b + 1] if is_gla else C[:, C_G1 + h:C_G1 + h + 1]
                        T.op('dve', lambda e, kt=kt, tv=tv, dec=dec: e.scalar_tensor_tensor(sn_[:, kt * dv:(kt + 1) * dv], SinV[slot][:, kt * dv:(kt + 1) * dv], dec, tv[:, 0:dv], ALU.mult, ALU.add),
                             reads=[(skey, slot), ('tmpv', ti), 'aTs', 'C'], writes=[(nkey, slot)])
                    sbb = Sb_[(base + b) % 2]
                    T.op('dve', lambda e: e.tensor_copy(sbb[:, 0:2 * dv], sn_[:, 0:2 * dv]), reads=[(nkey, slot)], writes=[('Sb', (base + b) % 2)])
                    T.dma('sp', (so[b, h].rearrange("(p k) v -> p k v", k=2) if is_gla else so[b, h].rearrange("(k p) v -> p k v", p=128)), sn_[:, 0:2 * dv].rearrange("p (k v) -> p k v", k=2), sb=(nkey, slot), load=False)

                for b0 in range(nsl):
                    load(b0, (base + b0) % nsl)
                stage_a(0)
                stage_a(1)
                stage_b(0)
                for b in range(NS):
                    if b + 2 < NS:
                        stage_a(b + 2)
                    if b + 1 < NS:
                        stage_b(b + 1)
                    if b + nsl < NS:
                        load(b + nsl, (base + b + nsl) % nsl)
                    sbb = Sb_[(base + b) % 2]
                    qi = (base + b) % 4
                    qz = QZ[qi]
                    T.op('dve', lambda e: e.tensor_copy(qz[:, :, b:b + 1], qsrc[:, h * 2:h * 2 + 2, b:b + 1]), reads=[qkey], writes=[('QZ', qi)])
                    T.group('pe', [(lambda e, kt=kt: e.matmul(po[0:NS, 0:dv], lhsT=qz[:, kt, :], rhs=sbb[:, kt * dv:(kt + 1) * dv],
                                                             start=(b == 0 and kt == 0), stop=(b == NS - 1 and kt == 1))) for kt in range(2)],
                            reads=[('Sb', (base + b) % 2), ('QZ', qi)], writes=[pko])
                    if b >= 2:
                        qj = (base + b - 2) % 4
                        T.op('dve', lambda e: e.memset(QZ[qj][:, :, b - 2:b - 1], 0.0), writes=[('QZ', qj)])
                    yield
                for bb in (NS - 2, NS - 1):
                    qj = (base + bb) % 4
                    T.op('dve', lambda e: e.memset(QZ[qj][:, :, bb:bb + 1], 0.0), writes=[('QZ', qj)])
                it[0] = base + NS
                si = (hidx * 2) % 60
                ss = stat2[:, si:si + 1]
                rs = stat2[:, si + 1:si + 2]
                T.op('act', lambda e: e.activation(junk2[:, 0:dv], po[0:NS, 0:dv], AF.Square, scale=float(dv ** -0.5), accum_out=ss), reads=[pko], writes=['junk2', ('stat2', si)])
                rstd_op(rs, ss, [('stat2', si)], [('stat2', si + 1)])
                G = (Gs_a if is_gla else Gs_b)[:, h * dv:(h + 1) * dv]
                T.op('dve', lambda e: e.scalar_tensor_tensor(mss[:, h * dv:(h + 1) * dv], po[0:NS, 0:dv], rs, G, ALU.mult, ALU.mult),
                     reads=[pko, ('stat2', si + 1), 'Gs_a', 'Gs_b'], writes=['mss'])
                reserved.discard(7)

            def sample_gen():
                for h in range(2):
                    yield from sample_head(True, h, h)
                T.dma('sp', mrgsa[:, :], mss[:], sb='mss', load=False, dram='mrgsa')
                T.coll(mrgsa_f.ap().opt(), gat_sa_f.ap().opt(), ['mrgsa'], ['gat_sa'], PAIRS)
                T.barrier()
                it[0] = 0
                for h in range(4):
                    yield from sample_head(False, h, 2 + h)
                T.dma('sp', mrgsb[:, :], mss[:], sb='mss', load=False, dram='mrgsb')
                T.coll(mrgsb_f.ap().opt(), gat_sb_f.ap().opt(), ['mrgsb'], ['gat_sb'], PAIRS)

            NFB = sb2("nfb", [128, 2048])
            T.dma('sp', NFB[:], nfb[:, :], sb='NFB', load=True)
            WP = sb2("WP", [128, 2, 2048], BF16)
            T.dma('pool', WP[:].rearrange("p k c -> p (k c)"), wproj[:, :], sb='WP', load=True, max_dma_last_dim=8192)
            hbuf = sb2("h", [128, 2, 2048])
            mT = sb2("mT", [128, 16, 256], BF16)
            hnT = sb2("hnT", [128, 16, 256], BF16)
            ma_t = sb2("ma_t", [128, 2048], BF16); mb_t = sb2("mb_t", [128, 2048], BF16)
            ma2_t = sb2("ma2_t", [128, 2048], BF16); mb2_t = sb2("mb2_t", [128, 2048], BF16)
            pst = sb2("pst", [128, 256]); psc = sb2("psc", [128, 256], BF16); pTg = sb2("pTg", [128, 2, 256], BF16)
            hn = sb2("hn", [128, 2048], BF16)
            gsig = [sb2("gsig%d" % i, [128, 512]) for i in range(2)]
            stat3 = sb2("stat3", [128, 64])
            c3 = dict(st=0, g=0)

            def st3(n):
                i = c3['st']
                if i + n > 64:
                    i = 0
                c3['st'] = i + n
                return i

            def post_group(grp):
                for li, (tt, rows) in enumerate(grp):
                    for hf, (ta, tb) in enumerate(((ma_t, mb_t), (ma2_t, mb2_t))):
                        for rk in range(2):
                            if tt < 8:
                                ra = rk * NT + tt * 128
                                srca, srcb, dka, dkb = gat_a_l[hf][1][ra:ra + rows, :], gat_b_l[hf][1][ra:ra + rows, :], ('gat_a', hf), ('gat_b', hf)
                            else:
                                ra = rk * NS + hf * NS1
                                srca, srcb, dka, dkb = gat_sa[ra:ra + rows, :], gat_sb[ra:ra + rows, :], 'gat_sa', 'gat_sb'
                            T.dma('pool', ta[0:rows, rk * 1024:(rk + 1) * 1024], srca, sb=('mld', hf, 0, rk), load=True, dram=dka)
                            T.dma('pool', tb[0:rows, rk * 1024:(rk + 1) * 1024], srcb, sb=('mld', hf, 1, rk), load=True, dram=dkb)
                    mkeys = [('mld', hf_, ab_, rk_) for hf_ in range(2) for ab_ in range(2) for rk_ in range(2)]
                    T.op('dve', lambda e: e.tensor_tensor(ma_t[0:rows, :], ma_t[0:rows, :], mb_t[0:rows, :], ALU.add), reads=mkeys, writes=mkeys)
                    T.op('dve', lambda e: e.tensor_tensor(ma2_t[0:rows, :], ma2_t[0:rows, :], mb2_t[0:rows, :], ALU.add), reads=mkeys, writes=mkeys)
                    T.op('dve', lambda e: e.tensor_scalar(ma_t[0:rows, :], ma_t[0:rows, :], C[0:rows, C_MASK:C_MASK + 1], None, ALU.mult), reads=mkeys + ['C'], writes=mkeys)
                    T.op('dve', lambda e: e.scalar_tensor_tensor(ma_t[0:rows, :], ma2_t[0:rows, :], C[0:rows, C_MASK + 1:C_MASK + 2], ma_t[0:rows, :], ALU.mult, ALU.add),
                         reads=mkeys + ['C'], writes=mkeys + ['ma_t'])
                    for q in range(2):
                        pt, pkt = nps()
                        ptb = pt.bitcast(BF16)
                        T.group('pe', [(lambda e, j=j: e.transpose(ptb[:, j * 128:j * 128 + rows], ma_t[0:rows, (q * 8 + j) * 128:(q * 8 + j + 1) * 128], identB[0:rows, 0:rows]))
                                       for j in range(8)], reads=['ma_t', 'CB'] + mkeys, writes=[pkt])
                        src = ptb[:, 0:1024].rearrange("p (j t) -> p j t", j=8)[:, :, 0:rows]
                        T.op('act', lambda e, src=src, q=q: e.activation(mT[:, q * 8:(q + 1) * 8, li * 128:li * 128 + rows], src, AF.Copy), reads=[pkt], writes=[('mT', li)])
                    yield
                for cb in range(4):
                    Wv, wk = w_next()
                    for li, (tt, rows) in enumerate(grp):
                        ps, pk = nps()
                        for k0 in range(0, 16, 4):
                            T.group('pe', [(lambda e, kt=kt: e.matmul(ps[0:rows, :], lhsT=mT[:, kt, li * 128:li * 128 + rows], rhs=Wv[:, kt, :], start=(kt == 0), stop=(kt == 15)))
                                           for kt in range(k0, k0 + 4)], reads=[wk, ('mT', li)], writes=[pk])
                            if k0 < 12:
                                yield
                        xsrc = (xpost[tt * 128:tt * 128 + rows, cb * 512:(cb + 1) * 512] if tt < 8 else xs16[:, cb * 512:(cb + 1) * 512])
                        T.dma('pool', hbuf[0:rows, li, cb * 512:(cb + 1) * 512], xsrc, sb=('h', li, cb), load=True)
                        T.op('dve', lambda e, li=li, rows=rows, ps=ps: e.tensor_tensor(hbuf[0:rows, li, cb * 512:(cb + 1) * 512], ps[0:rows, :], hbuf[0:rows, li, cb * 512:(cb + 1) * 512], ALU.add),
                             reads=[pk, ('h', li, cb)], writes=[('h', li, cb)])
                        yield
                for li, (tt, rows) in enumerate(grp):
                    si = st3(2)
                    ss = stat3[0:rows, si:si + 1]
                    rs = stat3[0:rows, si + 1:si + 2]
                    T.op('act', lambda e: e.activation(hn[0:rows, :], hbuf[0:rows, li, :], AF.Square, scale=float(2048 ** -0.5), accum_out=ss),
                         reads=[('h', li, cb) for cb in range(4)], writes=['hn', ('stat3', si)])
                    rstd_op(rs, ss, [('stat3', si)], [('stat3', si + 1)])
                    T.op('dve', lambda e: e.tensor_scalar(hn[0:rows, :], hbuf[0:rows, li, :], rs, None, ALU.mult),
                         reads=[('h', li, cb) for cb in range(4)] + [('stat3', si + 1)], writes=['hn'])
                    for q in range(2):
                        pt, pkt = nps()
                        ptb = pt.bitcast(BF16)
                        T.group('pe', [(lambda e, j=j: e.transpose(ptb[:, j * 128:j * 128 + rows], hn[0:rows, (q * 8 + j) * 128:(q * 8 + j + 1) * 128], identB[0:rows, 0:rows]))
                                       for j in range(8)], reads=['hn', 'CB'], writes=[pkt])
                        src = ptb[:, 0:1024].rearrange("p (j t) -> p j t", j=8)[:, :, 0:rows]
                        gn_ = C[:, C_NP + q * 8:C_NP + (q + 1) * 8].unsqueeze(2).to_broadcast([128, 8, rows])
                        T.op('dve', lambda e, src=src, q=q, gn_=gn_: e.tensor_tensor(hnT[:, q * 8:(q + 1) * 8, li * 128:li * 128 + rows], src, gn_, ALU.mult),
                             reads=[pkt, 'C'], writes=[('hnT', li)])
                    psrc = pm[tt * 128:tt * 128 + rows, :] if tt < 8 else psd[:, :]
                    T.dma('pool', pst[0:rows, :], psrc, sb='pst', load=True)
                    T.op('pool', lambda e, rows=rows: e.tensor_copy(psc[0:rows, :], pst[0:rows, :]), reads=['pst'], writes=['psc'])
                    pt, pkt = nps()
                    ptb = pt.bitcast(BF16)
                    T.group('pe', [(lambda e, j=j, rows=rows: e.transpose(ptb[:, j * 128:j * 128 + rows], psc[0:rows, j * 128:(j + 1) * 128], identB[0:rows, 0:rows]))
                                   for j in range(2)], reads=['psc', 'CB'], writes=[pkt])
                    T.op('act', lambda e, rows=rows, ptb=ptb, li=li: e.activation(pTg[:, :, li * 128:li * 128 + rows],
                                                                               ptb[:, 0:256].rearrange("p (j t) -> p j t", j=2)[:, :, 0:rows], AF.Copy),
                         reads=[pkt], writes=[('pTg', li)])
                    yield
                for cb in range(4):
                    Wv, wk = w_next()
                    for li, (tt, rows) in enumerate(grp):
                        ps, pk = nps()
                        for k0 in range(0, 16, 4):
                            T.group('pe', [(lambda e, kt=kt: e.matmul(ps[0:rows, :], lhsT=hnT[:, kt, li * 128:li * 128 + rows], rhs=Wv[:, kt, :], start=(kt == 0), stop=(kt == 15)))
                                           for kt in range(k0, k0 + 4)], reads=[wk, ('hnT', li)], writes=[pk])
                            if k0 < 12:
                                yield
                        gi = c3['g'] % 2
                        c3['g'] += 1
                        gs = gsig[gi]
                        T.op('act', lambda e, rows=rows, ps=ps: e.activation(gs[0:rows, :], ps[0:rows, :], AF.Sigmoid), reads=[pk], writes=[('gsig', gi)])
                        pp, pkp = nps()
                        T.group('pe', [(lambda e, k2=k2: e.matmul(pp[0:rows, :], lhsT=pTg[:, k2, li * 128:li * 128 + rows], rhs=WP[:, k2, cb * 512:(cb + 1) * 512], start=(k2 == 0), stop=(k2 == 1)))
                                       for k2 in range(2)], reads=['WP', ('pTg', li)], writes=[pkp])
                        T.op('dve', lambda e, rows=rows, pp=pp: e.tensor_tensor(gs[0:rows, :], gs[0:rows, :], pp[0:rows, :], ALU.mult), reads=[('gsig', gi), pkp], writes=[('gsig', gi)])
                        T.op('pool', lambda e, rows=rows, li=li: e.tensor_tensor(hbuf[0:rows, li, cb * 512:(cb + 1) * 512], hbuf[0:rows, li, cb * 512:(cb + 1) * 512], gs[0:rows, :], ALU.add),
                             reads=[('gsig', gi), ('h', li, cb)], writes=[('h', li, cb)])
                        yield
                for li, (tt, rows) in enumerate(grp):
                    si = st3(2)
                    ss = stat3[0:rows, si:si + 1]
                    rs = stat3[0:rows, si + 1:si + 2]
                    T.op('act', lambda e: e.activation(hn[0:rows, :], hbuf[0:rows, li, :], AF.Square, scale=float(2048 ** -0.5), accum_out=ss),
                         reads=[('h', li, cb) for cb in range(4)], writes=['hn', ('stat3', si)])
                    rstd_op(rs, ss, [('stat3', si)], [('stat3', si + 1)])
                    dst = y_m[tt * 128:tt * 128 + rows, :] if tt < 8 else y_s[:, :]
                    for cb in range(4):
                        T.op('dve', lambda e, cb=cb: e.scalar_tensor_tensor(hbuf[0:rows, li, cb * 512:(cb + 1) * 512], hbuf[0:rows, li, cb * 512:(cb + 1) * 512], rs,
                                                                            NFB[0:rows, cb * 512:(cb + 1) * 512], ALU.mult, ALU.mult),
                             reads=[('h', li, cb), ('stat3', si + 1), 'NFB'], writes=[('h', li, cb)])
                        T.dma('pool', dst[:, cb * 512:(cb + 1) * 512], hbuf[0:rows, li, cb * 512:(cb + 1) * 512], sb=('h', li, cb), load=False)
                    yield

            def post_gen():
                for grp in ([(0, 128), (1, 128)], [(2, 128), (3, 128)], [(4, 128), (5, 128)], [(6, 128), (7, 128)]):
                    yield from post_group(grp)

            coll_a(1)
            coll_b(1)
            sg_ = sample_gen()
            pg_ = post_gen()
            alive_s, alive_p = True, True
            acc = 0.0
            while alive_s or alive_p:
                if alive_s:
                    acc += SAMPLE_PER_POST if alive_p else 1.0
                    while acc >= 1.0 and alive_s:
                        acc -= 1.0
                        try:
                            next(sg_)
                        except StopIteration:
                            alive_s = False
                if alive_p:
                    try:
                        next(pg_)
                    except StopIteration:
                        alive_p = False
            for _ in post_group([(8, NS1)]):
                pass
            T.barrier()
        esB.__exit__(None, None, None)
    return nc


_NC_CACHE = {}


def _host_consts(r, norm_mix, norm_ple, gla_norm, w_gla_up, b_gla):
    lg = _gammas()
    cst = np.zeros((128, NCST), np.float32)
    cst[:, C_ID:C_ID + 128] = np.eye(128, dtype=np.float32)
    i = np.arange(128)
    U = (i[:, None] <= i[None, :]).astype(np.float32)
    cst[:, C_TRI:C_TRI + 128] = U
    for hl in range(4):
        h = 4 * r + hl
        d = np.exp(-(i[:, None] + 1.0) * lg[h]) / 16.0 * U
        cst[:, C_DT + hl * 128:C_DT + (hl + 1) * 128] = d.astype(np.float32)
        cst[:, C_GAM + hl * 128:C_GAM + (hl + 1) * 128] = np.exp((i[None, :] + 1.0) * lg[h]).astype(np.float32)
        cst[:, C_GKH + hl] = (np.exp((127.0 - i) * lg[h]) / 16.0).astype(np.float32)
        cst[:, C_G128 + hl] = np.float32(np.exp(128.0 * lg[h]))
        cst[:, C_G1 + hl] = np.float32(np.exp(lg[h]))
    cst[:, C_MASK] = 1.0 if r == 0 else 0.0
    cst[:, C_MASK + 1] = 1.0 if r == 1 else 0.0
    cst[:, C_NM:C_NM + 16] = norm_mix.reshape(16, 128).T
    cst[:, C_NP:C_NP + 16] = norm_ple.reshape(16, 128).T
    cst[:, C_GN:C_GN + 512] = gla_norm.reshape(1, 512)
    inv = (1.0 / (np.float32(10000.0) ** np.linspace(0.0, 1.0, 128, dtype=np.float32))).astype(np.float32)
    ang = (np.float32(16384.0) * inv).astype(np.float32)
    cst[:, C_CS] = np.cos(ang); cst[:, C_CS + 1] = np.sin(ang)
    cst[:, C_CS + 2] = np.cos(ang) / 16.0; cst[:, C_CS + 3] = np.sin(ang) / 16.0
    for hl in range(2):
        cols = (2 * r + hl) * 256 + GLA_PERM
        cst[0:16, C_WUP + hl * 256:C_WUP + (hl + 1) * 256] = w_gla_up[:, cols]
        cst[16, C_WUP + hl * 256:C_WUP + (hl + 1) * 256] = b_gla[cols]
    cbf = np.zeros((128, NCBF), np.float32)
    cbf[:, B_ID:B_ID + 128] = np.eye(128, dtype=np.float32)
    return cst, cbf.astype(ml_dtypes.bfloat16), inv


def _rot_tables(inv, pos0):
    pos = (pos0 + np.arange(NT)).astype(np.float32)
    ang = pos[:, None] * inv[None, :]
    return np.ascontiguousarray(np.cos(ang).T.astype(np.float32)), np.ascontiguousarray(np.sin(ang).T.astype(np.float32))


def _pack_w(w, ncb):
    K, N = w.shape
    nk = K // 128
    a = w.reshape(nk, 128, N // ncb, ncb).transpose(1, 2, 0, 3)
    return np.ascontiguousarray(a.reshape(128, -1))


def kernel(x_prompt, x_sample, state_gla, state_ret, p_prompt, p_sample, norm_mix, w_in, w_gla_up, b_gla, gla_norm,
           w_out, norm_ple, w_ple_gate, w_ple_proj, norm_final):
    f = lambda a: np.asarray(a, dtype=np.float32)
    x_prompt, x_sample, state_gla, state_ret, p_prompt, p_sample = map(f, (x_prompt, x_sample, state_gla, state_ret, p_prompt, p_sample))
    w_in = f(w_in)[0]
    w3 = w_in.reshape(16, 128, NIN)
    per_rank = []
    for r in range(2):
        cst, cbf, inv = _host_consts(r, f(norm_mix)[0], f(norm_ple)[0], f(gla_norm)[0], f(w_gla_up)[0], f(b_gla)[0])
        wpk = np.empty((128, WTOT), np.float32)
        for lk in LKEYS:
            gk = lk if lk == 'r' else '%s%d' % (lk[:2], 2 * r + int(lk[2:]))
            c0, n = WBL[gk]
            cols = (c0 + GLA_PERM) if lk[:2] in ('qa', 'ka') else np.arange(c0, c0 + n)
            wpk[:, WOFF[lk]:WOFF[lk] + 16 * n] = w3[:, :, cols].transpose(1, 0, 2).reshape(128, 16 * n)
        per_rank.append((cst, cbf, wpk))
    woutp = _pack_w(f(w_out)[0], 512)
    wgatep = _pack_w(f(w_ple_gate)[0], 512)
    wprojp = np.ascontiguousarray(f(w_ple_proj)[0].reshape(2, 128, 2048).transpose(1, 0, 2).reshape(128, 4096))
    nfb = np.ascontiguousarray(np.broadcast_to(f(norm_final).reshape(1, 2048), (128, 2048)))
    cos0, sin0 = _rot_tables(inv, 0)
    cos1, sin1 = _rot_tables(inv, 1024)
    in_maps = []
    for c in range(8):
        b, r = c // 2, c % 2
        cst, cbf, wpk = per_rank[r]
        sl = slice(r * 1024, (r + 1) * 1024)
        s0 = 32 * b
        in_maps.append(dict(
            xp=np.ascontiguousarray(x_prompt[b, 0:1024]), xm=np.ascontiguousarray(x_prompt[b, 1024:2048]),
            xpost=np.ascontiguousarray(x_prompt[b, sl]),
            xs=np.ascontiguousarray(x_sample[s0:s0 + 32, 0]), xs16=np.ascontiguousarray(x_sample[s0 + 16 * r:s0 + 16 * r + 16, 0]),
            pm=np.ascontiguousarray(p_prompt[0, b, sl]), psd=np.ascontiguousarray(p_sample[0, s0 + 16 * r:s0 + 16 * r + 16, 0]),
            sg=np.ascontiguousarray(state_gla[0, s0:s0 + 32, 2 * r:2 * r + 2]), sr=np.ascontiguousarray(state_ret[0, s0:s0 + 32, 4 * r:4 * r + 4]),
            wpk=wpk, wout=woutp, wgate=wgatep, wproj=wprojp, cst=cst, cbf=cbf,
            cosm=cos1, sinm=sin1, cosp=cos0, sinp=sin0, nfb=nfb))
    if 'nc' not in _NC_CACHE:
        _NC_CACHE['nc'] = build_nc()
    res = run_bass_kernel_spmd(_NC_CACHE['nc'], in_maps, core_ids=list(range(8)))
    R = res.results
    y_prompt = np.empty((4, 2048, 2048), np.float32)
    y_sample = np.empty((128, 1, 2048), np.float32)
    ngp = np.empty((1, 4, 4, 256, 512), np.float32)
    nrp = np.empty((1, 4, 8, 256, 256), np.float32)
    ngs = np.empty((1, 128, 4, 256, 512), np.float32)
    nrs = np.empty((1, 128, 8, 256, 256), np.float32)
    for c in range(8):
        b, r = c // 2, c % 2
        s0 = 32 * b
        y_prompt[b, r * 1024:(r + 1) * 1024] = R[c]["y_m"]
        y_sample[s0 + 16 * r:s0 + 16 * r + 16, 0] = R[c]["y_s"]
        ngp[0, b, 2 * r:2 * r + 2] = R[c]["ng"]
        nrp[0, b, 4 * r:4 * r + 4] = R[c]["nr"]
        ngs[0, s0:s0 + 32, 2 * r:2 * r + 2] = R[c]["nsg"]
        nrs[0, s0:s0 + 32, 4 * r:4 * r + 4] = R[c]["nsr"]
    return (y_prompt, y_sample, ngp, nrp, ngs, nrs)
```

```python
import contextlib
import numpy as np
import ml_dtypes
import concourse.bass as bass
import concourse.mybir as mybir
from concourse.bass_utils import run_bass_kernel_spmd

F32 = mybir.dt.float32
BF16 = mybir.dt.bfloat16
AF = mybir.ActivationFunctionType
ALU = mybir.AluOpType

NIN = 18448
OFF = dict(qa=0, ka=1024, va=2048, ga=4096, ra=6144, qb=6160, kb=8208, vb=10256, gb=12304, ma=14352, mb=16400)
EPS = 1e-6
NT = 1024
NS = 32
NS1 = 16
SAMPLE_PER_POST = 0.7
STRICT_SAME_ENGINE = False

C_ID = 0
C_TRI = 128
C_DT = 256
C_GAM = C_DT + 512
C_GKH = C_GAM + 512
C_G128 = C_GKH + 4
C_G1 = C_G128 + 4
C_MASK = C_G1 + 4
C_NM = C_MASK + 2
C_NP = C_NM + 16
C_GN = C_NP + 16
C_CS = C_GN + 512
C_WUP = C_CS + 4
NCST = C_WUP + 512
B_ID = 0
NCBF = 128


GLA_PERM = (2 * np.arange(128)[None, :] + np.arange(2)[:, None]).reshape(256)


def _gammas():
    h = np.arange(8, dtype=np.float64)
    return np.log(1.0 - np.exp2(-5.0 - h))


def wblocks():
    bl = {}
    bl['r'] = (OFF['ra'], 16)
    for h in range(4):
        bl['qa%d' % h] = (OFF['qa'] + h * 256, 256)
        bl['ka%d' % h] = (OFF['ka'] + h * 256, 256)
        bl['va%d' % h] = (OFF['va'] + h * 512, 512)
        bl['ga%d' % h] = (OFF['ga'] + h * 512, 512)
        bl['ma%d' % h] = (OFF['ma'] + h * 512, 512)
        bl['qb%d' % h] = (OFF['qb'] + h * 512, 512)
        bl['kb%d' % h] = (OFF['kb'] + h * 512, 512)
        bl['vb%d' % h] = (OFF['vb'] + h * 512, 512)
        bl['gb%d' % h] = (OFF['gb'] + h * 512, 512)
        bl['mb%d' % h] = (OFF['mb'] + h * 512, 512)
    offs = {}
    o = 0
    for k, (c0, n) in bl.items():
        offs[k] = o
        o += 16 * n
    return bl, offs, o


WBL, _WOFF_G, _WTOT_G = wblocks()
LKEYS = ['r'] + ['%s%d' % (k, g) for g in range(2) for k in ('qa', 'ka', 'va', 'ga', 'ma', 'qb', 'kb', 'vb', 'gb', 'mb')]
WOFF = {}
_o = 0
for _k in LKEYS:
    WOFF[_k] = _o
    _o += 16 * WBL[_k][1]
WTOT = _o


class Res:
    __slots__ = ("lw", "rd", "dsem", "dcnt")

    def __init__(self):
        self.lw = None
        self.rd = {}
        self.dsem = None
        self.dcnt = 0


class Trk:
    def __init__(self, nc, es):
        self.nc = nc
        self.es = es
        self.E = {}
        for n, e in (("pe", nc.tensor), ("act", nc.scalar), ("dve", nc.vector), ("pool", nc.gpsimd), ("sp", nc.sync)):
            sem = es.enter_context(nc.semaphore("s_" + n))
            self.E[n] = dict(e=e, sem=sem, cnt=0, seen={})
        self.R = {}
        self.nd = 0

    def res(self, key):
        r = self.R.get(key)
        if r is None:
            r = self.R[key] = Res()
        return r

    def _deps(self, reads, writes):
        d = []
        for k in reads:
            r = self.res(k)
            if r.lw is not None:
                d.append((r.lw, 'raw'))
        for k in writes:
            r = self.res(k)
            if r.lw is not None:
                d.append((r.lw, 'waw'))
            for x in r.rd.values():
                d.append((x, 'war'))
        return d

    def _wait(self, en, deps):
        E = self.E[en]
        for (src, kind) in deps:
            tag, ref, val = src
            if tag == 'E':
                if ref == en and (en == 'pe' or (kind != 'raw' and not STRICT_SAME_ENGINE)):
                    continue
                key = ('E', ref)
                sem = self.E[ref]['sem']
            else:
                key = ('D', id(ref))
                sem = ref.dsem
            if E['seen'].get(key, 0) >= val:
                continue
            E['e'].wait_ge(sem, val)
            E['seen'][key] = val

    def _commit(self, me, mkey, reads, writes):
        for k in reads:
            self.res(k).rd[mkey] = me
        for k in writes:
            r = self.res(k)
            r.lw = me
            r.rd = {}

    def op(self, en, fn, reads=(), writes=()):
        self._wait(en, self._deps(reads, writes))
        E = self.E[en]
        ins = fn(E['e'])
        E['cnt'] += 1
        ins.then_inc(E['sem'], 1)
        self._commit(('E', en, E['cnt']), ('E', en), reads, writes)

    def group(self, en, fns, reads=(), writes=()):
        self._wait(en, self._deps(reads, writes))
        E = self.E[en]
        ins = None
        for f in fns:
            ins = f(E['e'])
        E['cnt'] += 1
        ins.then_inc(E['sem'], 1)
        self._commit(('E', en, E['cnt']), ('E', en), reads, writes)

    def dma(self, en, out, in_, sb, load, dram=None, extra=(), **kw):
        r = self.res(sb)
        if r.dsem is None:
            r.dsem = self.es.enter_context(self.nc.semaphore("d%d" % self.nd))
            self.nd += 1
        if load:
            reads = [dram] if dram is not None else []
            writes = [sb] + list(extra)
        else:
            reads = [sb]
            writes = [dram] if dram is not None else []
        self._wait(en, self._deps(reads, writes))
        ins = self.E[en]['e'].dma_start(out=out, in_=in_, **kw)
        r.dcnt += 16
        ins.then_inc(r.dsem, 16)
        self._commit(('D', r, r.dcnt), ('D', id(r)), reads, writes)

    def coll(self, ins_ap, outs_ap, reads, writes, groups):
        r = self.res(('coll', self.nd))
        r.dsem = self.es.enter_context(self.nc.semaphore("c%d" % self.nd))
        self.nd += 1
        self._wait('pool', self._deps(reads, writes))
        ins = self.nc.gpsimd.collective_compute("AllGather", ALU.bypass, replica_groups=groups, ins=[ins_ap], outs=[outs_ap])
        r.dcnt += 1
        ins.then_inc(r.dsem, 1)
        self._commit(('D', r, r.dcnt), ('D', id(r)), reads, writes)

    def barrier(self):
        for en, E in self.E.items():
            for e2, E2 in self.E.items():
                if e2 != en and E2['cnt'] > 0 and E['seen'].get(('E', e2), 0) < E2['cnt']:
                    E['e'].wait_ge(E2['sem'], E2['cnt'])
                    E['seen'][('E', e2)] = E2['cnt']
            for r in self.R.values():
                if r.dsem is not None and r.dcnt > 0 and E['seen'].get(('D', id(r)), 0) < r.dcnt:
                    E['e'].wait_ge(r.dsem, r.dcnt)
                    E['seen'][('D', id(r))] = r.dcnt


def build_nc(stop_after=None):
    nc = bass.Bass("TRN2", target_bir_lowering=False)

    def din(name, shape, dt=F32):
        return nc.dram_tensor(name, list(shape), dt, kind="ExternalInput").ap()

    def dout(name, shape, dt=F32):
        return nc.dram_tensor(name, list(shape), dt, kind="ExternalOutput").ap()

    def dint(name, shape, dt=F32):
        return nc.dram_tensor(name, list(shape), dt, kind="Internal").ap()

    xm = din("xm", [NT, 2048]); xp = din("xp", [NT, 2048]); xs = din("xs", [NS, 2048])
    xpost = din("xpost", [NT, 2048]); xs16 = din("xs16", [NS1, 2048])
    pm = din("pm", [NT, 256]); psd = din("psd", [NS1, 256])
    sg = din("sg", [NS, 2, 256, 512]); sr = din("sr", [NS, 4, 256, 256])
    wpk = din("wpk", [128, WTOT]); wout = din("wout", [128, 16 * 2048]); wgate = din("wgate", [128, 16 * 2048])
    wproj = din("wproj", [128, 2 * 2048])
    cst = din("cst", [128, NCST]); cbf = din("cbf", [128, NCBF], BF16)
    cosm = din("cosm", [128, NT]); sinm = din("sinm", [128, NT]); cosp = din("cosp", [128, NT]); sinp = din("sinp", [128, NT])
    nfb = din("nfb", [128, 2048])
    y_m = dout("y_m", [NT, 2048]); y_s = dout("y_s", [NS1, 2048])
    ng = dout("ng", [2, 256, 512]); nr = dout("nr", [4, 256, 256])
    nsg = dout("nsg", [NS, 2, 256, 512]); nsr = dout("nsr", [NS, 4, 256, 256])
    def cbuf(name, p, n):
        t = nc.dram_tensor(name, [p, n], F32, kind="Internal")
        return t, t.bitcast(BF16).ap().rearrange("p (a c) -> (p a) c", c=1024)
    mrga_l = [cbuf("mrga%d" % i, 128, 4096) for i in range(2)]; mrgb_l = [cbuf("mrgb%d" % i, 128, 4096) for i in range(2)]
    gat_a_l = [cbuf("gat_a%d" % i, 256, 4096) for i in range(2)]; gat_b_l = [cbuf("gat_b%d" % i, 256, 4096) for i in range(2)]
    mrgsa_f, mrgsa = cbuf("mrgsa", NS, 512); mrgsb_f, mrgsb = cbuf("mrgsb", NS, 512)
    gat_sa_f, gat_sa = cbuf("gat_sa", 2 * NS, 512); gat_sb_f, gat_sb = cbuf("gat_sb", 2 * NS, 512)
    spg = dint("spg", [2, 256, 512]); spr = dint("spr", [4, 256, 256])
    PAIRS = [[0, 1], [2, 3], [4, 5], [6, 7]]
    cur = dict(first=True, smp=False, pidx=0)

    es = contextlib.ExitStack()
    with es:
        T = Trk(nc, es)

        def sb(name, shape, dt=F32):
            return es.enter_context(nc.sbuf_tensor("t_" + name, list(shape), dt))

        PS = [es.enter_context(nc.psum_tensor("ps%d" % i, [128, 512], F32)) for i in range(8)]
        psi = [0]

        reserved = set()

        def rstd_op(o, i, reads, writes):
            T.op('act', lambda e: e.activation(o, i, AF.Ln, bias=EPS), reads=reads, writes=writes)
            T.op('act', lambda e: e.activation(o, o, AF.Exp, scale=-0.5), reads=writes, writes=writes)

        def nps():
            i = psi[0]
            while i in reserved:
                i = (i + 1) % 8
            psi[0] = (i + 1) % 8
            return PS[i], ('ps', i)

        W = [sb("W%d" % i, [128, 8192], BF16) for i in range(3)]
        C = sb("cst", [128, NCST])
        CB = sb("cbf", [128, NCBF], BF16)
        esB = contextlib.ExitStack()
        esB.__enter__()

        def sbB(name, shape, dt=F32):
            return esB.enter_context(nc.sbuf_tensor("t_" + name, list(shape), dt))
        qTs_a = sbB("qTs_a", [128, 4, NS]); kTs_a = sbB("kTs_a", [128, 4, NS]); aTs = sbB("aTs", [128, 4, NS])
        qTs_b = sbB("qTs_b", [128, 8, NS]); kTs_b = sbB("kTs_b", [128, 8, NS])
        vs_a = sbB("vs_a", [NS, 1024], BF16); vs_b = sbB("vs_b", [NS, 1024], BF16)
        Gs_a = sbB("Gs_a", [NS, 1024], BF16); Gs_b = sbB("Gs_b", [NS, 1024], BF16)

        T.dma('sp', C[:], cst[:, :], sb='C', load=True)
        T.dma('sp', CB[:], cbf[:, :], sb='CB', load=True)
        ident = C[:, C_ID:C_ID + 128]
        tri = C[:, C_TRI:C_TRI + 128]
        identB = CB[:, B_ID:B_ID + 128]

        wsched = []
        wstate = dict(issued=0, cur=-1)

        def w_issue_upto(n):
            while wstate['issued'] < min(n, len(wsched)):
                j = wstate['issued']
                src, ncols, nk = wsched[j][0:3]
                dk = wsched[j][3] if len(wsched[j]) > 3 else None
                T.dma('pool', W[j % 3][:, 0:nk * ncols], src, sb=('W', j % 3), load=True, dram=dk, max_dma_last_dim=8192)
                wstate['issued'] += 1

        def w_next(hold=False):
            wstate['cur'] += 1
            j = wstate['cur']
            w_issue_upto(j + 2 if hold else j + 3)
            src, ncols, nk = wsched[j][0:3]
            return W[j % 3][:, 0:nk * ncols].rearrange("p (k c) -> p k c", k=nk), ('W', j % 3)

        def sched_in(key):
            c0, n = WBL[key]
            wsched.append((wpk[:, WOFF[key]:WOFF[key] + 16 * n], n, 16))

        for pidx_ in range(2):
            sched_in('r')
            for g in range(2):
                for k in ('qa', 'ka', 'va', 'ga', 'ma'):
                    sched_in('%s%d' % (k, g))
                for k in ('qb', 'kb', 'vb', 'gb', 'mb'):
                    sched_in('%s%d' % (k, g))
        n_in_blocks = len(wsched)
        wcv = [dint("wcv%d" % i, [128, 8192], BF16) for i in range(8)]
        for grp in range(5):
            for cb in range(4):
                wsched.append((wcv[cb][:, :], 512, 16, ('wcv', cb)))
            for cb in range(4):
                wsched.append((wcv[4 + cb][:, :], 512, 16, ('wcv', 4 + cb)))

        es1 = contextlib.ExitStack()
        with es1:
            def sb1(name, shape, dt=F32):
                return es1.enter_context(nc.sbuf_tensor("t_" + name, list(shape), dt))

            uT = sb1("uT", [128, 16, NT], BF16)
            uTs = sb1("uTs", [128, 16, NS], BF16)
            Et = sb1("E", [128, 2, NT])
            qT = sb1("qT", [128, 4, NT], BF16); kT = sb1("kT", [128, 4, NT], BF16)
            vt = sb1("v", [128, 8, 512], BF16)
            cosT = sb1("cos", [128, NT]); sinT = sb1("sin", [128, NT])
            raT = sb1("raT", [17, NT + NS])
            xst = sb1("xst", [128, 2048]); junk5 = sb1("junk5", [128, 512], BF16)
            xsc = [sb1("xsc%d" % i, [128, 512]) for i in range(2)]
            stat = sb1("stat", [128, 64])
            rtA = [sb1("rtA%d" % i, [128, 512]) for i in range(1)]
            rtB = [sb1("rtB%d" % i, [128, 512]) for i in range(1)]
            rtR = [sb1("rtR%d" % i, [128, 512]) for i in range(1)]
            Lx = sb1("Lx", [128, 256]); Lt = sb1("Lt", [128, 256])
            ATm = [sb1("ATm%d" % i, [128, 256], BF16) for i in range(2)]
            khT = sb1("khT", [128, 2, 128], BF16)
            khat = [sb1("khat%d" % i, [128, 512], BF16) for i in range(2)]
            slt = sb1("sl", [128, 512]); smt = sb1("sm", [128, 512]); Gt = sb1("G", [128, 512])
            St = sb1("S", [128, 1024]); Sbf = sb1("Sbf", [128, 1024], BF16)
            mst = [sb1("mst%d" % i, [128, 512], BF16) for i in range(2)]
            cnt = dict(st=0, rot=0, at=0, kh=0, ms=0)

            def new_stat(n=1):
                i = cnt['st']
                if i + n > 64:
                    i = 0
                cnt['st'] = i + n
                return i

            T.op('pool', lambda e: e.memset(raT[:], 1.0), writes=[('raT', 0), ('raT', 1), ('raT', 's')])

            qTf = qT.bitcast(F32)[:].rearrange("p a b -> p (a b)")
            qT_keys = [('qT', ct_, c_) for ct_ in range(4) for c_ in range(8)]
            kTf = kT.bitcast(F32)[:].rearrange("p a b -> p (a b)")
            kT_keys = [('kT', ct_, c_) for ct_ in range(4) for c_ in range(8)]
            Ef = Et[:].rearrange("p a b -> p (a b)")
            E_keys = [('E', c_) for c_ in range(8)]
            nt_cnt = [0]

            def norm_transpose(x_ap, rows, dst, dst_key, col0, gain_off):
                alt = nt_cnt[0] % 4
                nt_cnt[0] += 1
                if alt == 0:
                    xst_, xk = xst, ['xst']
                    T.dma('sp', xst_[0:rows, :], x_ap, sb='xst', load=True)
                elif alt == 1:
                    xst_, xk = qTf, ['qTf'] + qT_keys
                    T.dma('sp', xst_[0:rows, :], x_ap, sb='qTf', load=True, extra=qT_keys)
                elif alt == 2:
                    xst_, xk = kTf, ['kTf'] + kT_keys
                    T.dma('sp', xst_[0:rows, :], x_ap, sb='kTf', load=True, extra=kT_keys)
                else:
                    xst_, xk = Ef, ['Ef'] + E_keys
                    T.dma('sp', xst_[0:rows, :], x_ap, sb='Ef', load=True, extra=E_keys)
                si = new_stat(2)
                ss = stat[0:rows, si:si + 1]
                rs = stat[0:rows, si + 1:si + 2]
                vj = vt[:, 0:4, :].rearrange("p a b -> p (a b)")
                T.op('act', lambda e: e.activation(vj[0:rows, :], xst_[0:rows, :], AF.Square, scale=float(2048 ** -0.5), accum_out=ss),
                     reads=xk, writes=[('v', 0), ('v', 1), ('v', 2), ('v', 3), ('stat', si)])
                rstd_op(rs, ss, [('stat', si)], [('stat', si + 1)])
                for cq in range(4):
                    xc = xsc[cq % 2]
                    T.op('dve', lambda e: e.tensor_scalar(xc[0:rows, :], xst_[0:rows, cq * 512:(cq + 1) * 512], rs, None, ALU.mult),
                         reads=xk + [('stat', si + 1)], writes=[('xsc', cq % 2)])
                    ps, pk = nps()
                    T.group('pe', [(lambda e, j=j: e.transpose(ps[:, j * rows:(j + 1) * rows], xc[0:rows, j * 128:(j + 1) * 128], ident[0:rows, 0:rows]))
                                   for j in range(4)], reads=[('xsc', cq % 2), 'C'], writes=[pk])
                    gsl = C[:, gain_off + cq * 4:gain_off + cq * 4 + 4].unsqueeze(2).to_broadcast([128, 4, rows])
                    T.op('dve', lambda e: e.tensor_tensor(dst[:, cq * 4:cq * 4 + 4, col0:col0 + rows], ps[:, 0:4 * rows].rearrange("p (j t) -> p j t", j=4), gsl, ALU.mult),
                         reads=[pk, 'C'], writes=[dst_key])

            def fm_block(Wv, wk, c0, ncol_tile, evac, evac_s):
                for half in range(2):
                    ps, pk = nps()
                    T.group('pe', [(lambda e, kt=kt: e.matmul(ps[0:ncol_tile, :], lhsT=Wv[:, kt, c0:c0 + ncol_tile], rhs=uT[:, kt, half * 512:(half + 1) * 512],
                                                             start=(kt == 0), stop=(kt == 15))) for kt in range(16)],
                            reads=[wk] + [('uT', half * 4 + i) for i in range(4)], writes=[pk])
                    evac(ps, pk, half)
                if evac_s is not None:
                    ps, pk = nps()
                    T.group('pe', [(lambda e, kt=kt: e.matmul(ps[0:ncol_tile, 0:NS], lhsT=Wv[:, kt, c0:c0 + ncol_tile], rhs=uTs[:, kt, :],
                                                             start=(kt == 0), stop=(kt == 15))) for kt in range(16)],
                            reads=[wk, 'uTs'], writes=[pk])
                    evac_s(ps, pk)

            def tm_tile(Wv, wk, tt, ncols):
                ps, pk = nps()
                T.group('pe', [(lambda e, kt=kt: e.matmul(ps[:, 0:ncols], lhsT=uT[:, kt, tt * 128:(tt + 1) * 128], rhs=Wv[:, kt, 0:ncols],
                                                         start=(kt == 0), stop=(kt == 15))) for kt in range(16)],
                        reads=[wk, ('uT', tt)], writes=[pk])
                return ps, pk

            def tm_tile_s(Wv, wk, ncols):
                ps, pk = nps()
                T.group('pe', [(lambda e, kt=kt: e.matmul(ps[0:NS, 0:ncols], lhsT=uTs[:, kt, :], rhs=Wv[:, kt, 0:ncols],
                                                         start=(kt == 0), stop=(kt == 15))) for kt in range(16)],
                        reads=[wk, 'uTs'], writes=[pk])
                return ps, pk

            def store_state(dst, nk_free, key):
                if nk_free == 512:
                    src = St[:].rearrange("p (k v) -> p k v", k=2)
                else:
                    src = St[:].rearrange("p (h k v) -> p h k v", h=2, k=2)
                T.dma('sp', dst, src, sb='St', load=False, dram=key)

            def run_pass(pidx):
                full = True
                cur['pidx'] = pidx
                cur['first'] = (pidx == 0)
                cur['smp'] = (pidx == 1)
                x_d = xm if pidx == 1 else xp
                T.dma('sp', cosT[:], (cosm if pidx == 1 else cosp)[:, :], sb='cos', load=True)
                T.dma('sp', sinT[:], (sinm if pidx == 1 else sinp)[:, :], sb='sin', load=True)
                for tt in range(8):
                    norm_transpose(x_d[tt * 128:(tt + 1) * 128, :], 128, uT, ('uT', tt), tt * 128, C_NM)
                if cur['smp']:
                    norm_transpose(xs[:, :], NS, uTs, 'uTs', 0, C_NM)

                Wv, wk = w_next()

                def ev_r(ps, pk, half):
                    T.op('act', lambda e: e.activation(raT[0:16, half * 512:(half + 1) * 512], ps[0:16, :], AF.Copy),
                         reads=[pk], writes=[('raT', half)])

                def ev_rs(ps, pk):
                    T.op('act', lambda e: e.activation(raT[0:16, NT:NT + NS], ps[0:16, 0:NS], AF.Copy), reads=[pk], writes=[('raT', 's')])

                fm_block(Wv, wk, 0, 16, ev_r, ev_rs if cur['smp'] else None)

                for g in range(2):
                    gla_unit(g, full)
                    ret_unit(g, full)

            cvt_pending = [(i, (wout if i < 4 else wgate)[:, (i % 4) * 8192:(i % 4 + 1) * 8192]) for i in range(8)]

            coll_pending = []

            def emit_coll():
                if coll_pending and cur['pidx'] == 1:
                    coll_pending.pop(0)()

            def emit_cvt(n):
                for _ in range(n):
                    if cvt_pending and cur['pidx'] == 0:
                        i, src = cvt_pending.pop(0)
                        T.dma('pool', wcv[i][:, :], src, sb=('wcvs', i), load=True, extra=[('wcv', i)], max_dma_last_dim=8192)

            def gla_unit(h, full):
                emit_cvt(2)
                if h == 1:
                    emit_coll()
                wup = C[0:17, C_WUP + h * 256:C_WUP + (h + 1) * 256]
                for c in range(8):
                    ps, pk = nps()
                    T.group('pe', [lambda e: e.matmul(ps[:, 0:256], lhsT=raT[0:17, c * 128:(c + 1) * 128], rhs=wup, start=True, stop=True)],
                            reads=[('raT', c // 4), 'C'], writes=[pk])
                    T.op('act', lambda e: e.activation(Lx[:], ps[:, 0:256], AF.Exp, scale=-1.0), reads=[pk], writes=['Lx'])
                    T.op('act', lambda e: e.activation(Lt[:], Lx[:], AF.Ln, bias=1.0), reads=['Lx'], writes=['Lt'])
                    ps2, pk2 = nps()
                    T.group('pe', [(lambda e, kt=kt: e.matmul(ps2[:, kt * 128:(kt + 1) * 128], lhsT=Lt[:, kt * 128:(kt + 1) * 128], rhs=tri, start=True, stop=True))
                                   for kt in range(2)], reads=['Lt', 'C'], writes=[pk2])
                    cum = ps2[:, 0:256].rearrange("p (k t) -> p k t", k=2)
                    T.op('act', lambda e: e.activation(Et[:, :, c * 128:(c + 1) * 128], cum, AF.Exp, scale=-1.0 / 16.0), reads=[pk2], writes=[('E', c)])
                if cur['smp']:
                    for kt in range(2):
                        ps, pk = nps()
                        T.group('pe', [lambda e: e.matmul(ps[:, 0:NS], lhsT=C[0:17, C_WUP + h * 256 + kt * 128:C_WUP + h * 256 + (kt + 1) * 128],
                                                          rhs=raT[0:17, NT:NT + NS], start=True, stop=True)], reads=[('raT', 's'), 'C'], writes=[pk])
                        T.op('act', lambda e: e.activation(Lx[:, 0:NS], ps[:, 0:NS], AF.Exp, scale=-1.0), reads=[pk], writes=['Lx'])
                        T.op('act', lambda e: e.activation(Lx[:, 32:32 + NS], Lx[:, 0:NS], AF.Ln, bias=1.0), reads=['Lx'], writes=['Lx'])
                        T.op('act', lambda e: e.activation(aTs[:, h * 2 + kt, :], Lx[:, 32:32 + NS], AF.Exp, scale=-1.0 / 16.0), reads=['Lx'], writes=['aTs'])

                if full:
                    Wv, wk = w_next()
                    for ct in range(2):
                        def ev_q(ps, pk, half, ct=ct):
                            T.op('dve', lambda e: e.scalar_tensor_tensor(qT[:, ct, half * 512:(half + 1) * 512], ps[:], 1.0 / 16.0,
                                                                         Et[:, ct, half * 512:(half + 1) * 512], ALU.mult, ALU.mult),
                                 reads=[pk] + [('E', half * 4 + i) for i in range(4)], writes=[('qT', ct, half * 4 + i) for i in range(4)])

                        def ev_qs(ps, pk, ct=ct):
                            T.op('dve', lambda e: e.tensor_scalar(qTs_a[:, h * 2 + ct, :], ps[:, 0:NS], 1.0 / 16.0, None, ALU.mult), reads=[pk], writes=['qTs_a'])
                        fm_block(Wv, wk, ct * 128, 128, ev_q, ev_qs if cur['smp'] else None)
                Wv, wk = w_next()
                for ct in range(2):
                    def ev_k(ps, pk, half, ct=ct):
                        T.op('dve', lambda e: e.reciprocal(rtA[0][:], Et[:, ct, half * 512:(half + 1) * 512]),
                             reads=[('E', half * 4 + i) for i in range(4)], writes=[('rtA', 0)])
                        T.op('dve', lambda e: e.tensor_tensor(kT[:, ct, half * 512:(half + 1) * 512], ps[:], rtA[0][:], ALU.mult),
                             reads=[pk, ('rtA', 0)], writes=[('kT', ct, half * 4 + i) for i in range(4)])

                    def ev_ks(ps, pk, ct=ct):
                        T.op('dve', lambda e: e.tensor_copy(kTs_a[:, h * 2 + ct, :], ps[:, 0:NS]), reads=[pk], writes=['kTs_a'])
                    fm_block(Wv, wk, ct * 128, 128, ev_k, ev_ks if cur['smp'] else None)
                Wv, wk = w_next()
                for tt in range(8):
                    ps, pk = tm_tile(Wv, wk, tt, 512)
                    T.op('act', lambda e: e.activation(vt[:, tt, :], ps[:], AF.Copy), reads=[pk], writes=[('v', tt)])
                if cur['smp']:
                    ps, pk = tm_tile_s(Wv, wk, 512)
                    T.op('act', lambda e: e.activation(vs_a[:, h * 512:(h + 1) * 512], ps[0:NS, :], AF.Copy), reads=[pk], writes=['vs_a'])

                S3 = St[:].rearrange("p (k v) -> p k v", k=2)
                Sb3 = Sbf[:].rearrange("p (k v) -> p k v", k=2)
                if not cur['first']:
                    T.dma('sp', S3, spg[h].rearrange("(p k) v -> p k v", k=2), sb='St', load=True, dram=('spg', h))
                else:
                    T.op('pool', lambda e: e.memset(St[:], 0.0), writes=['St'])
                T.op('pool', lambda e: e.tensor_copy(Sbf[:], St[:]), reads=['St'], writes=['Sbf'])
                Wg, wkg = w_next()
                Wm, wkm = w_next(hold=True)

                for c in range(8):
                    csl = slice(c * 128, (c + 1) * 128)
                    if full:
                        pg, pkg = tm_tile(Wg, wkg, c, 512)
                        T.op('act', lambda e: e.activation(slt[:], pg[:], AF.Silu), reads=[pkg], writes=['sl'])
                        T.op('pool', lambda e: e.tensor_tensor(slt[:], slt[:], C[:, C_GN:C_GN + 512], ALU.mult), reads=['sl', 'C'], writes=['sl'])
                        pa, pka = nps()
                        T.group('pe', [(lambda e, kt=kt: e.matmul(pa[:, 0:128], lhsT=kT[:, kt, csl], rhs=qT[:, kt, csl], start=(kt == 0), stop=(kt == 1)))
                                       for kt in range(2)], reads=[('kT', 0, c), ('kT', 1, c), ('qT', 0, c), ('qT', 1, c)], writes=[pka])
                        ai = cnt['at'] % 2
                        cnt['at'] += 1
                        at = ATm[ai]
                        T.op('dve', lambda e: e.tensor_tensor(at[:, 0:128], pa[:, 0:128], tri, ALU.mult), reads=[pka, 'C'], writes=[('ATm', ai, 0)])
                    for kt in range(2):
                        T.op('dve', lambda e, kt=kt: e.tensor_scalar(khT[:, kt, :], kT[:, kt, csl], Et[:, kt, c * 128 + 127:c * 128 + 128], None, ALU.mult),
                             reads=[('kT', kt, c), ('E', c)], writes=['khT'])
                    pt, pkt = nps()
                    ptb = pt.bitcast(BF16)
                    T.group('pe', [(lambda e, kt=kt: e.transpose(ptb[:, kt * 128:(kt + 1) * 128], khT[:, kt, :], identB)) for kt in range(2)],
                            reads=['khT', 'CB'], writes=[pkt])
                    ki = cnt['kh'] % 2
                    cnt['kh'] += 1
                    kh = khat[ki]
                    T.op('act', lambda e: e.activation(kh[:, 0:256], ptb[:, 0:256], AF.Copy), reads=[pkt], writes=[('khat', ki, 0), ('khat', ki, 1)])
                    if full:
                        pmm, pkm = tm_tile(Wm, wkm, c, 512)
                        T.op('act', lambda e: e.activation(smt[:], pmm[:], AF.Sigmoid), reads=[pkm], writes=['sm'])
                        T.op('pool', lambda e: e.tensor_tensor(Gt[:], slt[:], smt[:], ALU.mult), reads=['sl', 'sm'], writes=['G'])
                        po, pko = nps()
                        T.group('pe', [lambda e: e.matmul(po[:], lhsT=at[:, 0:128], rhs=vt[:, c, :], start=True, stop=False)] +
                                [(lambda e, kt=kt: e.matmul(po[:], lhsT=qT[:, kt, csl], rhs=Sb3[:, kt, :], start=False, stop=(kt == 1))) for kt in range(2)],
                                reads=[('ATm', ai, 0), ('v', c), ('qT', 0, c), ('qT', 1, c), 'Sbf'], writes=[pko])
                    for kt in range(2):
                        pu, pku = nps()
                        T.group('pe', [lambda e, kt=kt: e.matmul(pu[:], lhsT=kh[:, kt * 128:(kt + 1) * 128], rhs=vt[:, c, :], start=True, stop=True)],
                                reads=[('khat', ki, 0), ('khat', ki, 1), ('v', c)], writes=[pku])
                        T.op('dve', lambda e, kt=kt: e.scalar_tensor_tensor(S3[:, kt, :], S3[:, kt, :], Et[:, kt, c * 128 + 127:c * 128 + 128], pu[:], ALU.mult, ALU.add),
                             reads=['St', ('E', c), pku], writes=['St'])
                    if full:
                        T.op('pool', lambda e: e.tensor_copy(Sbf[:], St[:]), reads=['St'], writes=['Sbf'])
                        si = new_stat(2)
                        ss = stat[:, si:si + 1]
                        rs = stat[:, si + 1:si + 2]
                        T.op('act', lambda e: e.activation(junk5[:], po[:], AF.Square, scale=float(512 ** -0.5), accum_out=ss),
                             reads=[pko], writes=['junk5', ('stat', si)])
                        rstd_op(rs, ss, [('stat', si)], [('stat', si + 1)])
                        mi = cnt['ms'] % 2
                        cnt['ms'] += 1
                        ms = mst[mi]
                        T.op('dve', lambda e: e.scalar_tensor_tensor(ms[:], po[:], rs, Gt[:], ALU.mult, ALU.mult),
                             reads=[pko, ('stat', si + 1), 'G'], writes=[('mst', mi)])
                        T.dma('sp', mrga_l[cur['pidx']][1][c * 128:(c + 1) * 128, h * 512:(h + 1) * 512], ms[:], sb=('mst', mi), load=False, dram=('mrga', cur['pidx'], c, h))
                if cur['smp']:
                    pg, pkg = tm_tile_s(Wg, wkg, 512)
                    T.op('act', lambda e: e.activation(slt[0:NS, :], pg[0:NS, :], AF.Silu), reads=[pkg], writes=['sl'])
                    T.op('pool', lambda e: e.tensor_tensor(slt[0:NS, :], slt[0:NS, :], C[0:NS, C_GN:C_GN + 512], ALU.mult), reads=['sl', 'C'], writes=['sl'])
                    pmm, pkm = tm_tile_s(Wm, wkm, 512)
                    T.op('act', lambda e: e.activation(smt[0:NS, :], pmm[0:NS, :], AF.Sigmoid), reads=[pkm], writes=['sm'])
                    T.op('pool', lambda e: e.tensor_tensor(Gs_a[:, h * 512:(h + 1) * 512], slt[0:NS, :], smt[0:NS, :], ALU.mult), reads=['sl', 'sm'], writes=['Gs_a'])
                if not cur['first']:
                    store_state(ng[h].rearrange("(p k) v -> p k v", k=2), 512, ('ng', h))
                else:
                    store_state(spg[h].rearrange("(p k) v -> p k v", k=2), 512, ('spg', h))

            def rotary_evac(Wv, wk, hh, dstT, is_q, head, full):
                for half in range(2):
                    hs = slice(half * 512, (half + 1) * 512)
                    pss = []
                    for i in range(2):
                        ps, pk = nps()
                        c0 = hh * 256 + i * 128
                        T.group('pe', [(lambda e, kt=kt: e.matmul(ps[:], lhsT=Wv[:, kt, c0:c0 + 128], rhs=uT[:, kt, hs], start=(kt == 0), stop=(kt == 15)))
                                       for kt in range(16)], reads=[wk] + [('uT', half * 4 + j) for j in range(4)], writes=[pk])
                        pss.append((ps, pk))
                    (p1, k1), (p2, k2) = pss
                    ri = 0
                    cnt['rot'] += 1
                    tA, tB, tR = rtA[ri], rtB[ri], rtR[ri]
                    wkeys = lambda t: [(('qT' if is_q else 'kT'), hh * 2 + t, half * 4 + j) for j in range(4)]
                    T.op('dve', lambda e: e.tensor_tensor(tA[:], p1[:], cosT[:, hs], ALU.mult), reads=[k1, 'cos'], writes=[('rtA', ri)])
                    T.op('dve', lambda e: e.tensor_tensor(tB[:], p2[:], sinT[:, hs], ALU.mult), reads=[k2, 'sin'], writes=[('rtB', ri)])
                    if is_q:
                        T.op('pool', lambda e: e.tensor_tensor(tR[:], tA[:], tB[:], ALU.subtract), reads=[('rtA', ri), ('rtB', ri)], writes=[('rtR', ri)])
                        for cc in range(4):
                            T.op('pool', lambda e, cc=cc: e.tensor_tensor(dstT[:, hh * 2, half * 512 + cc * 128:half * 512 + (cc + 1) * 128], tR[:, cc * 128:(cc + 1) * 128],
                                                                        C[:, C_GAM + head * 128:C_GAM + (head + 1) * 128], ALU.mult),
                                 reads=[('rtR', ri), 'C'], writes=[wkeys(0)[cc]])
                    else:
                        T.op('pool', lambda e: e.tensor_tensor(dstT[:, hh * 2, hs], tA[:], tB[:], ALU.subtract), reads=[('rtA', ri), ('rtB', ri)], writes=wkeys(0))
                    T.op('dve', lambda e: e.tensor_tensor(tA[:], p1[:], sinT[:, hs], ALU.mult), reads=[k1, 'sin'], writes=[('rtA', ri)])
                    T.op('dve', lambda e: e.tensor_tensor(tB[:], p2[:], cosT[:, hs], ALU.mult), reads=[k2, 'cos'], writes=[('rtB', ri)])
                    if is_q:
                        T.op('pool', lambda e: e.tensor_tensor(tR[:], tA[:], tB[:], ALU.add), reads=[('rtA', ri), ('rtB', ri)], writes=[('rtR', ri)])
                        for cc in range(4):
                            T.op('pool', lambda e, cc=cc: e.tensor_tensor(dstT[:, hh * 2 + 1, half * 512 + cc * 128:half * 512 + (cc + 1) * 128], tR[:, cc * 128:(cc + 1) * 128],
                                                                        C[:, C_GAM + head * 128:C_GAM + (head + 1) * 128], ALU.mult),
                                 reads=[('rtR', ri), 'C'], writes=[wkeys(1)[cc]])
                    else:
                        T.op('pool', lambda e: e.tensor_tensor(dstT[:, hh * 2 + 1, hs], tA[:], tB[:], ALU.add), reads=[('rtA', ri), ('rtB', ri)], writes=wkeys(1))
                if full and cur['smp']:
                    pss = []
                    for i in range(2):
                        ps, pk = nps()
                        c0 = hh * 256 + i * 128
                        T.group('pe', [(lambda e, kt=kt: e.matmul(ps[:, 0:NS], lhsT=Wv[:, kt, c0:c0 + 128], rhs=uTs[:, kt, :], start=(kt == 0), stop=(kt == 15)))
                                       for kt in range(16)], reads=[wk, 'uTs'], writes=[pk])
                        pss.append((ps, pk))
                    (p1, k1), (p2, k2) = pss
                    dsts = qTs_b if is_q else kTs_b
                    dkey = 'qTs_b' if is_q else 'kTs_b'
                    o = C_CS if is_q else C_CS + 2
                    cs = C[:, o:o + 1]
                    sn = C[:, o + 1:o + 2]
                    ct0 = head * 2
                    T.op('dve', lambda e: e.tensor_scalar(Lx[:, 64:64 + NS], p2[:, 0:NS], sn, None, ALU.mult), reads=[k2, 'C'], writes=['Lx'])
                    T.op('dve', lambda e: e.scalar_tensor_tensor(dsts[:, ct0, :], p1[:, 0:NS], cs, Lx[:, 64:64 + NS], ALU.mult, ALU.subtract),
                         reads=[k1, 'C', 'Lx'], writes=[dkey])
                    T.op('dve', lambda e: e.tensor_scalar(Lx[:, 96:96 + NS], p1[:, 0:NS], sn, None, ALU.mult), reads=[k1, 'C'], writes=['Lx'])
                    T.op('dve', lambda e: e.scalar_tensor_tensor(dsts[:, ct0 + 1, :], p2[:, 0:NS], cs, Lx[:, 96:96 + NS], ALU.mult, ALU.add),
                         reads=[k2, 'C', 'Lx'], writes=[dkey])

            def ret_unit(j, full):
                emit_cvt(2)
                if j == 0:
                    emit_coll()
                heads = (2 * j, 2 * j + 1)
                if full:
                    Wv, wk = w_next()
                    for hh in range(2):
                        rotary_evac(Wv, wk, hh, qT, True, heads[hh], True)
                Wv, wk = w_next()
                for hh in range(2):
                    rotary_evac(Wv, wk, hh, kT, False, heads[hh], full)
                Wv, wk = w_next()
                for tt in range(8):
                    ps, pk = tm_tile(Wv, wk, tt, 512)
                    T.op('act', lambda e: e.activation(vt[:, tt, :], ps[:], AF.Copy), reads=[pk], writes=[('v', tt)])
                if cur['smp']:
                    ps, pk = tm_tile_s(Wv, wk, 512)
                    T.op('act', lambda e: e.activation(vs_b[:, j * 512:(j + 1) * 512], ps[0:NS, :], AF.Copy), reads=[pk], writes=['vs_b'])
                S4 = St[:].rearrange("p (h k v) -> p h k v", h=2, k=2)
                Sb4 = Sbf[:].rearrange("p (h k v) -> p h k v", h=2, k=2)
                if not cur['first']:
                    T.dma('sp', S4, spr[2 * j:2 * j + 2].rearrange("h (k p) v -> p h k v", p=128), sb='St', load=True, dram=('spr', j))
                else:
                    T.op('pool', lambda e: e.memset(St[:], 0.0), writes=['St'])
                T.op('pool', lambda e: e.tensor_copy(Sbf[:], St[:]), reads=['St'], writes=['Sbf'])
                Wg, wkg = w_next()
                Wm, wkm = w_next(hold=True)
                for c in range(8):
                    csl = slice(c * 128, (c + 1) * 128)
                    if full:
                        pg, pkg = tm_tile(Wg, wkg, c, 512)
                        T.op('act', lambda e: e.activation(slt[:], pg[:], AF.Silu), reads=[pkg], writes=['sl'])
                        ai = cnt['at'] % 2
                        cnt['at'] += 1
                        at = ATm[ai]
                        for hh in range(2):
                            pa, pka = nps()
                            T.group('pe', [(lambda e, i=i: e.matmul(pa[:, 0:128], lhsT=kT[:, hh * 2 + i, csl], rhs=qT[:, hh * 2 + i, csl], start=(i == 0), stop=(i == 1)))
                                           for i in range(2)], reads=[('kT', hh * 2, c), ('kT', hh * 2 + 1, c), ('qT', hh * 2, c), ('qT', hh * 2 + 1, c)], writes=[pka])
                            hd = heads[hh]
                            T.op('dve', lambda e, hh=hh, hd=hd, pa=pa: e.tensor_tensor(at[:, hh * 128:(hh + 1) * 128], pa[:, 0:128], C[:, C_DT + hd * 128:C_DT + (hd + 1) * 128], ALU.mult),
                                 reads=[pka, 'C'], writes=[('ATm', ai, hh)])
                    ki = cnt['kh'] % 2
                    cnt['kh'] += 1
                    kh = khat[ki]
                    for hh in range(2):
                        pt, pkt = nps()
                        ptb = pt.bitcast(BF16)
                        T.group('pe', [(lambda e, i=i: e.transpose(ptb[:, i * 128:(i + 1) * 128], kT[:, hh * 2 + i, csl], identB)) for i in range(2)],
                                reads=[('kT', hh * 2, c), ('kT', hh * 2 + 1, c), 'CB'], writes=[pkt])
                        hd = heads[hh]
                        T.op('act', lambda e, hh=hh, hd=hd, ptb=ptb: e.activation(kh[:, hh * 256:(hh + 1) * 256], ptb[:, 0:256], AF.Copy, scale=C[:, C_GKH + hd:C_GKH + hd + 1]),
                             reads=[pkt, 'C'], writes=[('khat', ki, hh)])
                    if full:
                        pmm, pkm = tm_tile(Wm, wkm, c, 512)
                        T.op('act', lambda e: e.activation(smt[:], pmm[:], AF.Sigmoid), reads=[pkm], writes=['sm'])
                        T.op('pool', lambda e: e.tensor_tensor(Gt[:], slt[:], smt[:], ALU.mult), reads=['sl', 'sm'], writes=['G'])
                        po, pko = nps()
                        fns = []
                        for hh in range(2):
                            osl = slice(hh * 256, (hh + 1) * 256)
                            fns.append(lambda e, hh=hh, osl=osl: e.matmul(po[:, osl], lhsT=at[:, hh * 128:(hh + 1) * 128], rhs=vt[:, c, osl], start=True, stop=False))
                            for i in range(2):
                                fns.append(lambda e, hh=hh, osl=osl, i=i: e.matmul(po[:, osl], lhsT=qT[:, hh * 2 + i, csl], rhs=Sb4[:, hh, i, :], start=False, stop=(i == 1)))
                        T.group('pe', fns, reads=[('ATm', ai, 0), ('ATm', ai, 1), ('v', c), 'Sbf'] + [('qT', t, c) for t in range(4)], writes=[pko])
                    for hh in range(2):
                        pu, pku = nps()
                        osl = slice(hh * 256, (hh + 1) * 256)
                        T.group('pe', [(lambda e, i=i: e.matmul(pu[:, i * 256:(i + 1) * 256], lhsT=kh[:, hh * 256 + i * 128:hh * 256 + (i + 1) * 128], rhs=vt[:, c, osl], start=True, stop=True))
                                       for i in range(2)], reads=[('khat', ki, hh), ('v', c)], writes=[pku])
                        hd = heads[hh]
                        T.op('dve', lambda e, hh=hh, hd=hd, pu=pu: e.scalar_tensor_tensor(St[:, hh * 512:(hh + 1) * 512], St[:, hh * 512:(hh + 1) * 512], C[:, C_G128 + hd:C_G128 + hd + 1], pu[:], ALU.mult, ALU.add),
                             reads=['St', pku, 'C'], writes=['St'])
                    if full:
                        T.op('pool', lambda e: e.tensor_copy(Sbf[:], St[:]), reads=['St'], writes=['Sbf'])
                        si = new_stat(4)
                        for hh in range(2):
                            osl = slice(hh * 256, (hh + 1) * 256)
                            T.op('act', lambda e, hh=hh, osl=osl: e.activation(junk5[:, osl], po[:, osl], AF.Square, scale=float(256 ** -0.5), accum_out=stat[:, si + hh:si + hh + 1]),
                                 reads=[pko], writes=['junk5', ('stat', si + hh)])
                        rstd_op(stat[:, si + 2:si + 4], stat[:, si:si + 2], [('stat', si), ('stat', si + 1)], [('stat', si + 2), ('stat', si + 3)])
                        mi = cnt['ms'] % 2
                        cnt['ms'] += 1
                        ms = mst[mi]
                        for hh in range(2):
                            osl = slice(hh * 256, (hh + 1) * 256)
                            T.op('dve', lambda e, hh=hh, osl=osl: e.scalar_tensor_tensor(ms[:, osl], po[:, osl], stat[:, si + 2 + hh:si + 3 + hh], Gt[:, osl], ALU.mult, ALU.mult),
                                 reads=[pko, ('stat', si + 2 + hh), 'G'], writes=[('mst', mi)])
                        T.dma('sp', mrgb_l[cur['pidx']][1][c * 128:(c + 1) * 128, j * 512:(j + 1) * 512], ms[:], sb=('mst', mi), load=False, dram=('mrgb', cur['pidx'], c, j))
                if cur['smp']:
                    pg, pkg = tm_tile_s(Wg, wkg, 512)
                    T.op('act', lambda e: e.activation(slt[0:NS, :], pg[0:NS, :], AF.Silu), reads=[pkg], writes=['sl'])
                    pmm, pkm = tm_tile_s(Wm, wkm, 512)
                    T.op('act', lambda e: e.activation(smt[0:NS, :], pmm[0:NS, :], AF.Sigmoid), reads=[pkm], writes=['sm'])
                    T.op('pool', lambda e: e.tensor_tensor(Gs_b[:, j * 512:(j + 1) * 512], slt[0:NS, :], smt[0:NS, :], ALU.mult), reads=['sl', 'sm'], writes=['Gs_b'])
                if not cur['first']:
                    store_state(nr[2 * j:2 * j + 2].rearrange("h (k p) v -> p h k v", p=128), 256, ('nr', j))
                else:
                    store_state(spr[2 * j:2 * j + 2].rearrange("h (k p) v -> p h k v", p=128), 256, ('spr', j))

            def coll_a(p_):
                mk = [('mrga', p_, c_, h_) for c_ in range(8) for h_ in range(2)]
                T.coll(mrga_l[p_][0].ap().opt(), gat_a_l[p_][0].ap().opt(), mk, [('gat_a', p_)], PAIRS)

            def coll_b(p_):
                mk = [('mrgb', p_, c_, h_) for c_ in range(8) for h_ in range(2)]
                T.coll(mrgb_l[p_][0].ap().opt(), gat_b_l[p_][0].ap().opt(), mk, [('gat_b', p_)], PAIRS)

            run_pass(0)
            coll_pending.extend([lambda: coll_a(0), lambda: coll_b(0)])
            run_pass(1)
            T.barrier()

        es2 = contextlib.ExitStack()
        with es2:
            def sb2(name, shape, dt=F32):
                return es2.enter_context(nc.sbuf_tensor("t_" + name, list(shape), dt))
            Sin = [sb2("Sin%d" % i, [128, 1024]) for i in range(3)]
            Sn = [sb2("Sn%d" % i, [128, 1024]) for i in range(3)]
            tmpv = [sb2("tmpv%d" % i, [128, 512]) for i in range(4)]
            QZ = [sb2("QZ%d" % i, [128, 2, NS], BF16) for i in range(4)]
            selb = [sb2("selb%d" % i, [NS, 128], BF16) for i in range(2)]
            for i in range(4):
                T.op('dve', lambda e, i=i: e.memset(QZ[i][:], 0.0), writes=[('QZ', i)])
            Sb_ = [sb2("Sb%d" % i, [128, 1024], BF16) for i in range(2)]
            stat2 = sb2("stat2", [NS, 64])
            junk2 = sb2("junk2", [NS, 512], BF16)
            mss = sb2("mss", [NS, 1024], BF16)
            it = [0]

            def sample_head(is_gla, h, hidx):
                dv = 512 if is_gla else 256
                sd, so = (sg, nsg) if is_gla else (sr, nsr)
                vs_ = vs_a if is_gla else vs_b
                vkey = 'vs_a' if is_gla else 'vs_b'
                qsrc = qTs_a if is_gla else qTs_b
                qkey = 'qTs_a' if is_gla else 'qTs_b'
                ksrc = kTs_a if is_gla else kTs_b
                kkey = 'kTs_a' if is_gla else 'kTs_b'
                po, pko = PS[7], ('ps', 7)
                reserved.add(7)

                if is_gla:
                    nsl, skey, nkey = 3, 'Sin', 'Sn'
                    SinV = [Sin[i][:, 0:1024] for i in range(3)]
                    SnV = [Sn[i][:, 0:1024] for i in range(3)]
                else:
                    nsl, skey, nkey = 6, 'SinR', 'SnR'
                    SinV = [Sin[i // 2][:, (i % 2) * 512:(i % 2 + 1) * 512] for i in range(6)]
                    SnV = [Sn[i // 2][:, (i % 2) * 512:(i % 2 + 1) * 512] for i in range(6)]

                def load(b, slot):
                    T.dma('sp', SinV[slot].rearrange("p (k v) -> p k v", k=2), (sd[b, h].rearrange("(p k) v -> p k v", k=2) if is_gla else sd[b, h].rearrange("(k p) v -> p k v", p=128)),
                          sb=(skey, slot), load=True)
                base = it[0]

                def stage_a(b):
                    pv, pkv = nps()
                    sbi = (base + b) % 2
                    sl_ = selb[sbi]
                    T.op('dve', lambda e: e.tensor_copy(sl_[:], identB[0:NS, b:b + 1].to_broadcast([NS, 128])), reads=['CB'], writes=[('selb', sbi)])
                    T.group('pe', [lambda e: e.matmul(pv[:, 0:dv], lhsT=sl_[:], rhs=vs_[:, h * dv:(h + 1) * dv], start=True, stop=True)],
                            reads=[('selb', sbi), vkey], writes=[pkv])
                    for kt in range(2):
                        ct = h * 2 + kt
                        ti = ((base + b) % 2) * 2 + kt
                        tv = tmpv[ti]
                        kcol = ksrc[:, ct, b:b + 1]
                        T.op('act', lambda e, tv=tv, kcol=kcol: e.activation(tv[:, 0:dv], pv[:, 0:dv], AF.Copy, scale=kcol),
                             reads=[pkv, kkey], writes=[('tmpv', ti)])

                def stage_b(b):
                    slot = (base + b) % nsl
                    sn_ = SnV[slot]
                    for kt in range(2):
                        ct = h * 2 + kt
                        ti = ((base + b) % 2) * 2 + kt
                        tv = tmpv[ti]
                        dec = aTs[:, ct, b:b + 1] if is_gla else C[:, C_G1 + h:C_G1 + h + 1]
                        T.op('dve', lambda e, kt=kt, tv=tv, dec=dec: e.scalar_tensor_tensor(sn_[:, kt * dv:(kt + 1) * dv], SinV[slot][:, kt * dv:(kt + 1) * dv], dec, tv[:, 0:dv], ALU.mult, ALU.add),
                             reads=[(skey, slot), ('tmpv', ti), 'aTs', 'C'], writes=[(nkey, slot)])
                    sbb = Sb_[(base + b) % 2]
                    T.op('dve', lambda e: e.tensor_copy(sbb[:, 0:2 * dv], sn_[:, 0:2 * dv]), reads=[(nkey, slot)], writes=[('Sb', (base + b) % 2)])
                    T.dma('sp', (so[b, h].rearrange("(p k) v -> p k v", k=2) if is_gla else so[b, h].rearrange("(k p) v -> p k v", p=128)), sn_[:, 0:2 * dv].rearrange("p (k v) -> p k v", k=2), sb=(nkey, slot), load=False)

                for b0 in range(nsl):
                    load(b0, (base + b0) % nsl)
                stage_a(0)
                stage_a(1)
                stage_b(0)
                for b in range(NS):
                    if b + 2 < NS:
                        stage_a(b + 2)
                    if b + 1 < NS:
                        stage_b(b + 1)
                    if b + nsl < NS:
                        load(b + nsl, (base + b + nsl) % nsl)
                    sbb = Sb_[(base + b) % 2]
                    qi = (base + b) % 4
                    qz = QZ[qi]
                    T.op('dve', lambda e: e.tensor_copy(qz[:, :, b:b + 1], qsrc[:, h * 2:h * 2 + 2, b:b + 1]), reads=[qkey], writes=[('QZ', qi)])
                    T.group('pe', [(lambda e, kt=kt: e.matmul(po[0:NS, 0:dv], lhsT=qz[:, kt, :], rhs=sbb[:, kt * dv:(kt + 1) * dv],
                                                             start=(b == 0 and kt == 0), stop=(b == NS - 1 and kt == 1))) for kt in range(2)],
                            reads=[('Sb', (base + b) % 2), ('QZ', qi)], writes=[pko])
                    if b >= 2:
                        qj = (base + b - 2) % 4
                        T.op('dve', lambda e: e.memset(QZ[qj][:, :, b - 2:b - 1], 0.0), writes=[('QZ', qj)])
                    yield
                for bb in (NS - 2, NS - 1):
                    qj = (base + bb) % 4
                    T.op('dve', lambda e: e.memset(QZ[qj][:, :, bb:bb + 1], 0.0), writes=[('QZ', qj)])
                it[0] = base + NS
                si = (hidx * 2) % 60
                ss = stat2[:, si:si + 1]
                rs = stat2[:, si + 1:si + 2]
                T.op('act', lambda e: e.activation(junk2[:, 0:dv], po[0:NS, 0:dv], AF.Square, scale=float(dv ** -0.5), accum_out=ss), reads=[pko], writes=['junk2', ('stat2', si)])
                rstd_op(rs, ss, [('stat2', si)], [('stat2', si + 1)])
                G = (Gs_a if is_gla else Gs_b)[:, h * dv:(h + 1) * dv]
                T.op('dve', lambda e: e.scalar_tensor_tensor(mss[:, h * dv:(h + 1) * dv], po[0:NS, 0:dv], rs, G, ALU.mult, ALU.mult),
                     reads=[pko, ('stat2', si + 1), 'Gs_a', 'Gs_b'], writes=['mss'])
                reserved.discard(7)

            def sample_gen():
                for h in range(2):
                    yield from sample_head(True, h, h)
                T.dma('sp', mrgsa[:, :], mss[:], sb='mss', load=False, dram='mrgsa')
                T.coll(mrgsa_f.ap().opt(), gat_sa_f.ap().opt(), ['mrgsa'], ['gat_sa'], PAIRS)
                T.barrier()
                it[0] = 0
                for h in range(4):
                    yield from sample_head(False, h, 2 + h)
                T.dma('sp', mrgsb[:, :], mss[:], sb='mss', load=False, dram='mrgsb')
                T.coll(mrgsb_f.ap().opt(), gat_sb_f.ap().opt(), ['mrgsb'], ['gat_sb'], PAIRS)

            NFB = sb2("nfb", [128, 2048])
            T.dma('sp', NFB[:], nfb[:, :], sb='NFB', load=True)
            WP = sb2("WP", [128, 2, 2048], BF16)
            T.dma('pool', WP[:].rearrange("p k c -> p (k c)"), wproj[:, :], sb='WP', load=True, max_dma_last_dim=8192)
            hbuf = sb2("h", [128, 2, 2048])
            mT = sb2("mT", [128, 16, 256], BF16)
            hnT = sb2("hnT", [128, 16, 256], BF16)
            ma_t = sb2("ma_t", [128, 2048], BF16); mb_t = sb2("mb_t", [128, 2048], BF16)
            ma2_t = sb2("ma2_t", [128, 2048], BF16); mb2_t = sb2("mb2_t", [128, 2048], BF16)
            pst = sb2("pst", [128, 256]); psc = sb2("psc", [128, 256], BF16); pTg = sb2("pTg", [128, 2, 256], BF16)
            hn = sb2("hn", [128, 2048], BF16)
            gsig = [sb2("gsig%d" % i, [128, 512]) for i in range(2)]
            stat3 = sb2("stat3", [128, 64])
            c3 = dict(st=0, g=0)

            def st3(n):
                i = c3['st']
                if i + n > 64:
                    i = 0
                c3['st'] = i + n
                return i

            def post_group(grp):
                for li, (tt, rows) in enumerate(grp):
                    for hf, (ta, tb) in enumerate(((ma_t, mb_t), (ma2_t, mb2_t))):
                        for rk in range(2):
                            if tt < 8:
                                ra = rk * NT + tt * 128
                                srca, srcb, dka, dkb = gat_a_l[hf][1][ra:ra + rows, :], gat_b_l[hf][1][ra:ra + rows, :], ('gat_a', hf), ('gat_b', hf)
                            else:
                                ra = rk * NS + hf * NS1
                                srca, srcb, dka, dkb = gat_sa[ra:ra + rows, :], gat_sb[ra:ra + rows, :], 'gat_sa', 'gat_sb'
                            T.dma('pool', ta[0:rows, rk * 1024:(rk + 1) * 1024], srca, sb=('mld', hf, 0, rk), load=True, dram=dka)
                            T.dma('pool', tb[0:rows, rk * 1024:(rk + 1) * 1024], srcb, sb=('mld', hf, 1, rk), load=True, dram=dkb)
                    mkeys = [('mld', hf_, ab_, rk_) for hf_ in range(2) for ab_ in range(2) for rk_ in range(2)]
                    T.op('dve', lambda e: e.tensor_tensor(ma_t[0:rows, :], ma_t[0:rows, :], mb_t[0:rows, :], ALU.add), reads=mkeys, writes=mkeys)
                    T.op('dve', lambda e: e.tensor_tensor(ma2_t[0:rows, :], ma2_t[0:rows, :], mb2_t[0:rows, :], ALU.add), reads=mkeys, writes=mkeys)
                    T.op('dve', lambda e: e.tensor_scalar(ma_t[0:rows, :], ma_t[0:rows, :], C[0:rows, C_MASK:C_MASK + 1], None, ALU.mult), reads=mkeys + ['C'], writes=mkeys)
                    T.op('dve', lambda e: e.scalar_tensor_tensor(ma_t[0:rows, :], ma2_t[0:rows, :], C[0:rows, C_MASK + 1:C_MASK + 2], ma_t[0:rows, :], ALU.mult, ALU.add),
                         reads=mkeys + ['C'], writes=mkeys + ['ma_t'])
                    for q in range(2):
                        pt, pkt = nps()
                        ptb = pt.bitcast(BF16)
                        T.group('pe', [(lambda e, j=j: e.transpose(ptb[:, j * 128:j * 128 + rows], ma_t[0:rows, (q * 8 + j) * 128:(q * 8 + j + 1) * 128], identB[0:rows, 0:rows]))
                                       for j in range(8)], reads=['ma_t', 'CB'] + mkeys, writes=[pkt])
                        src = ptb[:, 0:1024].rearrange("p (j t) -> p j t", j=8)[:, :, 0:rows]
                        T.op('act', lambda e, src=src, q=q: e.activation(mT[:, q * 8:(q + 1) * 8, li * 128:li * 128 + rows], src, AF.Copy), reads=[pkt], writes=[('mT', li)])
                    yield
                for cb in range(4):
                    Wv, wk = w_next()
                    for li, (tt, rows) in enumerate(grp):
                        ps, pk = nps()
                        for k0 in range(0, 16, 4):
                            T.group('pe', [(lambda e, kt=kt: e.matmul(ps[0:rows, :], lhsT=mT[:, kt, li * 128:li * 128 + rows], rhs=Wv[:, kt, :], start=(kt == 0), stop=(kt == 15)))
                                           for kt in range(k0, k0 + 4)], reads=[wk, ('mT', li)], writes=[pk])
                            if k0 < 12:
                                yield
                        xsrc = (xpost[tt * 128:tt * 128 + rows, cb * 512:(cb + 1) * 512] if tt < 8 else xs16[:, cb * 512:(cb + 1) * 512])
                        T.dma('pool', hbuf[0:rows, li, cb * 512:(cb + 1) * 512], xsrc, sb=('h', li, cb), load=True)
                        T.op('dve', lambda e, li=li, rows=rows, ps=ps: e.tensor_tensor(hbuf[0:rows, li, cb * 512:(cb + 1) * 512], ps[0:rows, :], hbuf[0:rows, li, cb * 512:(cb + 1) * 512], ALU.add),
                             reads=[pk, ('h', li, cb)], writes=[('h', li, cb)])
                        yield
                for li, (tt, rows) in enumerate(grp):
                    si = st3(2)
                    ss = stat3[0:rows, si:si + 1]
                    rs = stat3[0:rows, si + 1:si + 2]
                    T.op('act', lambda e: e.activation(hn[0:rows, :], hbuf[0:rows, li, :], AF.Square, scale=float(2048 ** -0.5), accum_out=ss),
                         reads=[('h', li, cb) for cb in range(4)], writes=['hn', ('stat3', si)])
                    rstd_op(rs, ss, [('stat3', si)], [('stat3', si + 1)])
                    T.op('dve', lambda e: e.tensor_scalar(hn[0:rows, :], hbuf[0:rows, li, :], rs, None, ALU.mult),
                         reads=[('h', li, cb) for cb in range(4)] + [('stat3', si + 1)], writes=['hn'])
                    for q in range(2):
                        pt, pkt = nps()
                        ptb = pt.bitcast(BF16)
                        T.group('pe', [(lambda e, j=j: e.transpose(ptb[:, j * 128:j * 128 + rows], hn[0:rows, (q * 8 + j) * 128:(q * 8 + j + 1) * 128], identB[0:rows, 0:rows]))
                                       for j in range(8)], reads=['hn', 'CB'], writes=[pkt])
                        src = ptb[:, 0:1024].rearrange("p (j t) -> p j t", j=8)[:, :, 0:rows]
                        gn_ = C[:, C_NP + q * 8:C_NP + (q + 1) * 8].unsqueeze(2).to_broadcast([128, 8, rows])
                        T.op('dve', lambda e, src=src, q=q, gn_=gn_: e.tensor_tensor(hnT[:, q * 8:(q + 1) * 8, li * 128:li * 128 + rows], src, gn_, ALU.mult),
                             reads=[pkt, 'C'], writes=[('hnT', li)])
                    psrc = pm[tt * 128:tt * 128 + rows, :] if tt < 8 else psd[:, :]
                    T.dma('pool', pst[0:rows, :], psrc, sb='pst', load=True)
                    T.op('pool', lambda e, rows=rows: e.tensor_copy(psc[0:rows, :], pst[0:rows, :]), reads=['pst'], writes=['psc'])
                    pt, pkt = nps()
                    ptb = pt.bitcast(BF16)
                    T.group('pe', [(lambda e, j=j, rows=rows: e.transpose(ptb[:, j * 128:j * 128 + rows], psc[0:rows, j * 128:(j + 1) * 128], identB[0:rows, 0:rows]))
                                   for j in range(2)], reads=['psc', 'CB'], writes=[pkt])
                    T.op('act', lambda e, rows=rows, ptb=ptb, li=li: e.activation(pTg[:, :, li * 128:li * 128 + rows],
                                                                               ptb[:, 0:256].rearrange("p (j t) -> p j t", j=2)[:, :, 0:rows], AF.Copy),
                         reads=[pkt], writes=[('pTg', li)])
                    yield
                for cb in range(4):
                    Wv, wk = w_next()
                    for li, (tt, rows) in enumerate(grp):
                        ps, pk = nps()
                        for k0 in range(0, 16, 4):
                            T.group('pe', [(lambda e, kt=kt: e.matmul(ps[0:rows, :], lhsT=hnT[:, kt, li * 128:li * 128 + rows], rhs=Wv[:, kt, :], start=(kt == 0), stop=(kt == 15)))
                                           for kt in range(k0, k0 + 4)], reads=[wk, ('hnT', li)], writes=[pk])
                            if k0 < 12:
                                yield
                        gi = c3['g'] % 2
                        c3['g'] += 1
                        gs = gsig[gi]
                        T.op('act', lambda e, rows=rows, ps=ps: e.activation(gs[0:rows, :], ps[0:rows, :], AF.Sigmoid), reads=[pk], writes=[('gsig', gi)])
                        pp, pkp = nps()
                        T.group('pe', [(lambda e, k2=k2: e.matmul(pp[0:rows, :], lhsT=pTg[:, k2, li * 128:li * 128 + rows], rhs=WP[:, k2, cb * 512:(cb + 1) * 512], start=(k2 == 0), stop=(k2 == 1)))
                                       for k2 in range(2)], reads=['WP', ('pTg', li)], writes=[pkp])
                        T.op('dve', lambda e, rows=rows, pp=pp: e.tensor_tensor(gs[0:rows, :], gs[0:rows, :], pp[0:rows, :], ALU.mult), reads=[('gsig', gi), pkp], writes=[('gsig', gi)])
                        T.op('pool', lambda e, rows=rows, li=li: e.tensor_tensor(hbuf[0:rows, li, cb * 512:(cb + 1) * 512], hbuf[0:rows, li, cb * 512:(cb + 1) * 512], gs[0:rows, :], ALU.add),
                             reads=[('gsig', gi), ('h', li, cb)], writes=[('h', li, cb)])
                        yield
                for li, (tt, rows) in enumerate(grp):
                    si = st3(2)
                    ss = stat3[0:rows, si:si + 1]
                    rs = stat3[0:rows, si + 1:si + 2]
                    T.op('act', lambda e: e.activation(hn[0:rows, :], hbuf[0:rows, li, :], AF.Square, scale=float(2048 ** -0.5), accum_out=ss),
                         reads=[('h', li, cb) for cb in range(4)], writes=['hn', ('stat3', si)])
                    rstd_op(rs, ss, [('stat3', si)], [('stat3', si + 1)])
                    dst = y_m[tt * 128:tt * 128 + rows, :] if tt < 8 else y_s[:, :]
                    for cb in range(4):
                        T.op('dve', lambda e, cb=cb: e.scalar_tensor_tensor(hbuf[0:rows, li, cb * 512:(cb + 1) * 512], hbuf[0:rows, li, cb * 512:(cb + 1) * 512], rs,
                                                                            NFB[0:rows, cb * 512:(cb + 1) * 512], ALU.mult, ALU.mult),
                             reads=[('h', li, cb), ('stat3', si + 1), 'NFB'], writes=[('h', li, cb)])
                        T.dma('pool', dst[:, cb * 512:(cb + 1) * 512], hbuf[0:rows, li, cb * 512:(cb + 1) * 512], sb=('h', li, cb), load=False)
                    yield

            def post_gen():
                for grp in ([(0, 128), (1, 128)], [(2, 128), (3, 128)], [(4, 128), (5, 128)], [(6, 128), (7, 128)]):
                    yield from post_group(grp)

            coll_a(1)
            coll_b(1)
            sg_ = sample_gen()
            pg_ = post_gen()
            alive_s, alive_p = True, True
            acc = 0.0
            while alive_s or alive_p:
                if alive_s:
                    acc += SAMPLE_PER_POST if alive_p else 1.0
                    while acc >= 1.0 and alive_s:
                        acc -= 1.0
                        try:
                            next(sg_)
                        except StopIteration:
                            alive_s = False
                if alive_p:
                    try:
                        next(pg_)
                    except StopIteration:
                        alive_p = False
            for _ in post_group([(8, NS1)]):
                pass
            T.barrier()
        esB.__exit__(None, None, None)
    return nc


_NC_CACHE = {}


def _host_consts(r, norm_mix, norm_ple, gla_norm, w_gla_up, b_gla):
    lg = _gammas()
    cst = np.zeros((128, NCST), np.float32)
    cst[:, C_ID:C_ID + 128] = np.eye(128, dtype=np.float32)
    i = np.arange(128)
    U = (i[:, None] <= i[None, :]).astype(np.float32)
    cst[:, C_TRI:C_TRI + 128] = U
    for hl in range(4):
        h = 4 * r + hl
        d = np.exp(-(i[:, None] + 1.0) * lg[h]) / 16.0 * U
        cst[:, C_DT + hl * 128:C_DT + (hl + 1) * 128] = d.astype(np.float32)
        cst[:, C_GAM + hl * 128:C_GAM + (hl + 1) * 128] = np.exp((i[None, :] + 1.0) * lg[h]).astype(np.float32)
        cst[:, C_GKH + hl] = (np.exp((127.0 - i) * lg[h]) / 16.0).astype(np.float32)
        cst[:, C_G128 + hl] = np.float32(np.exp(128.0 * lg[h]))
        cst[:, C_G1 + hl] = np.float32(np.exp(lg[h]))
    cst[:, C_MASK] = 1.0 if r == 0 else 0.0
    cst[:, C_MASK + 1] = 1.0 if r == 1 else 0.0
    cst[:, C_NM:C_NM + 16] = norm_mix.reshape(16, 128).T
    cst[:, C_NP:C_NP + 16] = norm_ple.reshape(16, 128).T
    cst[:, C_GN:C_GN + 512] = gla_norm.reshape(1, 512)
    inv = (1.0 / (np.float32(10000.0) ** np.linspace(0.0, 1.0, 128, dtype=np.float32))).astype(np.float32)
    ang = (np.float32(16384.0) * inv).astype(np.float32)
    cst[:, C_CS] = np.cos(ang); cst[:, C_CS + 1] = np.sin(ang)
    cst[:, C_CS + 2] = np.cos(ang) / 16.0; cst[:, C_CS + 3] = np.sin(ang) / 16.0
    for hl in range(2):
        cols = (2 * r + hl) * 256 + GLA_PERM
        cst[0:16, C_WUP + hl * 256:C_WUP + (hl + 1) * 256] = w_gla_up[:, cols]
        cst[16, C_WUP + hl * 256:C_WUP + (hl + 1) * 256] = b_gla[cols]
    cbf = np.zeros((128, NCBF), np.float32)
    cbf[:, B_ID:B_ID + 128] = np.eye(128, dtype=np.float32)
    return cst, cbf.astype(ml_dtypes.bfloat16), inv


def _rot_tables(inv, pos0):
    pos = (pos0 + np.arange(NT)).astype(np.float32)
    ang = pos[:, None] * inv[None, :]
    return np.ascontiguousarray(np.cos(ang).T.astype(np.float32)), np.ascontiguousarray(np.sin(ang).T.astype(np.float32))


def _pack_w(w, ncb):
    K, N = w.shape
    nk = K // 128
    a = w.reshape(nk, 128, N // ncb, ncb).transpose(1, 2, 0, 3)
    return np.ascontiguousarray(a.reshape(128, -1))


def kernel(x_prompt, x_sample, state_gla, state_ret, p_prompt, p_sample, norm_mix, w_in, w_gla_up, b_gla, gla_norm,
           w_out, norm_ple, w_ple_gate, w_ple_proj, norm_final):
    f = lambda a: np.asarray(a, dtype=np.float32)
    x_prompt, x_sample, state_gla, state_ret, p_prompt, p_sample = map(f, (x_prompt, x_sample, state_gla, state_ret, p_prompt, p_sample))
    w_in = f(w_in)[0]
    w3 = w_in.reshape(16, 128, NIN)
    per_rank = []
    for r in range(2):
        cst, cbf, inv = _host_consts(r, f(norm_mix)[0], f(norm_ple)[0], f(gla_norm)[0], f(w_gla_up)[0], f(b_gla)[0])
        wpk = np.empty((128, WTOT), np.float32)
        for lk in LKEYS:
            gk = lk if lk == 'r' else '%s%d' % (lk[:2], 2 * r + int(lk[2:]))
            c0, n = WBL[gk]
            cols = (c0 + GLA_PERM) if lk[:2] in ('qa', 'ka') else np.arange(c0, c0 + n)
            wpk[:, WOFF[lk]:WOFF[lk] + 16 * n] = w3[:, :, cols].transpose(1, 0, 2).reshape(128, 16 * n)
        per_rank.append((cst, cbf, wpk))
    woutp = _pack_w(f(w_out)[0], 512)
    wgatep = _pack_w(f(w_ple_gate)[0], 512)
    wprojp = np.ascontiguousarray(f(w_ple_proj)[0].reshape(2, 128, 2048).transpose(1, 0, 2).reshape(128, 4096))
    nfb = np.ascontiguousarray(np.broadcast_to(f(norm_final).reshape(1, 2048), (128, 2048)))
    cos0, sin0 = _rot_tables(inv, 0)
    cos1, sin1 = _rot_tables(inv, 1024)
    in_maps = []
    for c in range(8):
        b, r = c // 2, c % 2
        cst, cbf, wpk = per_rank[r]
        sl = slice(r * 1024, (r + 1) * 1024)
        s0 = 32 * b
        in_maps.append(dict(
            xp=np.ascontiguousarray(x_prompt[b, 0:1024]), xm=np.ascontiguousarray(x_prompt[b, 1024:2048]),
            xpost=np.ascontiguousarray(x_prompt[b, sl]),
            xs=np.ascontiguousarray(x_sample[s0:s0 + 32, 0]), xs16=np.ascontiguousarray(x_sample[s0 + 16 * r:s0 + 16 * r + 16, 0]),
            pm=np.ascontiguousarray(p_prompt[0, b, sl]), psd=np.ascontiguousarray(p_sample[0, s0 + 16 * r:s0 + 16 * r + 16, 0]),
            sg=np.ascontiguousarray(state_gla[0, s0:s0 + 32, 2 * r:2 * r + 2]), sr=np.ascontiguousarray(state_ret[0, s0:s0 + 32, 4 * r:4 * r + 4]),
            wpk=wpk, wout=woutp, wgate=wgatep, wproj=wprojp, cst=cst, cbf=cbf,
            cosm=cos1, sinm=sin1, cosp=cos0, sinp=sin0, nfb=nfb))
    if 'nc' not in _NC_CACHE:
        _NC_CACHE['nc'] = build_nc()
    res = run_bass_kernel_spmd(_NC_CACHE['nc'], in_maps, core_ids=list(range(8)))
    R = res.results
    y_prompt = np.empty((4, 2048, 2048), np.float32)
    y_sample = np.empty((128, 1, 2048), np.float32)
    ngp = np.empty((1, 4, 4, 256, 512), np.float32)
    nrp = np.empty((1, 4, 8, 256, 256), np.float32)
    ngs = np.empty((1, 128, 4, 256, 512), np.float32)
    nrs = np.empty((1, 128, 8, 256, 256), np.float32)
    for c in range(8):
        b, r = c // 2, c % 2
        s0 = 32 * b
        y_prompt[b, r * 1024:(r + 1) * 1024] = R[c]["y_m"]
        y_sample[s0 + 16 * r:s0 + 16 * r + 16, 0] = R[c]["y_s"]
        ngp[0, b, 2 * r:2 * r + 2] = R[c]["ng"]
        nrp[0, b, 4 * r:4 * r + 4] = R[c]["nr"]
        ngs[0, s0:s0 + 32, 2 * r:2 * r + 2] = R[c]["nsg"]
        nrs[0, s0:s0 + 32, 4 * r:4 * r + 4] = R[c]["nsr"]
    return (y_prompt, y_sample, ngp, nrp, ngs, nrs)
```
